# Optimizing a Trainium2 kernel written in Bass

```python
import jax, jax.numpy as jnp
from jax import lax
import numpy as np

D_MODEL = 2048
BATCH = 2
SEQ = 16384
DEPTH = 2

GRID_W = 64
NA_HEADS = 8
NA_HEAD_DIM = 128
NA_WIDTH = NA_HEADS * NA_HEAD_DIM
NA_MAX_ROWS = 8
NA_COLS = 16
SG_GROUPS = 8
SG_GROUP_DIM = 128
SG_WIDTH = SG_GROUPS * SG_GROUP_DIM
SG_CHUNK = 128
SPLIT_WIDTHS = (NA_WIDTH, NA_WIDTH, NA_WIDTH, NA_WIDTH, SG_WIDTH, SG_WIDTH, SG_WIDTH, D_MODEL, D_MODEL)
N_IN = sum(SPLIT_WIDTHS)
SPLIT_POINTS = tuple(int(s) for s in np.cumsum(SPLIT_WIDTHS)[:-1])
RMS_EPS = 1e-6
LN_EPS = 1e-5

kernel_name = "hybrid_natten_gmlp_encoder"


def rms_norm(x, g):
    xf = x.astype(jnp.float32)
    y = xf * lax.rsqrt(jnp.mean(xf * xf, axis=-1, keepdims=True) + RMS_EPS)
    return (y * g.astype(jnp.float32)).astype(x.dtype)


def layer_norm(x, g, b):
    xf = x.astype(jnp.float32)
    mu = jnp.mean(xf, axis=-1, keepdims=True)
    xc = xf - mu
    var = jnp.mean(xc * xc, axis=-1, keepdims=True)
    y = xc * lax.rsqrt(var + LN_EPS) * g.astype(jnp.float32) + b.astype(jnp.float32)
    return y.astype(x.dtype)


def neighbourhood_attention(q, k, v, rpb, rows):
    kr = min(NA_MAX_ROWS, rows)
    col = jnp.arange(GRID_W)
    col_start = jnp.clip(col - NA_COLS // 2, 0, GRID_W - NA_COLS)
    col_idx = col_start[:, None] + jnp.arange(NA_COLS)[None, :]
    dc = col_idx - col[:, None] + (NA_COLS - 1)
    rpb_c = rpb[:, :, dc]
    scale = NA_HEAD_DIM ** -0.5
    b = q.shape[0]

    def one_row(r):
        rs = jnp.clip(r - kr // 2, 0, rows - kr)
        k_rows = lax.dynamic_slice_in_dim(k, rs, kr, axis=1)
        v_rows = lax.dynamic_slice_in_dim(v, rs, kr, axis=1)
        k_win = k_rows[:, :, col_idx]
        v_win = v_rows[:, :, col_idx]
        q_r = lax.dynamic_index_in_dim(q, r, axis=1, keepdims=False)
        dr = rs + jnp.arange(kr) - r + (NA_MAX_ROWS - 1)
        bias = rpb_c[:, dr].transpose(0, 2, 1, 3)
        s = jnp.einsum('bqhd,bkqchd->bhqkc', q_r, k_win).astype(jnp.float32) * scale
        s = s + bias[None].astype(jnp.float32)
        p = jax.nn.softmax(s.reshape(b, NA_HEADS, GRID_W, kr * NA_COLS), axis=-1)
        p = p.reshape(s.shape).astype(v.dtype)
        return jnp.einsum('bhqkc,bkqchd->bqhd', p, v_win)

    out = lax.map(one_row, jnp.arange(rows))
    return out.transpose(1, 0, 2, 3, 4)


def spatial_gating(u, v, ln_g, ln_b, w_s, b_s):
    b, t, _ = v.shape
    v = layer_norm(v, ln_g, ln_b)
    vc = v.reshape(b, t // SG_CHUNK, SG_CHUNK, SG_GROUPS, SG_GROUP_DIM)
    s = jnp.einsum('gst,bntgc->bnsgc', w_s, vc) + b_s.T[None, None, :, :, None]
    return u * s.reshape(b, t, SG_WIDTH)


def hybrid_layer(x, pre_g, post_g, w_in, rpb, ln_g, ln_b, w_s, b_s, w_pa, w_pb, w_out):
    b, t, _ = x.shape
    rows = t // GRID_W
    h = rms_norm(x, pre_g)
    proj = jnp.einsum('btd,dn->btn', h, w_in)
    q, k, v, z_a, u, vs, z_b, g_a, g_b = jnp.split(proj, SPLIT_POINTS, axis=-1)
    grid = (b, rows, GRID_W, NA_HEADS, NA_HEAD_DIM)
    y_a = neighbourhood_attention(q.reshape(grid), k.reshape(grid), v.reshape(grid), rpb, rows)
    y_a = y_a.reshape(b, t, NA_WIDTH) * jax.nn.silu(z_a)
    y_b = spatial_gating(u, vs, ln_g, ln_b, w_s, b_s) * jax.nn.silu(z_b)
    merged = (jax.nn.sigmoid(g_a) * jnp.einsum('btc,cd->btd', y_a, w_pa)
              + jax.nn.sigmoid(g_b) * jnp.einsum('btc,cd->btd', y_b, w_pb))
    out = jnp.einsum('btd,de->bte', merged, w_out)
    return x + rms_norm(out, post_g)


def setup_inputs(seed: int = 0) -> dict:
    key = jax.random.key(seed)
    ks = jax.random.split(key, 14)
    f32 = jnp.float32
    n = lambda k, shape: jax.random.normal(k, shape, f32)
    return {
        "x": n(ks[0], (BATCH, SEQ, D_MODEL)),
        "pre_norm_g": 1.0 + 0.02 * n(ks[1], (DEPTH, D_MODEL)),
        "post_norm_g": 1.0 + 0.02 * n(ks[2], (DEPTH, D_MODEL)),
        "w_in": n(ks[3], (DEPTH, D_MODEL, N_IN)) * D_MODEL ** -0.5,
        "na_rpb": 0.1 * n(ks[4], (DEPTH, NA_HEADS, 2 * NA_MAX_ROWS - 1, 2 * NA_COLS - 1)),
        "sg_ln_g": 1.0 + 0.02 * n(ks[5], (DEPTH, SG_WIDTH)),
        "sg_ln_b": 0.02 * n(ks[6], (DEPTH, SG_WIDTH)),
        "sg_w": n(ks[7], (DEPTH, SG_GROUPS, SG_CHUNK, SG_CHUNK)) * SG_CHUNK ** -0.5,
        "sg_b": 1.0 + 0.02 * n(ks[8], (DEPTH, SG_GROUPS, SG_CHUNK)),
        "w_proj_a": n(ks[9], (DEPTH, NA_WIDTH, D_MODEL)) * NA_WIDTH ** -0.5,
        "w_proj_b": n(ks[10], (DEPTH, SG_WIDTH, D_MODEL)) * SG_WIDTH ** -0.5,
        "w_out": n(ks[11], (DEPTH, D_MODEL, D_MODEL)) * D_MODEL ** -0.5,
    }


def reference(x, pre_norm_g, post_norm_g, w_in, na_rpb, sg_ln_g, sg_ln_b, sg_w, sg_b,
              w_proj_a, w_proj_b, w_out):
    for l in range(DEPTH):
        x = hybrid_layer(x, pre_norm_g[l], post_norm_g[l], w_in[l], na_rpb[l], sg_ln_g[l],
                         sg_ln_b[l], sg_w[l], sg_b[l], w_proj_a[l], w_proj_b[l], w_out[l])
    return x
```

```python
from contextlib import ExitStack
import numpy as np
import ml_dtypes
import concourse.bass as bass
import concourse.mybir as mybir
from concourse.bass_utils import run_bass_kernel_spmd

F32 = mybir.dt.float32
BF16 = mybir.dt.bfloat16
AF = mybir.ActivationFunctionType
ALU = mybir.AluOpType
AX = mybir.AxisListType

ENGS = ("pe", "act", "dve", "pool", "sp")

D = 2048
NIN = 11264
NS = 79
SLAB = 4096
QSCALE = 128 ** -0.5
EXPSILU = True
TAILQ = "pool"
NEG = -30000.0


class Tok:
    __slots__ = ("name", "w", "r")

    def __init__(self, name):
        self.name = name
        self.w = None
        self.r = []


class Op:
    __slots__ = ("eng", "fn", "idx", "deps", "signal", "dma_key", "cnt")


class Prog:
    def __init__(self, nc):
        self.nc = nc
        self.ops = {e: [] for e in ENGS}
        self.dma_counts = {}

    def op(self, eng, fn, reads=(), writes=(), dma_key=None):
        o = Op()
        o.eng = eng
        o.fn = fn
        o.idx = len(self.ops[eng])
        o.signal = False
        o.dma_key = dma_key
        o.cnt = 0
        if dma_key is not None:
            c = self.dma_counts.get(dma_key, 0) + 16
            self.dma_counts[dma_key] = c
            o.cnt = c
        deps = {}

        def add(d, kind):
            if d is None or d is o:
                return
            if d.dma_key is None and d.eng == eng:
                if eng == "pe":
                    return
                if kind != "raw":
                    return
                if o.idx - d.idx > 8:
                    return
            deps[id(d)] = d

        for t in reads:
            add(t.w, "raw")
        for t in writes:
            add(t.w, "waw")
            for r in t.r:
                add(r, "war")
        for t in reads:
            t.r.append(o)
        for t in writes:
            t.w = o
            t.r = []
        o.deps = list(deps.values())
        for d in o.deps:
            if d.dma_key is None:
                d.signal = True
        self.ops[eng].append(o)
        return o

    def emit(self, final_dma_ops=()):
        nc = self.nc
        with ExitStack() as es:
            sems = {e: es.enter_context(nc.semaphore(f"s_{e}")) for e in ENGS}
            dsems = {k: es.enter_context(nc.semaphore(f"d_{k}")) for k in self.dma_counts}
            for e in ENGS:
                c = 0
                for o in self.ops[e]:
                    if o.dma_key is None and o.signal:
                        c += 1
                        o.cnt = c
            block = es.enter_context(nc.Block())

            def run(e, h):
                seen = {}
                for o in self.ops[e]:
                    need = {}
                    for d in o.deps:
                        s = dsems[d.dma_key] if d.dma_key is not None else sems[d.eng]
                        k = id(s)
                        if need.get(k, (None, 0))[1] < d.cnt:
                            need[k] = (s, d.cnt)
                    for k, (s, v) in need.items():
                        if seen.get(k, 0) < v:
                            h.wait_ge(s, v)
                            seen[k] = v
                    ins = o.fn(h)
                    if o.dma_key is not None:
                        ins.then_inc(dsems[o.dma_key], 16)
                    elif o.signal:
                        ins.then_inc(sems[e], 1)
                if e == "sp":
                    for o in final_dma_ops:
                        h.wait_ge(dsems[o.dma_key], o.cnt)

            @block.tensor
            def _(h):
                run("pe", h)

            @block.scalar
            def _(h):
                run("act", h)

            @block.vector
            def _(h):
                run("dve", h)

            @block.gpsimd
            def _(h):
                run("pool", h)

            @block.sync
            def _(h):
                run("sp", h)


NSW = 71


def slab_used(s):
    if s < 28:
        return 4096
    if s < 60:
        return 3072
    if s < 68:
        return 4096
    if s == 68:
        return 4096
    if s == 69:
        return 1024
    if s == 70:
        return 1024
    return 3072


def slab_gain(s):
    if s < 4:
        return 0
    if s < 12:
        return 1 + (s % 2)
    if s < 28:
        return 0
    if s < 60:
        return 3
    return None


def build_nc(n_layers, NR0, debug=False):
    nc = bass.Bass("TRN2", target_bir_lowering=False)
    dbg_ops = []

    def din(name, shape, dt=F32):
        return nc.dram_tensor(name, shape, dt, kind="ExternalInput").ap()

    xin = din("xin", [NR0 * 64, D])
    wsl = din("wsl", [n_layers, NSW, 128, SLAB])
    spt = din("spt", [n_layers, 8, 128, 3072])
    gains = din("gains", [n_layers, 4, 128, SLAB])
    sgb = din("sgb", [n_layers, 128, 1024])
    lnc = din("lnc", [n_layers, 128, 16])
    postg = din("postg", [n_layers, 128, D])
    ident_d = din("ident", [128, 128], BF16)
    ones_d = din("ones", [128, 128], BF16)
    NRL = NR0 - 8 * n_layers
    y = nc.dram_tensor("y", [NRL * 64, D], F32, kind="ExternalOutput").ap()
    wq = nc.dram_tensor("wq", [n_layers, NS, 128, SLAB], BF16, kind="Internal").ap()
    x1s = None
    if n_layers == 2:
        x1s = nc.dram_tensor("x1s", [(NR0 - 8) * 64, D], F32, kind="Internal").ap()

    with ExitStack() as es:
        off = [0]

        def alloc(nbytes):
            o = off[0]
            off[0] += (nbytes + 63) // 64 * 64
            return o

        o_hT = [alloc(16384), alloc(16384)]
        o_K = alloc(24576)
        o_V = alloc(24576)
        o_slab = [alloc(8192) for _ in range(3)]
        o_xn = alloc(8192)
        o_xr = alloc(8192)
        o_xs = alloc(4096)
        o_qT = [alloc(1024), alloc(1024)]
        o_sza = [alloc(2048), alloc(2048)]
        o_gt = alloc(2048)
        o_szb = alloc(2048)
        o_vn = alloc(8192)
        o_yb = alloc(8192)
        o_pT = [alloc(1536), alloc(1536)]
        o_mg = alloc(16384)
        o_msg = [alloc(2048), alloc(2048)]
        o_rden = alloc(2048)
        o_at = alloc(2048)
        o_bg = alloc(10240)
        o_sp = [alloc(1536), alloc(1536)]
        o_wst = alloc(2048)
        o_tt = alloc(4096)
        o_lnc = alloc(64)
        o_pg = alloc(8192)
        o_id = alloc(256)
        o_on = alloc(256)
        o_st = alloc(1024)
        o_junk = alloc(1024)
        TOTAL = off[0]
        assert TOTAL <= 212000, TOTAL
        A = es.enter_context(nc.sbuf_tensor("arena", [128, TOTAL // 2], BF16))
        PS = es.enter_context(nc.psum_tensor("ps", [128, 4096], F32))

        def bf(o, n):
            return A[:, o // 2:o // 2 + n]

        def f32(o, n):
            return A[:, o // 2:o // 2 + 2 * n].bitcast(F32)

        def bank(b, n=512, c0=0):
            return PS[:, b * 512 + c0:b * 512 + c0 + n]

        def bank_bf(b, n):
            return PS[:, b * 512:b * 512 + (n + 1) // 2].bitcast(BF16)

        hT = [bf(o, 8192).rearrange("p (k t) -> p k t", k=16) for o in o_hT]
        Kr = bf(o_K, 12288).rearrange("p (h t) -> p h t", h=8)
        Vr = bf(o_V, 12288).rearrange("p (s c) -> p s c", s=12)
        slabs = [bf(o, 4096) for o in o_slab]
        xn = f32(o_xn, 2048)
        xr = f32(o_xr, 2048)
        xs = bf(o_xs, 2048)
        qT = [bf(o, 512) for o in o_qT]
        sza = [f32(o, 512) for o in o_sza]
        gt = f32(o_gt, 512)
        szb = f32(o_szb, 512)
        vn = bf(o_vn, 4096).rearrange("p (t c) -> p t c", t=4)
        yaT = bf(o_vn, 4096).rearrange("p (h t) -> p h t", h=8)
        ybT = bf(o_yb, 4096).rearrange("p (h t) -> p h t", h=8)
        pT = [bf(o, 768) for o in o_pT]
        mgT = bf(o_mg, 8192).rearrange("p (k t) -> p k t", k=16)
        njunk = bf(o_mg, 2048)
        msg = [f32(o, 512) for o in o_msg]
        rden = f32(o_rden, 512)
        atb = f32(o_at, 512)
        bgen = bf(o_bg, 5120).rearrange("p (i h q) -> p i h q", i=5, h=8)
        spb = [bf(o, 768).rearrange("p (i q) -> p i q", i=6) for o in o_sp]
        wst = bf(o_wst, 1024).rearrange("p (g s) -> p g s", g=8)
        ttab = f32(o_tt, 1024).rearrange("p (g s) -> p g s", g=8)
        lncb = f32(o_lnc, 16)
        pgb = f32(o_pg, 2048)
        ident = bf(o_id, 128)
        ones = bf(o_on, 128)
        st = f32(o_st, 256)
        ojunk = bf(o_junk, 512)

        P = Prog(nc)
        T = Tok

        def dump(name, ap, n, dt, toks):
            if not debug:
                return
            d = nc.dram_tensor("dbg_" + name, [128, n], dt, kind="ExternalOutput").ap()
            dbg_ops.append(P.op("sp", lambda h: h.dma_start(out=d, in_=ap), reads=toks, dma_key="dbg"))

        t_hT = [T("hT0"), T("hT1")]
        t_K, t_V = T("K"), T("V")
        t_slab = [T(f"slab{i}") for i in range(3)]
        t_xn, t_xr, t_xs = T("xn"), T("xr"), T("xs")
        t_qT = [T("q0"), T("q1")]
        t_sza = [T("sza0"), T("sza1")]
        t_gt, t_szb = T("gt"), T("szb")
        t_vn, t_yb = T("vn"), T("yb")
        t_pT = [T("pT0"), T("pT1")]
        t_mg = T("mg")
        t_msg = [T("msg0"), T("msg1")]
        t_rden, t_at = T("rden"), T("at")
        t_bg = T("bg")
        t_sp = [T("sp0"), T("sp1")]
        t_wst, t_tt, t_lnc, t_pg = T("wst"), T("tt"), T("lnc"), T("pg")
        t_const = T("const")
        t_nst, t_lst, t_ost = T("nst"), T("lst"), T("ost")
        t_junk = T("junk")
        t_bank = [T(f"bank{i}") for i in range(8)]
        t_x1c = [T(f"x1c{i}") for i in range((NR0 - 8) * 64 // 128)]
        t_xrh = [T("xrh0"), T("xrh1")]
        t_lsa, t_lsd = T("lsa"), T("lsd")
        main_toks = (t_hT + [t_K, t_V] + t_slab + [t_xn, t_xr, t_xs] + t_qT + t_sza + [t_gt, t_szb, t_vn, t_yb]
                     + t_pT + [t_mg] + t_msg + [t_rden, t_at, t_bg] + t_sp + [t_wst, t_tt, t_lnc, t_pg, t_const,
                                                                           t_nst, t_lst, t_ost, t_junk, t_lsa, t_lsd] + t_xrh)

        NST = st[:, 0:16]
        LST = st[:, 16:64]
        OST = st[:, 64:96]
        EPS_RMS = st[:, 96:97]
        EPS_LN = st[:, 97:98]

        NB = 2
        pci = [f32(16384 * i, 4096) for i in range(NB)]
        pco = [bf(49152 + 8192 * i, 4096) for i in range(NB)]
        gti = [[f32(73728 + 16384 * (l * 4 + k), 4096) for k in range(4)] for l in range(n_layers)]
        assert 73728 + 16384 * 4 * n_layers <= TOTAL
        t_pci = [T(f"pci{i}") for i in range(NB)]
        t_pco = [T(f"pco{i}") for i in range(NB)]
        t_gain = [T(f"gain{k}") for k in range(4 * n_layers)]
        t_wq = [[T(f"wq{l}_{s}") for s in range(NS)] for l in range(n_layers)]
        jobs = [(l, s) for l in range(n_layers) for s in range(NS)]
        for l in range(n_layers):
            for k in range(4):
                P.op("sp", lambda h, l=l, k=k: h.dma_start(out=gti[l][k], in_=gains[l, k]), writes=[t_gain[l * 4 + k]],
                     dma_key=f"gain{l * 4 + k}")

        def pc_load(q):
            l, s = jobs[q]
            n = slab_used(s)
            i = q % NB
            srcap = wsl[l, s, :, 0:n] if s < NSW else spt[l, s - NSW]
            P.op("sp", lambda h: h.dma_start(out=pci[i][:, 0:n], in_=srcap), writes=[t_pci[i]], dma_key=f"pci{i}")

        def pc_cast(q):
            l, s = jobs[q]
            n = slab_used(s)
            i = q % NB
            g = slab_gain(s)
            if g is None:
                P.op("act", lambda h: h.activation(out=pco[i][:, 0:n], in_=pci[i][:, 0:n], func=AF.Copy),
                     reads=[t_pci[i]], writes=[t_pco[i]])
            else:
                eng = "dve" if (q % 3) != 2 else "pool"
                P.op(eng, lambda h: h.tensor_tensor(out=pco[i][:, 0:n], in0=pci[i][:, 0:n], in1=gti[l][g][:, 0:n], op=ALU.mult),
                     reads=[t_pci[i], t_gain[l * 4 + g]], writes=[t_pco[i]])

        def pc_store(q):
            l, s = jobs[q]
            n = slab_used(s)
            i = q % NB
            P.op("sp", lambda h: h.dma_start(out=wq[l, s, :, 0:n], in_=pco[i][:, 0:n]),
                 reads=[t_pco[i]], writes=[t_wq[l][s]], dma_key=f"pco{i}")

        pc_load(0)
        for q in range(len(jobs)):
            if q + 1 < len(jobs):
                pc_load(q + 1)
            pc_cast(q)
            pc_store(q)
        P.op("pool", lambda h: h.memset(st[:, 128:129], 0.0), writes=t_pci + t_pco + t_gain + main_toks)
        P.op("pool", lambda h: h.memset(EPS_RMS, 1e-6), writes=[t_const])
        P.op("pool", lambda h: h.memset(EPS_LN, 1e-5), writes=[t_const])
        P.op("sp", lambda h: h.dma_start(out=ident, in_=ident_d), writes=[t_const], dma_key="const")
        P.op("sp", lambda h: h.dma_start(out=ones, in_=ones_d), writes=[t_const], dma_key="const")

        class Stream:
            def __init__(self):
                self.plan = []
                self.issued = 0
                self.pos = 0

            def issue(self, upto):
                while self.issued < min(upto, len(self.plan)):
                    i = self.issued
                    l, s = self.plan[i]
                    n = slab_used(s)
                    b = i % 3
                    P.op("sp", lambda h, l=l, s=s, n=n, b=b: h.dma_start(out=slabs[b][:, 0:n], in_=wq[l, s, :, 0:n]),
                         reads=[t_wq[l][s]], writes=[t_slab[b]], dma_key=f"slab{b}")
                    self.issued += 1

            def get(self, l, s):
                i = self.pos
                assert self.plan[i] == (l, s), (i, self.plan[i], l, s)
                self.issue(i + 3)
                self.pos += 1
                return slabs[i % 3], t_slab[i % 3]

        stream = Stream()

        def layer_plan(l, nfull):
            pl = []
            kv = [(l, s) for s in range(0, 8)]
            pl += kv + kv
            for s_ in range(nfull):
                pl += [(l, s) for s in range(8, 12)]
                pl += [(l, s) for s in range(12, 20)]
                pl += kv
                pl += [(l, s) for s in range(20, 28)]
                pl += [(l, s) for s in range(28, 60)]
                pl += [(l, s) for s in range(60, 68)]
            return pl

        for l in range(n_layers):
            NR = NR0 - 8 * l
            stream.plan += layer_plan(l, (NR - 8) // 8)

        out_stores = []

        def do_layer(l):
            NR = NR0 - 8 * l
            nfull = (NR - 8) // 8
            src = xin if l == 0 else x1s
            dst = y if l == n_layers - 1 else x1s
            def src_tok(r0):
                return [t_x1c[r0 // 128]] if l > 0 else []

            def dst_tok(q0):
                return [t_x1c[q0 // 128]] if l < n_layers - 1 else []

            xrh = [f32(o_xr, 1024), f32(o_xr + 4096, 1024)]
            e0 = 8 - 4 * l if NR0 == 80 else 4
            if NR0 != 80:
                assert n_layers == 1

            def blk(j):
                if j == 0:
                    return 0, 256, 0
                if j == nfull + 1:
                    return NR - 4, 256, (NR - 4) // 2
                return 4 + 8 * (j - 1), 512, 2 + 4 * (j - 1)

            def slot(pi):
                return (pi + 2) % 12

            P.op("sp", lambda h, l=l: h.dma_start(out=bf(o_bg, 4096), in_=wq[l, 68, :, 0:4096]),
                 reads=[t_wq[l][68]], writes=[t_bg], dma_key="lc_bg")
            P.op("sp", lambda h, l=l: h.dma_start(out=bf(o_bg + 8192, 1024), in_=wq[l, 69, :, 0:1024]),
                 reads=[t_wq[l][69]], writes=[t_bg], dma_key="lc_bg")
            P.op("sp", lambda h, l=l: h.dma_start(out=bf(o_wst, 1024), in_=wq[l, 70, :, 0:1024]),
                 reads=[t_wq[l][70]], writes=[t_wst], dma_key="lc_wst")
            P.op("sp", lambda h, l=l: h.dma_start(out=f32(o_tt, 1024), in_=sgb[l]), writes=[t_tt], dma_key="lc_tt")
            P.op("sp", lambda h, l=l: h.dma_start(out=lncb, in_=lnc[l]), writes=[t_lnc], dma_key="lc_lnc")
            P.op("sp", lambda h, l=l: h.dma_start(out=pgb, in_=postg[l]), writes=[t_pg], dma_key="lc_pg")
            for c in range(2):
                P.op("pe", lambda h, c=c: h.matmul(bank(c), lhsT=ones, rhs=bf(o_wst + c * 1024, 512), start=True, stop=True),
                     reads=[t_const, t_wst], writes=[t_bank[c]])
            for g in range(8):
                P.op("dve", lambda h, g=g: h.scalar_tensor_tensor(
                    out=ttab[:, g, :], in0=bank(g // 4, 128, (g % 4) * 128), scalar=lncb[:, 8 + g:9 + g],
                    in1=ttab[:, g, :], op0=ALU.mult, op1=ALU.add),
                    reads=[t_bank[g // 4], t_lnc, t_tt], writes=[t_tt])

            def norm_elem_tile(j, t):
                row0, ntok, _ = blk(j)
                r0 = row0 * 64 + t * 128
                P.op("sp", lambda h: h.dma_start(out=xn, in_=src[r0:r0 + 128, :]),
                     reads=src_tok(r0), writes=[t_xn], dma_key="xn")
                P.op("act", lambda h: h.activation(out=xs, in_=xn, func=AF.Square, accum_out=NST[:, t:t + 1]),
                     reads=[t_xn], writes=[t_xs, t_nst])
                P.op("act", lambda h: h.activation(out=NST[:, 4 + t:5 + t], in_=NST[:, t:t + 1], func=AF.Sqrt,
                                                   scale=1.0 / D, bias=EPS_RMS),
                     reads=[t_nst, t_const], writes=[t_nst])
                P.op("dve", lambda h: h.reciprocal(out=NST[:, 8 + t:9 + t], in_=NST[:, 4 + t:5 + t]),
                     reads=[t_nst], writes=[t_nst])
                P.op("act", lambda h: h.activation(out=xs, in_=xn, func=AF.Copy, scale=NST[:, 8 + t:9 + t]),
                     reads=[t_xn, t_nst], writes=[t_xs])

            def norm_trans_tile(j, t):
                par = j % 2
                for kc in range(16):
                    P.op("pe", lambda h, kc=kc: h.transpose(out=bank_bf(6, 2048)[:, kc * 128:(kc + 1) * 128],
                                                            in_=xs[:, kc * 128:(kc + 1) * 128], identity=ident),
                         reads=[t_xs, t_const], writes=[t_bank[6], t_bank[7]])
                P.op("dve", lambda h: h.tensor_copy(
                    out=hT[par][:, :, t * 128:(t + 1) * 128],
                    in_=bank_bf(6, 2048).rearrange("p (k c) -> p k c", c=128)),
                    reads=[t_bank[6], t_bank[7]], writes=[t_hT[par]])

            def norm_elem(j):
                for t in range(blk(j)[1] // 128):
                    norm_elem_tile(j, t)
                    norm_trans_tile(j, t)

            def kv_phase(j):
                row0, ntok, p0 = blk(j)
                nt = ntok // 128
                par = j % 2
                s0 = slot(p0)
                for i in range(4):
                    sl, tsl = stream.get(l, i)
                    for ch in range(2):
                        hd = 2 * i + ch
                        b = (2 * i + ch) % 8
                        for kc in range(16):
                            P.op("pe", lambda h, sl=sl, ch=ch, kc=kc, b=b: h.matmul(
                                bank(b, ntok), lhsT=sl[:, ch * 2048 + kc * 128:ch * 2048 + (kc + 1) * 128],
                                rhs=hT[par][:, kc, 0:ntok], start=(kc == 0), stop=(kc == 15)),
                                reads=[tsl, t_hT[par]], writes=[t_bank[b]])
                        P.op("dve", lambda h, hd=hd, b=b: h.tensor_copy(out=Kr[:, hd, s0 * 128:s0 * 128 + ntok], in_=bank(b, ntok)),
                             reads=[t_bank[b]], writes=[t_K])
                for cg in range(2):
                    for kh in range(2):
                        sl, tsl = stream.get(l, 4 + cg * 2 + kh)
                        for t in range(nt):
                            b = cg * 4 + t
                            for kcl in range(8):
                                kc = kh * 8 + kcl
                                P.op("pe", lambda h, sl=sl, t=t, kc=kc, kcl=kcl, b=b: h.matmul(
                                    bank(b), lhsT=hT[par][:, kc, t * 128:(t + 1) * 128],
                                    rhs=sl[:, kcl * 512:(kcl + 1) * 512], start=(kc == 0), stop=(kc == 15)),
                                    reads=[tsl, t_hT[par]], writes=[t_bank[b]])
                    for t in range(nt):
                        b = cg * 4 + t
                        P.op("act", lambda h, t=t, b=b, cg=cg: h.activation(
                            out=Vr[:, s0 + t, cg * 512:(cg + 1) * 512], in_=bank(b), func=AF.Copy),
                            reads=[t_bank[b]], writes=[t_V])

            def vs_slab(j, cg, kh):
                par = j % 2
                sl, tsl = stream.get(l, 8 + cg * 2 + kh)
                for t in range(4):
                    b = cg * 4 + t
                    for kcl in range(8):
                        kc = kh * 8 + kcl
                        P.op("pe", lambda h, sl=sl, t=t, kc=kc, kcl=kcl, b=b: h.matmul(
                            bank(b), lhsT=hT[par][:, kc, t * 128:(t + 1) * 128],
                            rhs=sl[:, kcl * 512:(kcl + 1) * 512], start=(kc == 0), stop=(kc == 15)),
                            reads=[tsl, t_hT[par]], writes=[t_bank[b]])

            def g_phase_ln(j):
                for cg in range(2):
                    for t in range(4):
                        b = cg * 4 + t
                        c = cg * 4 + t
                        P.op("act", lambda h, b=b, c=c: h.activation(out=ojunk, in_=bank(b), func=AF.Square,
                                                                     accum_out=LST[:, 8 + c:9 + c]),
                             reads=[t_bank[b]], writes=[t_junk, t_lst])
                        P.op("dve", lambda h, b=b, c=c: h.tensor_reduce(out=LST[:, c:c + 1], in_=bank(b), axis=AX.X, op=ALU.add),
                             reads=[t_bank[b]], writes=[t_lst])
                rl = [t_lst]
                P.op("dve", lambda h: h.tensor_tensor(out=LST[:, 16:20], in0=LST[:, 0:4], in1=LST[:, 4:8], op=ALU.add), reads=rl, writes=rl)
                P.op("dve", lambda h: h.tensor_tensor(out=LST[:, 20:24], in0=LST[:, 8:12], in1=LST[:, 12:16], op=ALU.add), reads=rl, writes=rl)
                P.op("dve", lambda h: h.tensor_scalar(out=LST[:, 16:20], in0=LST[:, 16:20], scalar1=1.0 / 1024, scalar2=0.0,
                                                      op0=ALU.mult, op1=ALU.add), reads=rl, writes=rl)
                P.op("dve", lambda h: h.tensor_tensor(out=LST[:, 24:28], in0=LST[:, 16:20], in1=LST[:, 16:20], op=ALU.mult), reads=rl, writes=rl)
                P.op("dve", lambda h: h.scalar_tensor_tensor(out=LST[:, 24:28], in0=LST[:, 20:24], scalar=1.0 / 1024,
                                                             in1=LST[:, 24:28], op0=ALU.mult, op1=ALU.subtract), reads=rl, writes=rl)
                P.op("act", lambda h: h.activation(out=LST[:, 24:28], in_=LST[:, 24:28], func=AF.Sqrt, scale=1.0, bias=EPS_LN),
                     reads=[t_lst, t_const], writes=rl)
                P.op("dve", lambda h: h.reciprocal(out=LST[:, 28:32], in_=LST[:, 24:28]), reads=rl, writes=rl)
                P.op("dve", lambda h: h.scalar_tensor_tensor(out=LST[:, 32:36], in0=LST[:, 16:20], scalar=-1.0,
                                                             in1=LST[:, 28:32], op0=ALU.mult, op1=ALU.mult), reads=rl, writes=rl)
                for cg in range(2):
                    for t in range(4):
                        b = cg * 4 + t
                        P.op("act", lambda h, t=t, b=b, cg=cg: h.activation(
                            out=vn[:, t, cg * 512:(cg + 1) * 512], in_=bank(b), func=AF.Identity,
                            scale=LST[:, 28 + t:29 + t], bias=LST[:, 32 + t:33 + t]),
                            reads=[t_bank[b], t_lst], writes=[t_vn])

            def g_phase_uz(j, g):
                par = j % 2
                sl, tsl = stream.get(l, 12 + g)
                bs = (g % 2) * 3
                bu, bz, bsx = bs, bs + 1, bs + 2
                for which, b in ((0, bu), (1, bz)):
                    for kc in range(16):
                        P.op("pe", lambda h, sl=sl, which=which, kc=kc, b=b: h.matmul(
                            bank(b), lhsT=sl[:, which * 2048 + kc * 128:which * 2048 + (kc + 1) * 128],
                            rhs=hT[par][:, kc, :], start=(kc == 0), stop=(kc == 15)),
                            reads=[tsl, t_hT[par]], writes=[t_bank[b]])
                for t in range(4):
                    P.op("pe", lambda h, t=t, g=g, b=bsx: h.matmul(
                        bank(b, 128, t * 128), lhsT=vn[:, t, g * 128:(g + 1) * 128], rhs=wst[:, g, :], start=True, stop=True),
                        reads=[t_vn, t_wst], writes=[t_bank[bsx]])
                P.op("act", lambda h, b=bz: h.activation(out=szb, in_=bank(b), func=AF.Silu), reads=[t_bank[bz]], writes=[t_szb])
                for t in range(4):
                    P.op("dve", lambda h, t=t, g=g, b=bsx: h.scalar_tensor_tensor(
                        out=gt[:, t * 128:(t + 1) * 128], in0=bank(b, 128, t * 128), scalar=lncb[:, g:g + 1],
                        in1=ttab[:, g, :], op0=ALU.mult, op1=ALU.add),
                        reads=[t_bank[bsx], t_lnc, t_tt], writes=[t_gt])
                P.op("dve", lambda h, b=bu: h.tensor_tensor(out=gt, in0=gt, in1=bank(b), op=ALU.mult),
                     reads=[t_gt, t_bank[bu]], writes=[t_gt])
                P.op("pool", lambda h, g=g: h.tensor_tensor(out=ybT[:, g, :], in0=gt, in1=szb, op=ALU.mult),
                     reads=[t_gt, t_szb], writes=[t_yb])

            def q_proj(j, hd):
                par = j % 2
                sl, tsl = stream.get(l, 20 + hd)
                for which, b in ((0, 0), (1, 1)):
                    for kc in range(16):
                        P.op("pe", lambda h, sl=sl, which=which, kc=kc, b=b: h.matmul(
                            bank(b), lhsT=sl[:, which * 2048 + kc * 128:which * 2048 + (kc + 1) * 128],
                            rhs=hT[par][:, kc, :], start=(kc == 0), stop=(kc == 15)),
                            reads=[tsl, t_hT[par]], writes=[t_bank[b]])
                P.op("dve", lambda h, hd=hd: h.tensor_scalar(out=qT[hd % 2], in0=bank(0), scalar1=QSCALE, scalar2=0.0,
                                                             op0=ALU.mult, op1=ALU.add),
                     reads=[t_bank[0]], writes=[t_qT[hd % 2]])
                if EXPSILU:
                    P.op("act", lambda h, hd=hd: h.activation(out=sza[hd % 2], in_=bank(1), func=AF.Exp, scale=-1.0),
                         reads=[t_bank[1]], writes=[t_sza[hd % 2]])
                    P.op("dve", lambda h, hd=hd: h.tensor_scalar(out=sza[hd % 2], in0=sza[hd % 2], scalar1=1.0, scalar2=0.0,
                                                                 op0=ALU.add, op1=ALU.add),
                         reads=[t_sza[hd % 2]], writes=[t_sza[hd % 2]])
                    P.op("dve", lambda h, hd=hd: h.reciprocal(out=sza[hd % 2], in_=sza[hd % 2]),
                         reads=[t_sza[hd % 2]], writes=[t_sza[hd % 2]])
                    P.op("dve", lambda h, hd=hd: h.tensor_tensor(out=sza[hd % 2], in0=sza[hd % 2], in1=bank(1), op=ALU.mult),
                         reads=[t_bank[1], t_sza[hd % 2]], writes=[t_sza[hd % 2]])
                else:
                    P.op("act", lambda h, hd=hd: h.activation(out=sza[hd % 2], in_=bank(1), func=AF.Silu),
                         reads=[t_bank[1]], writes=[t_sza[hd % 2]])

            sp_ctr = [0]

            def attn(j, hd):
                _, _, p0 = blk(j)
                tiles = []
                for t in range(4):
                    tp = p0 + t
                    row = 2 * tp
                    spi = None
                    if row == e0:
                        spi, offs = 0, list(range(-2, 4))
                    elif row == e0 + 2:
                        spi, offs = 1, list(range(-2, 3))
                    elif row == e0 + 60:
                        spi, offs = 2, list(range(-2, 3))
                    elif row == e0 + 62:
                        spi, offs = 3, list(range(-3, 3))
                    else:
                        offs = list(range(-2, 3))
                    tiles.append((tp, spi, offs))

                def S(t):
                    tp, spi, offs = tiles[t]
                    bx, by = (4, 5) if t % 2 == 0 else (6, 7)
                    tb = None
                    ttb = t_bg
                    if spi is not None:
                        k = sp_ctr[0] % 2
                        sp_ctr[0] += 1
                        sidx = 71 + spi * 2 + hd // 4
                        P.op("sp", lambda h, k=k, sidx=sidx: h.dma_start(
                            out=bf(o_sp[k], 768), in_=wq[l, sidx, :, (hd % 4) * 768:(hd % 4) * 768 + 768]),
                            reads=[t_wq[l][sidx]], writes=[t_sp[k]], dma_key=f"spt{k}")
                        tb = spb[k]
                        ttb = t_sp[k]
                    for i, o in enumerate(offs):
                        pi = tp + o
                        sk = slot(pi)
                        b = bx if i < 4 else by
                        c0 = (i % 4) * 128
                        P.op("pe", lambda h, sk=sk, b=b, c0=c0, t=t: h.matmul(
                            bank(b, 128, c0), lhsT=Kr[:, hd, sk * 128:(sk + 1) * 128], rhs=qT[hd % 2][:, t * 128:(t + 1) * 128],
                            start=True, stop=False),
                            reads=[t_K, t_qT[hd % 2]], writes=[t_bank[b]])
                        if tb is None:
                            rhs = bgen[:, i, hd, :]
                        else:
                            rhs = tb[:, i, :]
                        P.op("pe", lambda h, b=b, c0=c0, rhs=rhs: h.matmul(bank(b, 128, c0), lhsT=ident, rhs=rhs, start=False, stop=True),
                             reads=[t_const, ttb], writes=[t_bank[b]])

                def E(t):
                    tp, spi, offs = tiles[t]
                    bx, by = (4, 5) if t % 2 == 0 else (6, 7)
                    n2 = (len(offs) - 4) * 128
                    P.op("act", lambda h, t=t, bx=bx: h.activation(out=pT[t % 2][:, 0:512], in_=bank(bx), func=AF.Exp),
                         reads=[t_bank[bx]], writes=[t_pT[t % 2]])
                    P.op("act", lambda h, t=t, by=by, n2=n2: h.activation(out=pT[t % 2][:, 512:512 + n2], in_=bank(by, n2), func=AF.Exp),
                         reads=[t_bank[by]], writes=[t_pT[t % 2]])

                def PV(t):
                    tp, spi, offs = tiles[t]
                    n = len(offs)
                    for i, o in enumerate(offs):
                        sk = slot(tp + o)
                        P.op("pe", lambda h, sk=sk, i=i, t=t, n=n: h.matmul(
                            bank(2, 128, t * 128), lhsT=Vr[:, sk, hd * 128:(hd + 1) * 128], rhs=pT[t % 2][:, i * 128:(i + 1) * 128],
                            start=(i == 0), stop=(i == n - 1)),
                            reads=[t_V, t_pT[t % 2]], writes=[t_bank[2]])
                        P.op("pe", lambda h, i=i, t=t, n=n: h.matmul(
                            bank(3, 128, t * 128), lhsT=ones, rhs=pT[t % 2][:, i * 128:(i + 1) * 128],
                            start=(i == 0), stop=(i == n - 1)),
                            reads=[t_const, t_pT[t % 2]], writes=[t_bank[3]])

                S(0)
                for t in range(4):
                    if t + 1 < 4:
                        S(t + 1)
                    E(t)
                    PV(t)
                P.op("dve", lambda h: h.reciprocal(out=rden, in_=bank(3)), reads=[t_bank[3]], writes=[t_rden])
                P.op("dve", lambda h: h.tensor_tensor(out=atb, in0=bank(2), in1=rden, op=ALU.mult),
                     reads=[t_bank[2], t_rden], writes=[t_at])
                P.op("pool", lambda h: h.tensor_tensor(out=yaT[:, hd, :], in0=atb, in1=sza[hd % 2], op=ALU.mult),
                     reads=[t_at, t_sza[hd % 2]], writes=[t_vn])

            def m_phase(j):
                par = j % 2
                for jj in range(16):
                    bs = (jj % 2) * 4
                    for half in range(2):
                        sl, tsl = stream.get(l, 28 + 2 * jj + half)
                        bg_, bp_ = bs + 2 * half, bs + 2 * half + 1
                        for kc in range(16):
                            P.op("pe", lambda h, sl=sl, kc=kc, b=bg_: h.matmul(
                                bank(b), lhsT=sl[:, kc * 128:(kc + 1) * 128], rhs=hT[par][:, kc, :],
                                start=(kc == 0), stop=(kc == 15)),
                                reads=[tsl, t_hT[par]], writes=[t_bank[bg_]])
                        yT = yaT if half == 0 else ybT
                        ty = t_vn if half == 0 else t_yb
                        for c in range(8):
                            P.op("pe", lambda h, sl=sl, c=c, b=bp_, yT=yT: h.matmul(
                                bank(b), lhsT=sl[:, 2048 + c * 128:2048 + (c + 1) * 128], rhs=yT[:, c, :],
                                start=(c == 0), stop=(c == 7)),
                                reads=[tsl, ty], writes=[t_bank[bp_]])
                        P.op("act", lambda h, half=half, b=bg_: h.activation(out=msg[half], in_=bank(b), func=AF.Sigmoid),
                             reads=[t_bank[bg_]], writes=[t_msg[half]])
                        P.op("dve", lambda h, half=half, b=bp_: h.tensor_tensor(out=msg[half], in0=msg[half], in1=bank(b), op=ALU.mult),
                             reads=[t_msg[half], t_bank[bp_]], writes=[t_msg[half]])
                    P.op("pool", lambda h, jj=jj: h.tensor_tensor(out=mgT[:, jj, :], in0=msg[0], in1=msg[1], op=ALU.add),
                         reads=t_msg, writes=[t_mg])

            def o_phase(j, early=None):
                row0, _, _ = blk(j)
                par = j % 2
                ob = [f32(o_msg[0], 2048), f32(o_yb, 2048), f32(o_hT[par], 2048), f32(o_hT[par] + 8192, 2048)]
                t_ob = [[t_msg[0], t_msg[1], t_rden, t_at], [t_yb], [t_hT[par]], [t_hT[par]]]
                for cg in range(4):
                    bs = (cg % 2) * 4
                    for kh in range(2):
                        sl, tsl = stream.get(l, 60 + cg * 2 + kh)
                        for t in range(4):
                            b = bs + t
                            for kcl in range(8):
                                kc = kh * 8 + kcl
                                P.op("pe", lambda h, sl=sl, t=t, kc=kc, kcl=kcl, b=b: h.matmul(
                                    bank(b), lhsT=mgT[:, kc, t * 128:(t + 1) * 128], rhs=sl[:, kcl * 512:(kcl + 1) * 512],
                                    start=(kc == 0), stop=(kc == 15)),
                                    reads=[tsl, t_mg], writes=[t_bank[b]])
                    for t in range(4):
                        b = bs + t
                        P.op("act", lambda h, t=t, b=b, cg=cg: h.activation(out=ob[t][:, cg * 512:(cg + 1) * 512], in_=bank(b), func=AF.Copy),
                             reads=[t_bank[b]], writes=t_ob[t])
                        P.op("act", lambda h, t=t, cg=cg: h.activation(out=ojunk, in_=ob[t][:, cg * 512:(cg + 1) * 512], func=AF.Square,
                                                                       accum_out=OST[:, t * 4 + cg:t * 4 + cg + 1]),
                             reads=t_ob[t], writes=[t_junk, t_ost])
                if early is not None:
                    early()
                ro = [t_ost]
                P.op("dve", lambda h: h.tensor_reduce(out=OST[:, 16:20], in_=OST[:, 0:16].rearrange("p (t c) -> p t c", c=4),
                                                      axis=AX.X, op=ALU.add), reads=ro, writes=ro)
                P.op("act", lambda h: h.activation(out=OST[:, 20:24], in_=OST[:, 16:20], func=AF.Sqrt, scale=1.0 / D, bias=EPS_RMS),
                     reads=[t_ost, t_const], writes=ro)
                P.op("dve", lambda h: h.reciprocal(out=OST[:, 24:28], in_=OST[:, 20:24]), reads=ro, writes=ro)
                for t in (1, 2, 3, 0):
                    r0 = row0 * 64 + t * 128
                    q0 = (row0 - 4) * 64 + t * 128
                    P.op(TAILQ, lambda h, r0=r0: h.dma_start(out=xr, in_=src[r0:r0 + 128, :]),
                         reads=src_tok(r0), writes=t_xrh, dma_key="xr")
                    P.op("dve", lambda h, t=t: h.scalar_tensor_tensor(out=ob[t], in0=ob[t], scalar=OST[:, 24 + t:25 + t], in1=pgb,
                                                                      op0=ALU.mult, op1=ALU.mult),
                         reads=t_ob[t] + [t_ost, t_pg], writes=t_ob[t])
                    for hf in range(2):
                        P.op("pool", lambda h, t=t, hf=hf: h.tensor_tensor(out=ob[t][:, hf * 1024:(hf + 1) * 1024],
                                                                          in0=ob[t][:, hf * 1024:(hf + 1) * 1024], in1=xrh[hf], op=ALU.add),
                             reads=t_ob[t] + [t_xrh[hf]], writes=t_ob[t])
                    o = P.op(TAILQ, lambda h, t=t, q0=q0: h.dma_start(out=dst[q0:q0 + 128, :], in_=ob[t]),
                             reads=t_ob[t], writes=dst_tok(q0), dma_key=f"out{t}")
                    if l == n_layers - 1:
                        out_stores.append(o)

            norm_elem(0)
            kv_phase(0)
            norm_elem(1)
            kv_phase(1)
            pre = False
            for s_ in range(nfull):
                j = s_ + 1
                ntn = blk(j + 1)[1] // 128
                if not pre:
                    norm_elem_tile(j + 1, 0)
                vs_slab(j, 0, 0)
                vs_slab(j, 0, 1)
                vs_slab(j, 1, 0)
                vs_slab(j, 1, 1)
                if j == 1 and l == 0:
                    dump("hT", bf(o_hT[1], 8192), 8192, BF16, [t_hT[1]])
                g_phase_ln(j)
                if j == 1 and l == 0:
                    dump("vn", bf(o_vn, 4096), 4096, BF16, [t_vn])
                tsched = {0: 0, 2: 1, 4: 2, 6: 3}
                for g in range(8):
                    g_phase_uz(j, g)
                    k = tsched.get(g)
                    if k is not None and k < ntn:
                        norm_trans_tile(j + 1, k)
                        if k + 1 < ntn:
                            norm_elem_tile(j + 1, k + 1)
                if j == 1 and l == 0:
                    dump("yb", bf(o_yb, 4096), 4096, BF16, [t_yb])
                kv_phase(j + 1)
                if j == 1 and l == 0:
                    dump("K", bf(o_K, 12288), 12288, BF16, [t_K])
                    dump("V", bf(o_V, 12288), 12288, BF16, [t_V])
                q_proj(j, 0)
                for hd in range(8):
                    if hd + 1 < 8:
                        q_proj(j, hd + 1)
                    attn(j, hd)
                if j == 1 and l == 0:
                    dump("ya", bf(o_vn, 4096), 4096, BF16, [t_vn])
                m_phase(j)
                if j == 1 and l == 0:
                    dump("mg", bf(o_mg, 8192), 8192, BF16, [t_mg])
                pre = (j + 1 <= nfull)
                o_phase(j, early=(lambda j=j: norm_elem_tile(j + 2, 0)) if pre else None)

        for l in range(n_layers):
            do_layer(l)

        assert stream.pos == len(stream.plan), (stream.pos, len(stream.plan))
        last = {}
        for o in out_stores:
            last[o.dma_key] = o
        P.emit(final_dma_ops=list(last.values()) + dbg_ops[-1:])
    return nc


def _a_chunk(W3, col0):
    return W3[:, :, col0:col0 + 128].transpose(1, 0, 2).reshape(128, -1)


def _b_slab(W, col0, kh):
    Wk = W.reshape(-1, 128, W.shape[1])
    return Wk[kh * 8:(kh + 1) * 8, :, col0:col0 + 512].transpose(1, 0, 2).reshape(128, 4096)


def _pair_table(rpb, qrow_g, offs, rows_total, generic):
    kc = np.arange(64)
    qc = np.arange(64)
    cs = np.clip(qc - 8, 0, 48)
    colvalid = (kc[:, None] >= cs[None, :]) & (kc[:, None] < cs[None, :] + 16)
    dc = kc[:, None] - qc[None, :] + 15
    dcc = np.clip(dc, 0, 30)
    out = np.full((2, 64, len(offs), 8, 2, 64), NEG, dtype=np.float32)
    for i, o in enumerate(offs):
        for kr in range(2):
            for qr in range(2):
                krow = qrow_g + 2 * o + kr
                qrow = qrow_g + qr
                if generic:
                    rs = qrow - 4
                else:
                    rs = min(max(qrow - 4, 0), rows_total - 8)
                    if qrow < 0 or qrow >= rows_total:
                        rs = qrow - 4
                dr = krow - qrow + 7
                if krow < rs or krow >= rs + 8 or dr < 0 or dr > 14:
                    continue
                vals = rpb[:, dr, :][:, dcc]
                sel = np.where(colvalid[None], vals, np.float32(NEG))
                out[kr, :, i, :, qr, :] = sel.transpose(1, 0, 2)
    return out.reshape(128, len(offs), 8, 128)


def _layer_arrays(lw, wsl):
    (pre_g, post_g, w_in, rpb, ln_g, ln_b, sg_w, sg_b, w_pa, w_pb, w_out) = lw
    W3 = w_in.reshape(16, 128, NIN)
    for i in range(4):
        wsl[i, :, 0:2048] = _a_chunk(W3, 1024 + (2 * i) * 128)
        wsl[i, :, 2048:4096] = _a_chunk(W3, 1024 + (2 * i + 1) * 128)
    for cg in range(2):
        for kh in range(2):
            wsl[4 + cg * 2 + kh] = _b_slab(w_in, 2048 + cg * 512, kh)
            wsl[8 + cg * 2 + kh] = _b_slab(w_in, 5120 + cg * 512, kh)
    for g in range(8):
        wsl[12 + g, :, 0:2048] = _a_chunk(W3, 4096 + g * 128)
        wsl[12 + g, :, 2048:4096] = _a_chunk(W3, 6144 + g * 128)
        wsl[20 + g, :, 0:2048] = _a_chunk(W3, g * 128)
        wsl[20 + g, :, 2048:4096] = _a_chunk(W3, 3072 + g * 128)
    PA3 = w_pa.reshape(8, 128, D)
    PB3 = w_pb.reshape(8, 128, D)
    for j in range(16):
        wsl[28 + 2 * j, :, 0:2048] = _a_chunk(W3, 7168 + j * 128)
        wsl[28 + 2 * j, :, 2048:3072] = _a_chunk(PA3, j * 128)
        wsl[29 + 2 * j, :, 0:2048] = _a_chunk(W3, 9216 + j * 128)
        wsl[29 + 2 * j, :, 2048:3072] = _a_chunk(PB3, j * 128)
    for cg in range(4):
        for kh in range(2):
            wsl[60 + cg * 2 + kh] = _b_slab(w_out, cg * 512, kh)
    return wsl


def _gains(pre_g):
    gc = pre_g.reshape(16, 128).T
    e = np.arange(SLAB)
    GA = gc[:, (e // 128) % 16]
    GB0 = gc[:, e // 512]
    GB1 = gc[:, 8 + e // 512]
    GM = np.where(e[None, :] < 2048, GA, np.float32(1.0))
    return np.stack([GA, GB0, GB1, GM]).astype(np.float32)


_NC_CACHE = {}


def _get_nc(n_layers, NR0):
    key = (n_layers, NR0)
    if key not in _NC_CACHE:
        _NC_CACHE[key] = build_nc(n_layers, NR0)
    return _NC_CACHE[key]


def _core_inputs(x, layers, NR0):
    nl = len(layers)
    ident = np.eye(128, dtype=np.float32).astype(ml_dtypes.bfloat16)
    ones = np.ones((128, 128), dtype=np.float32).astype(ml_dtypes.bfloat16)
    halo = (NR0 - 64) // 2
    xg = x.reshape(2, 256, 64, D)
    wsl = np.zeros((nl, NSW, 128, SLAB), dtype=np.float32)
    gains = np.zeros((nl, 4, 128, SLAB), dtype=np.float32)
    sgb = np.zeros((nl, 128, 1024), dtype=np.float32)
    lnc = np.zeros((nl, 128, 16), dtype=np.float32)
    postg = np.zeros((nl, 128, D), dtype=np.float32)
    for li, lw in enumerate(layers):
        (pre_g, post_g, w_in, rpb, ln_g, ln_b, sg_w, sg_b, w_pa, w_pb, w_out) = lw
        _layer_arrays(lw, wsl[li])
        gains[li] = _gains(pre_g)
        sgb[li] = np.broadcast_to(sg_b.reshape(1, 1024), (128, 1024))
        lnc[li, :, 0:8] = ln_g.reshape(8, 128).T
        lnc[li, :, 8:16] = ln_b.reshape(8, 128).T
        postg[li] = np.broadcast_to(post_g.reshape(1, D), (128, D))
        wsl[li, 70, :, 0:1024] = sg_w.transpose(2, 0, 1).reshape(128, 1024)
        g5 = _pair_table(rpb, 0, list(range(-2, 3)), 256, True).reshape(128, 5120)
        wsl[li, 68] = g5[:, 0:4096]
        wsl[li, 69, :, 0:1024] = g5[:, 4096:5120]
    in_maps = []
    for c in range(8):
        bi, p = c // 4, c % 4
        R0 = 64 * p
        xl = np.zeros((NR0, 64, D), dtype=np.float32)
        lo, hi = R0 - halo, R0 + 64 + halo
        slo, shi = max(lo, 0), min(hi, 256)
        xl[slo - lo:shi - lo] = xg[bi, slo:shi]
        spt = np.zeros((nl, 8, 128, 3072), dtype=np.float32)
        for li, lw in enumerate(layers):
            rpb = lw[3]
            specs = [(R0, list(range(-2, 4))), (R0 + 2, list(range(-2, 3))),
                     (R0 + 60, list(range(-2, 3))), (R0 + 62, list(range(-3, 3)))]
            for spi, (qrow, offs) in enumerate(specs):
                tb = _pair_table(rpb, qrow, offs, 256, False)
                full = np.full((128, 6, 8, 128), NEG, dtype=np.float32)
                full[:, 0:len(offs)] = tb
                hm = full.transpose(0, 2, 1, 3)
                spt[li, spi * 2 + 0] = hm[:, 0:4].reshape(128, 3072)
                spt[li, spi * 2 + 1] = hm[:, 4:8].reshape(128, 3072)
        in_maps.append({"xin": xl.reshape(NR0 * 64, D), "wsl": wsl, "spt": spt, "gains": gains, "sgb": sgb,
                        "lnc": lnc, "postg": postg, "ident": ident, "ones": ones})
    return in_maps


FUSED = True


def kernel(x, pre_norm_g, post_norm_g, w_in, na_rpb, sg_ln_g, sg_ln_b, sg_w, sg_b, w_proj_a, w_proj_b, w_out):
    f = lambda a: np.ascontiguousarray(np.asarray(a, dtype=np.float32))
    x = f(x)
    layers = []
    for l in range(2):
        layers.append(tuple(f(a[l]) for a in (pre_norm_g, post_norm_g, w_in, na_rpb, sg_ln_g, sg_ln_b, sg_w, sg_b,
                                               w_proj_a, w_proj_b, w_out)))
    if FUSED:
        nc = _get_nc(2, 80)
        in_maps = _core_inputs(x, layers, 80)
        res = run_bass_kernel_spmd(nc, in_maps, core_ids=list(range(8)))
        out = np.zeros((2, 256, 64, D), dtype=np.float32)
        for c in range(8):
            bi, p = c // 4, c % 4
            out[bi, 64 * p:64 * p + 64] = res.results[c]["y"].reshape(64, 64, D)
        return out.reshape(2, 16384, D)
    nc = _get_nc(1, 72)
    cur = x
    for l in range(2):
        in_maps = _core_inputs(cur, [layers[l]], 72)
        res = run_bass_kernel_spmd(nc, in_maps, core_ids=list(range(8)))
        out = np.zeros((2, 256, 64, D), dtype=np.float32)
        for c in range(8):
            bi, p = c // 4, c % 4
            out[bi, 64 * p:64 * p + 64] = res.results[c]["y"].reshape(64, 64, D)
        cur = out.reshape(2, 16384, D)
    return cur
```

```python
from contextlib import ExitStack
import numpy as np
import ml_dtypes
import concourse.bass as bass
import concourse.mybir as mybir
from concourse.bass_utils import run_bass_kernel_spmd

F32 = mybir.dt.float32
BF16 = mybir.dt.bfloat16
AF = mybir.ActivationFunctionType
ALU = mybir.AluOpType
AX = mybir.AxisListType

ENGS = ("pe", "act", "dve", "pool", "sp")

D = 2048
NIN = 11264
NS = 79
SLAB = 4096
QSCALE = 128 ** -0.5
EXPSILU = True
TAILQ = "pool"
NEG = -30000.0


class Tok:
    __slots__ = ("name", "w", "r")

    def __init__(self, name):
        self.name = name
        self.w = None
        self.r = []


class Op:
    __slots__ = ("eng", "fn", "idx", "deps", "signal", "dma_key", "cnt")


class Prog:
    def __init__(self, nc):
        self.nc = nc
        self.ops = {e: [] for e in ENGS}
        self.dma_counts = {}

    def op(self, eng, fn, reads=(), writes=(), dma_key=None):
        o = Op()
        o.eng = eng
        o.fn = fn
        o.idx = len(self.ops[eng])
        o.signal = False
        o.dma_key = dma_key
        o.cnt = 0
        if dma_key is not None:
            c = self.dma_counts.get(dma_key, 0) + 16
            self.dma_counts[dma_key] = c
            o.cnt = c
        deps = {}

        def add(d, kind):
            if d is None or d is o:
                return
            if d.dma_key is None and d.eng == eng:
                if eng == "pe":
                    return
                if kind != "raw":
                    return
                if o.idx - d.idx > 8:
                    return
            deps[id(d)] = d

        for t in reads:
            add(t.w, "raw")
        for t in writes:
            add(t.w, "waw")
            for r in t.r:
                add(r, "war")
        for t in reads:
            t.r.append(o)
        for t in writes:
            t.w = o
            t.r = []
        o.deps = list(deps.values())
        for d in o.deps:
            if d.dma_key is None:
                d.signal = True
        self.ops[eng].append(o)
        return o

    def emit(self, final_dma_ops=()):
        nc = self.nc
        with ExitStack() as es:
            sems = {e: es.enter_context(nc.semaphore(f"s_{e}")) for e in ENGS}
            dsems = {k: es.enter_context(nc.semaphore(f"d_{k}")) for k in self.dma_counts}
            for e in ENGS:
                c = 0
                for o in self.ops[e]:
                    if o.dma_key is None and o.signal:
                        c += 1
                        o.cnt = c
            block = es.enter_context(nc.Block())

            def run(e, h):
                seen = {}
                for o in self.ops[e]:
                    need = {}
                    for d in o.deps:
                        s = dsems[d.dma_key] if d.dma_key is not None else sems[d.eng]
                        k = id(s)
                        if need.get(k, (None, 0))[1] < d.cnt:
                            need[k] = (s, d.cnt)
                    for k, (s, v) in need.items():
                        if seen.get(k, 0) < v:
                            h.wait_ge(s, v)
                            seen[k] = v
                    ins = o.fn(h)
                    if o.dma_key is not None:
                        ins.then_inc(dsems[o.dma_key], 16)
                    elif o.signal:
                        ins.then_inc(sems[e], 1)
                if e == "sp":
                    for o in final_dma_ops:
                        h.wait_ge(dsems[o.dma_key], o.cnt)

            @block.tensor
            def _(h):
                run("pe", h)

            @block.scalar
            def _(h):
                run("act", h)

            @block.vector
            def _(h):
                run("dve", h)

            @block.gpsimd
            def _(h):
                run("pool", h)

            @block.sync
            def _(h):
                run("sp", h)


NSW = 71


def slab_used(s):
    if s < 28:
        return 4096
    if s < 60:
        return 3072
    if s < 68:
        return 4096
    if s == 68:
        return 4096
    if s == 69:
        return 1024
    if s == 70:
        return 1024
    return 3072


def slab_gain(s):
    if s < 4:
        return 0
    if s < 12:
        return 1 + (s % 2)
    if s < 28:
        return 0
    if s < 60:
        return 3
    return None


def build_nc(n_layers, NR0, debug=False):
    nc = bass.Bass("TRN2", target_bir_lowering=False)
    dbg_ops = []

    def din(name, shape, dt=F32):
        return nc.dram_tensor(name, shape, dt, kind="ExternalInput").ap()

    xin = din("xin", [NR0 * 64, D])
    wsl = din("wsl", [n_layers, NSW, 128, SLAB])
    spt = din("spt", [n_layers, 8, 128, 3072])
    gains = din("gains", [n_layers, 4, 128, SLAB])
    sgb = din("sgb", [n_layers, 128, 1024])
    lnc = din("lnc", [n_layers, 128, 16])
    postg = din("postg", [n_layers, 128, D])
    ident_d = din("ident", [128, 128], BF16)
    ones_d = din("ones", [128, 128], BF16)
    NRL = NR0 - 8 * n_layers
    y = nc.dram_tensor("y", [NRL * 64, D], F32, kind="ExternalOutput").ap()
    wq = nc.dram_tensor("wq", [n_layers, NS, 128, SLAB], BF16, kind="Internal").ap()
    x1s = None
    if n_layers == 2:
        x1s = nc.dram_tensor("x1s", [(NR0 - 8) * 64, D], F32, kind="Internal").ap()

    with ExitStack() as es:
        off = [0]

        def alloc(nbytes):
            o = off[0]
            off[0] += (nbytes + 63) // 64 * 64
            return o

        o_hT = [alloc(16384), alloc(16384)]
        o_K = alloc(24576)
        o_V = alloc(24576)
        o_slab = [alloc(8192) for _ in range(3)]
        o_xn = alloc(8192)
        o_xr = alloc(8192)
        o_xs = alloc(4096)
        o_qT = [alloc(1024), alloc(1024)]
        o_sza = [alloc(2048), alloc(2048)]
        o_gt = alloc(2048)
        o_szb = alloc(2048)
        o_vn = alloc(8192)
        o_yb = alloc(8192)
        o_pT = [alloc(1536), alloc(1536)]
        o_mg = alloc(16384)
        o_msg = [alloc(2048), alloc(2048)]
        o_rden = alloc(2048)
        o_at = alloc(2048)
        o_bg = alloc(10240)
        o_sp = [alloc(1536), alloc(1536)]
        o_wst = alloc(2048)
        o_tt = alloc(4096)
        o_lnc = alloc(64)
        o_pg = alloc(8192)
        o_id = alloc(256)
        o_on = alloc(256)
        o_st = alloc(1024)
        o_junk = alloc(1024)
        TOTAL = off[0]
        assert TOTAL <= 212000, TOTAL
        A = es.enter_context(nc.sbuf_tensor("arena", [128, TOTAL // 2], BF16))
        PS = es.enter_context(nc.psum_tensor("ps", [128, 4096], F32))

        def bf(o, n):
            return A[:, o // 2:o // 2 + n]

        def f32(o, n):
            return A[:, o // 2:o // 2 + 2 * n].bitcast(F32)

        def bank(b, n=512, c0=0):
            return PS[:, b * 512 + c0:b * 512 + c0 + n]

        def bank_bf(b, n):
            return PS[:, b * 512:b * 512 + (n + 1) // 2].bitcast(BF16)

        hT = [bf(o, 8192).rearrange("p (k t) -> p k t", k=16) for o in o_hT]
        Kr = bf(o_K, 12288).rearrange("p (h t) -> p h t", h=8)
        Vr = bf(o_V, 12288).rearrange("p (s c) -> p s c", s=12)
        slabs = [bf(o, 4096) for o in o_slab]
        xn = f32(o_xn, 2048)
        xr = f32(o_xr, 2048)
        xs = bf(o_xs, 2048)
        qT = [bf(o, 512) for o in o_qT]
        sza = [f32(o, 512) for o in o_sza]
        gt = f32(o_gt, 512)
        szb = f32(o_szb, 512)
        vn = bf(o_vn, 4096).rearrange("p (t c) -> p t c", t=4)
        yaT = bf(o_vn, 4096).rearrange("p (h t) -> p h t", h=8)
        ybT = bf(o_yb, 4096).rearrange("p (h t) -> p h t", h=8)
        pT = [bf(o, 768) for o in o_pT]
        mgT = bf(o_mg, 8192).rearrange("p (k t) -> p k t", k=16)
        njunk = bf(o_mg, 2048)
        msg = [f32(o, 512) for o in o_msg]
        rden = f32(o_rden, 512)
        atb = f32(o_at, 512)
        bgen = bf(o_bg, 5120).rearrange("p (i h q) -> p i h q", i=5, h=8)
        spb = [bf(o, 768).rearrange("p (i q) -> p i q", i=6) for o in o_sp]
        wst = bf(o_wst, 1024).rearrange("p (g s) -> p g s", g=8)
        ttab = f32(o_tt, 1024).rearrange("p (g s) -> p g s", g=8)
        lncb = f32(o_lnc, 16)
        pgb = f32(o_pg, 2048)
        ident = bf(o_id, 128)
        ones = bf(o_on, 128)
        st = f32(o_st, 256)
        ojunk = bf(o_junk, 512)

        P = Prog(nc)
        T = Tok

        def dump(name, ap, n, dt, toks):
            if not debug:
                return
            d = nc.dram_tensor("dbg_" + name, [128, n], dt, kind="ExternalOutput").ap()
            dbg_ops.append(P.op("sp", lambda h: h.dma_start(out=d, in_=ap), reads=toks, dma_key="dbg"))

        t_hT = [T("hT0"), T("hT1")]
        t_hTx = [[T("hT0a"), T("hT0b")], [T("hT1a"), T("hT1b")]]
        t_K, t_V = T("K"), T("V")
        t_slab = [T(f"slab{i}") for i in range(3)]
        t_xn, t_xr, t_xs = T("xn"), T("xr"), T("xs")
        t_qT = [T("q0"), T("q1")]
        t_sza = [T("sza0"), T("sza1")]
        t_gt, t_szb = T("gt"), T("szb")
        t_vn, t_yb = T("vn"), T("yb")
        t_pT = [T("pT0"), T("pT1")]
        t_mg = T("mg")
        t_msg = [T("msg0"), T("msg1")]
        t_rden, t_at = T("rden"), T("at")
        t_bg = T("bg")
        t_sp = [T("sp0"), T("sp1")]
        t_wst, t_tt, t_lnc, t_pg = T("wst"), T("tt"), T("lnc"), T("pg")
        t_const = T("const")
        t_nst, t_lst, t_ost = T("nst"), T("lst"), T("ost")
        t_junk = T("junk")
        t_bank = [T(f"bank{i}") for i in range(8)]
        t_x1c = [T(f"x1c{i}") for i in range((NR0 - 8) * 64 // 128)]
        t_xrh = [T("xrh0"), T("xrh1")]
        t_lsa, t_lsd = T("lsa"), T("lsd")
        main_toks = (t_hT + t_hTx[0] + t_hTx[1] + [t_K, t_V] + t_slab + [t_xn, t_xr, t_xs] + t_qT + t_sza + [t_gt, t_szb, t_vn, t_yb]
                     + t_pT + [t_mg] + t_msg + [t_rden, t_at, t_bg] + t_sp + [t_wst, t_tt, t_lnc, t_pg, t_const,
                                                                           t_nst, t_lst, t_ost, t_junk, t_lsa, t_lsd] + t_xrh)

        NST = st[:, 0:16]
        LST = st[:, 16:64]
        OST = st[:, 64:96]
        EPS_RMS = st[:, 96:97]
        EPS_LN = st[:, 97:98]

        NB = 2
        pci = [f32(16384 * i, 4096) for i in range(NB)]
        pco = [bf(49152 + 8192 * i, 4096) for i in range(NB)]
        gti = [[f32(73728 + 16384 * (l * 4 + k), 4096) for k in range(4)] for l in range(n_layers)]
        assert 73728 + 16384 * 4 * n_layers <= TOTAL
        t_pci = [T(f"pci{i}") for i in range(NB)]
        t_pco = [T(f"pco{i}") for i in range(NB)]
        t_gain = [T(f"gain{k}") for k in range(4 * n_layers)]
        t_wq = [[T(f"wq{l}_{s}") for s in range(NS)] for l in range(n_layers)]
        jobs = [(l, s) for l in range(n_layers) for s in range(NS)]
        for l in range(n_layers):
            for k in range(4):
                P.op("sp", lambda h, l=l, k=k: h.dma_start(out=gti[l][k], in_=gains[l, k]), writes=[t_gain[l * 4 + k]],
                     dma_key=f"gain{l * 4 + k}")

        def pc_load(q):
            l, s = jobs[q]
            n = slab_used(s)
            i = q % NB
            srcap = wsl[l, s, :, 0:n] if s < NSW else spt[l, s - NSW]
            P.op("sp", lambda h: h.dma_start(out=pci[i][:, 0:n], in_=srcap), writes=[t_pci[i]], dma_key=f"pci{i}")

        def pc_cast(q):
            l, s = jobs[q]
            n = slab_used(s)
            i = q % NB
            g = slab_gain(s)
            if g is None:
                P.op("act", lambda h: h.activation(out=pco[i][:, 0:n], in_=pci[i][:, 0:n], func=AF.Copy),
                     reads=[t_pci[i]], writes=[t_pco[i]])
            else:
                eng = "dve" if (q % 3) != 2 else "pool"
                P.op(eng, lambda h: h.tensor_tensor(out=pco[i][:, 0:n], in0=pci[i][:, 0:n], in1=gti[l][g][:, 0:n], op=ALU.mult),
                     reads=[t_pci[i], t_gain[l * 4 + g]], writes=[t_pco[i]])

        def pc_store(q):
            l, s = jobs[q]
            n = slab_used(s)
            i = q % NB
            P.op("sp", lambda h: h.dma_start(out=wq[l, s, :, 0:n], in_=pco[i][:, 0:n]),
                 reads=[t_pco[i]], writes=[t_wq[l][s]], dma_key=f"pco{i}")

        pc_load(0)
        for q in range(len(jobs)):
            if q + 1 < len(jobs):
                pc_load(q + 1)
            pc_cast(q)
            pc_store(q)
        P.op("pool", lambda h: h.memset(st[:, 128:129], 0.0), writes=t_pci + t_pco + t_gain + main_toks)
        P.op("pool", lambda h: h.memset(EPS_RMS, 1e-6), writes=[t_const])
        P.op("pool", lambda h: h.memset(EPS_LN, 1e-5), writes=[t_const])
        P.op("sp", lambda h: h.dma_start(out=ident, in_=ident_d), writes=[t_const], dma_key="const")
        P.op("sp", lambda h: h.dma_start(out=ones, in_=ones_d), writes=[t_const], dma_key="const")

        class Stream:
            def __init__(self):
                self.plan = []
                self.issued = 0
                self.pos = 0

            def issue(self, upto):
                while self.issued < min(upto, len(self.plan)):
                    i = self.issued
                    l, s = self.plan[i]
                    n = slab_used(s)
                    b = i % 3
                    P.op("sp", lambda h, l=l, s=s, n=n, b=b: h.dma_start(out=slabs[b][:, 0:n], in_=wq[l, s, :, 0:n]),
                         reads=[t_wq[l][s]], writes=[t_slab[b]], dma_key=f"slab{b}")
                    self.issued += 1

            def get(self, l, s):
                i = self.pos
                assert self.plan[i] == (l, s), (i, self.plan[i], l, s)
                self.issue(i + 3)
                self.pos += 1
                return slabs[i % 3], t_slab[i % 3]

        stream = Stream()

        def layer_plan(l, nfull):
            pl = []
            kv = [(l, s) for s in range(0, 8)]
            pl += kv + kv
            for s_ in range(nfull):
                pl += [(l, s) for s in range(8, 12)]
                pl += [(l, s) for s in range(12, 20)]
                pl += kv
                pl += [(l, s) for s in range(20, 28)]
                pl += [(l, s) for s in range(28, 60)]
                pl += [(l, s) for s in range(60, 68)]
            return pl

        for l in range(n_layers):
            NR = NR0 - 8 * l
            stream.plan += layer_plan(l, (NR - 8) // 8)

        out_stores = []

        def do_layer(l):
            NR = NR0 - 8 * l
            nfull = (NR - 8) // 8
            src = xin if l == 0 else x1s
            dst = y if l == n_layers - 1 else x1s
            def src_tok(r0):
                return [t_x1c[r0 // 128]] if l > 0 else []

            def dst_tok(q0):
                return [t_x1c[q0 // 128]] if l < n_layers - 1 else []

            xrh = [f32(o_xr, 1024), f32(o_xr + 4096, 1024)]
            e0 = 8 - 4 * l if NR0 == 80 else 4
            if NR0 != 80:
                assert n_layers == 1

            def blk(j):
                if j == 0:
                    return 0, 256, 0
                if j == nfull + 1:
                    return NR - 4, 256, (NR - 4) // 2
                return 4 + 8 * (j - 1), 512, 2 + 4 * (j - 1)

            def slot(pi):
                return (pi + 2) % 12

            P.op("sp", lambda h, l=l: h.dma_start(out=bf(o_bg, 4096), in_=wq[l, 68, :, 0:4096]),
                 reads=[t_wq[l][68]], writes=[t_bg], dma_key="lc_bg")
            P.op("sp", lambda h, l=l: h.dma_start(out=bf(o_bg + 8192, 1024), in_=wq[l, 69, :, 0:1024]),
                 reads=[t_wq[l][69]], writes=[t_bg], dma_key="lc_bg")
            P.op("sp", lambda h, l=l: h.dma_start(out=bf(o_wst, 1024), in_=wq[l, 70, :, 0:1024]),
                 reads=[t_wq[l][70]], writes=[t_wst], dma_key="lc_wst")
            P.op("sp", lambda h, l=l: h.dma_start(out=f32(o_tt, 1024), in_=sgb[l]), writes=[t_tt], dma_key="lc_tt")
            P.op("sp", lambda h, l=l: h.dma_start(out=lncb, in_=lnc[l]), writes=[t_lnc], dma_key="lc_lnc")
            P.op("sp", lambda h, l=l: h.dma_start(out=pgb, in_=postg[l]), writes=[t_pg], dma_key="lc_pg")
            for c in range(2):
                P.op("pe", lambda h, c=c: h.matmul(bank(c), lhsT=ones, rhs=bf(o_wst + c * 1024, 512), start=True, stop=True),
                     reads=[t_const, t_wst], writes=[t_bank[c]])
            for g in range(8):
                P.op("dve", lambda h, g=g: h.scalar_tensor_tensor(
                    out=ttab[:, g, :], in0=bank(g // 4, 128, (g % 4) * 128), scalar=lncb[:, 8 + g:9 + g],
                    in1=ttab[:, g, :], op0=ALU.mult, op1=ALU.add),
                    reads=[t_bank[g // 4], t_lnc, t_tt], writes=[t_tt])

            def norm_elem_tile(j, t):
                row0, ntok, _ = blk(j)
                r0 = row0 * 64 + t * 128
                P.op("sp", lambda h: h.dma_start(out=xn, in_=src[r0:r0 + 128, :]),
                     reads=src_tok(r0), writes=[t_xn], dma_key="xn")
                P.op("act", lambda h: h.activation(out=xs, in_=xn, func=AF.Square, accum_out=NST[:, t:t + 1]),
                     reads=[t_xn], writes=[t_xs, t_nst])
                P.op("act", lambda h: h.activation(out=NST[:, 4 + t:5 + t], in_=NST[:, t:t + 1], func=AF.Sqrt,
                                                   scale=1.0 / D, bias=EPS_RMS),
                     reads=[t_nst, t_const], writes=[t_nst])
                P.op("dve", lambda h: h.reciprocal(out=NST[:, 8 + t:9 + t], in_=NST[:, 4 + t:5 + t]),
                     reads=[t_nst], writes=[t_nst])
                P.op("act", lambda h: h.activation(out=xs, in_=xn, func=AF.Copy, scale=NST[:, 8 + t:9 + t]),
                     reads=[t_xn, t_nst], writes=[t_xs])

            def norm_trans_tile(j, t):
                par = j % 2
                for kc in range(16):
                    P.op("pe", lambda h, kc=kc: h.transpose(out=bank_bf(6, 2048)[:, kc * 128:(kc + 1) * 128],
                                                            in_=xs[:, kc * 128:(kc + 1) * 128], identity=ident),
                         reads=[t_xs, t_const], writes=[t_bank[6], t_bank[7]])
                P.op("dve", lambda h: h.tensor_copy(
                    out=hT[par][:, :, t * 128:(t + 1) * 128],
                    in_=bank_bf(6, 2048).rearrange("p (k c) -> p k c", c=128)),
                    reads=[t_bank[6], t_bank[7]], writes=[t_hT[par]] + t_hTx[par])

            def norm_elem(j):
                for t in range(blk(j)[1] // 128):
                    norm_elem_tile(j, t)
                    norm_trans_tile(j, t)

            def kv_phase(j):
                row0, ntok, p0 = blk(j)
                nt = ntok // 128
                par = j % 2
                s0 = slot(p0)
                for i in range(4):
                    sl, tsl = stream.get(l, i)
                    for ch in range(2):
                        hd = 2 * i + ch
                        b = (2 * i + ch) % 8
                        for kc in range(16):
                            P.op("pe", lambda h, sl=sl, ch=ch, kc=kc, b=b: h.matmul(
                                bank(b, ntok), lhsT=sl[:, ch * 2048 + kc * 128:ch * 2048 + (kc + 1) * 128],
                                rhs=hT[par][:, kc, 0:ntok], start=(kc == 0), stop=(kc == 15)),
                                reads=[tsl, t_hT[par]], writes=[t_bank[b]])
                        P.op("dve", lambda h, hd=hd, b=b: h.tensor_copy(out=Kr[:, hd, s0 * 128:s0 * 128 + ntok], in_=bank(b, ntok)),
                             reads=[t_bank[b]], writes=[t_K])
                for cg in range(2):
                    for kh in range(2):
                        sl, tsl = stream.get(l, 4 + cg * 2 + kh)
                        for t in range(nt):
                            b = cg * 4 + t
                            for kcl in range(8):
                                kc = kh * 8 + kcl
                                P.op("pe", lambda h, sl=sl, t=t, kc=kc, kcl=kcl, b=b: h.matmul(
                                    bank(b), lhsT=hT[par][:, kc, t * 128:(t + 1) * 128],
                                    rhs=sl[:, kcl * 512:(kcl + 1) * 512], start=(kc == 0), stop=(kc == 15)),
                                    reads=[tsl, t_hT[par]], writes=[t_bank[b]])
                    for t in range(nt):
                        b = cg * 4 + t
                        P.op("act", lambda h, t=t, b=b, cg=cg: h.activation(
                            out=Vr[:, s0 + t, cg * 512:(cg + 1) * 512], in_=bank(b), func=AF.Copy),
                            reads=[t_bank[b]], writes=[t_V])

            def vs_slab(j, cg, kh):
                par = j % 2
                sl, tsl = stream.get(l, 8 + cg * 2 + kh)
                for t in range(4):
                    b = cg * 4 + t
                    for kcl in range(8):
                        kc = kh * 8 + kcl
                        P.op("pe", lambda h, sl=sl, t=t, kc=kc, kcl=kcl, b=b: h.matmul(
                            bank(b), lhsT=hT[par][:, kc, t * 128:(t + 1) * 128],
                            rhs=sl[:, kcl * 512:(kcl + 1) * 512], start=(kc == 0), stop=(kc == 15)),
                            reads=[tsl, t_hT[par]], writes=[t_bank[b]])

            def g_phase_ln(j):
                for cg in range(2):
                    for t in range(4):
                        b = cg * 4 + t
                        c = cg * 4 + t
                        P.op("act", lambda h, b=b, c=c: h.activation(out=ojunk, in_=bank(b), func=AF.Square,
                                                                     accum_out=LST[:, 8 + c:9 + c]),
                             reads=[t_bank[b]], writes=[t_junk, t_lst])
                        P.op("dve", lambda h, b=b, c=c: h.tensor_reduce(out=LST[:, c:c + 1], in_=bank(b), axis=AX.X, op=ALU.add),
                             reads=[t_bank[b]], writes=[t_lst])
                rl = [t_lst]
                P.op("dve", lambda h: h.tensor_tensor(out=LST[:, 16:20], in0=LST[:, 0:4], in1=LST[:, 4:8], op=ALU.add), reads=rl, writes=rl)
                P.op("dve", lambda h: h.tensor_tensor(out=LST[:, 20:24], in0=LST[:, 8:12], in1=LST[:, 12:16], op=ALU.add), reads=rl, writes=rl)
                P.op("dve", lambda h: h.tensor_scalar(out=LST[:, 16:20], in0=LST[:, 16:20], scalar1=1.0 / 1024, scalar2=0.0,
                                                      op0=ALU.mult, op1=ALU.add), reads=rl, writes=rl)
                P.op("dve", lambda h: h.tensor_tensor(out=LST[:, 24:28], in0=LST[:, 16:20], in1=LST[:, 16:20], op=ALU.mult), reads=rl, writes=rl)
                P.op("dve", lambda h: h.scalar_tensor_tensor(out=LST[:, 24:28], in0=LST[:, 20:24], scalar=1.0 / 1024,
                                                             in1=LST[:, 24:28], op0=ALU.mult, op1=ALU.subtract), reads=rl, writes=rl)
                P.op("act", lambda h: h.activation(out=LST[:, 24:28], in_=LST[:, 24:28], func=AF.Sqrt, scale=1.0, bias=EPS_LN),
                     reads=[t_lst, t_const], writes=rl)
                P.op("dve", lambda h: h.reciprocal(out=LST[:, 28:32], in_=LST[:, 24:28]), reads=rl, writes=rl)
                P.op("dve", lambda h: h.scalar_tensor_tensor(out=LST[:, 32:36], in0=LST[:, 16:20], scalar=-1.0,
                                                             in1=LST[:, 28:32], op0=ALU.mult, op1=ALU.mult), reads=rl, writes=rl)
                for cg in range(2):
                    for t in range(4):
                        b = cg * 4 + t
                        P.op("act", lambda h, t=t, b=b, cg=cg: h.activation(
                            out=vn[:, t, cg * 512:(cg + 1) * 512], in_=bank(b), func=AF.Identity,
                            scale=LST[:, 28 + t:29 + t], bias=LST[:, 32 + t:33 + t]),
                            reads=[t_bank[b], t_lst], writes=[t_vn])

            def g_phase_uz(j, g):
                par = j % 2
                sl, tsl = stream.get(l, 12 + g)
                bs = (g % 2) * 3
                bu, bz, bsx = bs, bs + 1, bs + 2
                for which, b in ((0, bu), (1, bz)):
                    for kc in range(16):
                        P.op("pe", lambda h, sl=sl, which=which, kc=kc, b=b: h.matmul(
                            bank(b), lhsT=sl[:, which * 2048 + kc * 128:which * 2048 + (kc + 1) * 128],
                            rhs=hT[par][:, kc, :], start=(kc == 0), stop=(kc == 15)),
                            reads=[tsl, t_hT[par]], writes=[t_bank[b]])
                for t in range(4):
                    P.op("pe", lambda h, t=t, g=g, b=bsx: h.matmul(
                        bank(b, 128, t * 128), lhsT=vn[:, t, g * 128:(g + 1) * 128], rhs=wst[:, g, :], start=True, stop=True),
                        reads=[t_vn, t_wst], writes=[t_bank[bsx]])
                P.op("act", lambda h, b=bz: h.activation(out=szb, in_=bank(b), func=AF.Silu), reads=[t_bank[bz]], writes=[t_szb])
                for t in range(4):
                    P.op("dve", lambda h, t=t, g=g, b=bsx: h.scalar_tensor_tensor(
                        out=gt[:, t * 128:(t + 1) * 128], in0=bank(b, 128, t * 128), scalar=lncb[:, g:g + 1],
                        in1=ttab[:, g, :], op0=ALU.mult, op1=ALU.add),
                        reads=[t_bank[bsx], t_lnc, t_tt], writes=[t_gt])
                P.op("dve", lambda h, b=bu: h.tensor_tensor(out=gt, in0=gt, in1=bank(b), op=ALU.mult),
                     reads=[t_gt, t_bank[bu]], writes=[t_gt])
                P.op("pool", lambda h, g=g: h.tensor_tensor(out=ybT[:, g, :], in0=gt, in1=szb, op=ALU.mult),
                     reads=[t_gt, t_szb], writes=[t_yb])

            def q_proj(j, hd):
                par = j % 2
                sl, tsl = stream.get(l, 20 + hd)
                for which, b in ((0, 0), (1, 1)):
                    for kc in range(16):
                        P.op("pe", lambda h, sl=sl, which=which, kc=kc, b=b: h.matmul(
                            bank(b), lhsT=sl[:, which * 2048 + kc * 128:which * 2048 + (kc + 1) * 128],
                            rhs=hT[par][:, kc, :], start=(kc == 0), stop=(kc == 15)),
                            reads=[tsl, t_hT[par]], writes=[t_bank[b]])
                P.op("dve", lambda h, hd=hd: h.tensor_scalar(out=qT[hd % 2], in0=bank(0), scalar1=QSCALE, scalar2=0.0,
                                                             op0=ALU.mult, op1=ALU.add),
                     reads=[t_bank[0]], writes=[t_qT[hd % 2]])
                if EXPSILU:
                    P.op("act", lambda h, hd=hd: h.activation(out=sza[hd % 2], in_=bank(1), func=AF.Exp, scale=-1.0),
                         reads=[t_bank[1]], writes=[t_sza[hd % 2]])
                    P.op("dve", lambda h, hd=hd: h.tensor_scalar(out=sza[hd % 2], in0=sza[hd % 2], scalar1=1.0, scalar2=0.0,
                                                                 op0=ALU.add, op1=ALU.add),
                         reads=[t_sza[hd % 2]], writes=[t_sza[hd % 2]])
                    P.op("dve", lambda h, hd=hd: h.reciprocal(out=sza[hd % 2], in_=sza[hd % 2]),
                         reads=[t_sza[hd % 2]], writes=[t_sza[hd % 2]])
                    P.op("dve", lambda h, hd=hd: h.tensor_tensor(out=sza[hd % 2], in0=sza[hd % 2], in1=bank(1), op=ALU.mult),
                         reads=[t_bank[1], t_sza[hd % 2]], writes=[t_sza[hd % 2]])
                else:
                    P.op("act", lambda h, hd=hd: h.activation(out=sza[hd % 2], in_=bank(1), func=AF.Silu),
                         reads=[t_bank[1]], writes=[t_sza[hd % 2]])

            sp_ctr = [0]

            def attn(j, hd):
                _, _, p0 = blk(j)
                tiles = []
                for t in range(4):
                    tp = p0 + t
                    row = 2 * tp
                    spi = None
                    if row == e0:
                        spi, offs = 0, list(range(-2, 4))
                    elif row == e0 + 2:
                        spi, offs = 1, list(range(-2, 3))
                    elif row == e0 + 60:
                        spi, offs = 2, list(range(-2, 3))
                    elif row == e0 + 62:
                        spi, offs = 3, list(range(-3, 3))
                    else:
                        offs = list(range(-2, 3))
                    tiles.append((tp, spi, offs))

                def S(t):
                    tp, spi, offs = tiles[t]
                    bx, by = (4, 5) if t % 2 == 0 else (6, 7)
                    tb = None
                    ttb = t_bg
                    if spi is not None:
                        k = sp_ctr[0] % 2
                        sp_ctr[0] += 1
                        sidx = 71 + spi * 2 + hd // 4
                        P.op("sp", lambda h, k=k, sidx=sidx: h.dma_start(
                            out=bf(o_sp[k], 768), in_=wq[l, sidx, :, (hd % 4) * 768:(hd % 4) * 768 + 768]),
                            reads=[t_wq[l][sidx]], writes=[t_sp[k]], dma_key=f"spt{k}")
                        tb = spb[k]
                        ttb = t_sp[k]
                    for i, o in enumerate(offs):
                        pi = tp + o
                        sk = slot(pi)
                        b = bx if i < 4 else by
                        c0 = (i % 4) * 128
                        P.op("pe", lambda h, sk=sk, b=b, c0=c0, t=t: h.matmul(
                            bank(b, 128, c0), lhsT=Kr[:, hd, sk * 128:(sk + 1) * 128], rhs=qT[hd % 2][:, t * 128:(t + 1) * 128],
                            start=True, stop=False),
                            reads=[t_K, t_qT[hd % 2]], writes=[t_bank[b]])
                        if tb is None:
                            rhs = bgen[:, i, hd, :]
                        else:
                            rhs = tb[:, i, :]
                        P.op("pe", lambda h, b=b, c0=c0, rhs=rhs: h.matmul(bank(b, 128, c0), lhsT=ident, rhs=rhs, start=False, stop=True),
                             reads=[t_const, ttb], writes=[t_bank[b]])

                def E(t):
                    tp, spi, offs = tiles[t]
                    bx, by = (4, 5) if t % 2 == 0 else (6, 7)
                    n2 = (len(offs) - 4) * 128
                    P.op("act", lambda h, t=t, bx=bx: h.activation(out=pT[t % 2][:, 0:512], in_=bank(bx), func=AF.Exp),
                         reads=[t_bank[bx]], writes=[t_pT[t % 2]])
                    P.op("act", lambda h, t=t, by=by, n2=n2: h.activation(out=pT[t % 2][:, 512:512 + n2], in_=bank(by, n2), func=AF.Exp),
                         reads=[t_bank[by]], writes=[t_pT[t % 2]])

                def PV(t):
                    tp, spi, offs = tiles[t]
                    n = len(offs)
                    for i, o in enumerate(offs):
                        sk = slot(tp + o)
                        P.op("pe", lambda h, sk=sk, i=i, t=t, n=n: h.matmul(
                            bank(2, 128, t * 128), lhsT=Vr[:, sk, hd * 128:(hd + 1) * 128], rhs=pT[t % 2][:, i * 128:(i + 1) * 128],
                            start=(i == 0), stop=(i == n - 1)),
                            reads=[t_V, t_pT[t % 2]], writes=[t_bank[2]])
                        P.op("pe", lambda h, i=i, t=t, n=n: h.matmul(
                            bank(3, 128, t * 128), lhsT=ones, rhs=pT[t % 2][:, i * 128:(i + 1) * 128],
                            start=(i == 0), stop=(i == n - 1)),
                            reads=[t_const, t_pT[t % 2]], writes=[t_bank[3]])

                S(0)
                for t in range(4):
                    if t + 1 < 4:
                        S(t + 1)
                    E(t)
                    PV(t)
                P.op("dve", lambda h: h.reciprocal(out=rden, in_=bank(3)), reads=[t_bank[3]], writes=[t_rden])
                P.op("dve", lambda h: h.tensor_tensor(out=atb, in0=bank(2), in1=rden, op=ALU.mult),
                     reads=[t_bank[2], t_rden], writes=[t_at])
                P.op("pool", lambda h: h.tensor_tensor(out=yaT[:, hd, :], in0=atb, in1=sza[hd % 2], op=ALU.mult),
                     reads=[t_at, t_sza[hd % 2]], writes=[t_vn])

            def m_phase(j):
                par = j % 2
                for jj in range(16):
                    bs = (jj % 2) * 4
                    for half in range(2):
                        sl, tsl = stream.get(l, 28 + 2 * jj + half)
                        bg_, bp_ = bs + 2 * half, bs + 2 * half + 1
                        for kc in range(16):
                            P.op("pe", lambda h, sl=sl, kc=kc, b=bg_: h.matmul(
                                bank(b), lhsT=sl[:, kc * 128:(kc + 1) * 128], rhs=hT[par][:, kc, :],
                                start=(kc == 0), stop=(kc == 15)),
                                reads=[tsl, t_hT[par]], writes=[t_bank[bg_]])
                        yT = yaT if half == 0 else ybT
                        ty = t_vn if half == 0 else t_yb
                        for c in range(8):
                            P.op("pe", lambda h, sl=sl, c=c, b=bp_, yT=yT: h.matmul(
                                bank(b), lhsT=sl[:, 2048 + c * 128:2048 + (c + 1) * 128], rhs=yT[:, c, :],
                                start=(c == 0), stop=(c == 7)),
                                reads=[tsl, ty], writes=[t_bank[bp_]])
                        P.op("act", lambda h, half=half, b=bg_: h.activation(out=msg[half], in_=bank(b), func=AF.Sigmoid),
                             reads=[t_bank[bg_]], writes=[t_msg[half]])
                        P.op("dve", lambda h, half=half, b=bp_: h.tensor_tensor(out=msg[half], in0=msg[half], in1=bank(b), op=ALU.mult),
                             reads=[t_msg[half], t_bank[bp_]], writes=[t_msg[half]])
                    P.op("pool", lambda h, jj=jj: h.tensor_tensor(out=mgT[:, jj, :], in0=msg[0], in1=msg[1], op=ALU.add),
                         reads=t_msg, writes=[t_mg])

            def o_phase(j, early=None):
                row0, _, _ = blk(j)
                par = j % 2
                ob = [f32(o_msg[0], 2048), f32(o_yb, 2048), f32(o_hT[par], 2048), f32(o_hT[par] + 8192, 2048)]
                t_ob = [[t_msg[0], t_msg[1], t_rden, t_at], [t_yb], [t_hTx[par][0]], [t_hTx[par][1]]]
                for cg in range(4):
                    bs = (cg % 2) * 4
                    for kh in range(2):
                        sl, tsl = stream.get(l, 60 + cg * 2 + kh)
                        for t in range(4):
                            b = bs + t
                            for kcl in range(8):
                                kc = kh * 8 + kcl
                                P.op("pe", lambda h, sl=sl, t=t, kc=kc, kcl=kcl, b=b: h.matmul(
                                    bank(b), lhsT=mgT[:, kc, t * 128:(t + 1) * 128], rhs=sl[:, kcl * 512:(kcl + 1) * 512],
                                    start=(kc == 0), stop=(kc == 15)),
                                    reads=[tsl, t_mg], writes=[t_bank[b]])
                    for t in range(4):
                        b = bs + t
                        wt = t_ob[t] + ([t_hT[par]] if (cg == 0 and t >= 2) else [])
                        P.op("act", lambda h, t=t, b=b, cg=cg: h.activation(out=ob[t][:, cg * 512:(cg + 1) * 512], in_=bank(b), func=AF.Copy),
                             reads=[t_bank[b]], writes=wt)
                        P.op("act", lambda h, t=t, cg=cg: h.activation(out=ojunk, in_=ob[t][:, cg * 512:(cg + 1) * 512], func=AF.Square,
                                                                       accum_out=OST[:, t * 4 + cg:t * 4 + cg + 1]),
                             reads=t_ob[t], writes=[t_junk, t_ost])
                if early is not None:
                    early()
                ro = [t_ost]
                P.op("dve", lambda h: h.tensor_reduce(out=OST[:, 16:20], in_=OST[:, 0:16].rearrange("p (t c) -> p t c", c=4),
                                                      axis=AX.X, op=ALU.add), reads=ro, writes=ro)
                P.op("act", lambda h: h.activation(out=OST[:, 20:24], in_=OST[:, 16:20], func=AF.Sqrt, scale=1.0 / D, bias=EPS_RMS),
                     reads=[t_ost, t_const], writes=ro)
                P.op("dve", lambda h: h.reciprocal(out=OST[:, 24:28], in_=OST[:, 20:24]), reads=ro, writes=ro)
                for t in (1, 2, 3, 0):
                    r0 = row0 * 64 + t * 128
                    q0 = (row0 - 4) * 64 + t * 128
                    P.op(TAILQ, lambda h, r0=r0: h.dma_start(out=xr, in_=src[r0:r0 + 128, :]),
                         reads=src_tok(r0), writes=t_xrh, dma_key="xr")
                    P.op("dve", lambda h, t=t: h.scalar_tensor_tensor(out=ob[t], in0=ob[t], scalar=OST[:, 24 + t:25 + t], in1=pgb,
                                                                      op0=ALU.mult, op1=ALU.mult),
                         reads=t_ob[t] + [t_ost, t_pg], writes=t_ob[t])
                    for hf in range(2):
                        P.op("pool", lambda h, t=t, hf=hf: h.tensor_tensor(out=ob[t][:, hf * 1024:(hf + 1) * 1024],
                                                                          in0=ob[t][:, hf * 1024:(hf + 1) * 1024], in1=xrh[hf], op=ALU.add),
                             reads=t_ob[t] + [t_xrh[hf]], writes=t_ob[t])
                    o = P.op(TAILQ, lambda h, t=t, q0=q0: h.dma_start(out=dst[q0:q0 + 128, :], in_=ob[t]),
                             reads=t_ob[t], writes=dst_tok(q0), dma_key=f"out{t}")
                    if l == n_layers - 1:
                        out_stores.append(o)

            norm_elem(0)
            kv_phase(0)
            norm_elem(1)
            kv_phase(1)
            pre = False
            for s_ in range(nfull):
                j = s_ + 1
                ntn = blk(j + 1)[1] // 128
                if not pre:
                    norm_elem_tile(j + 1, 0)
                vs_slab(j, 0, 0)
                vs_slab(j, 0, 1)
                vs_slab(j, 1, 0)
                vs_slab(j, 1, 1)
                if j == 1 and l == 0:
                    dump("hT", bf(o_hT[1], 8192), 8192, BF16, [t_hT[1]])
                g_phase_ln(j)
                if j == 1 and l == 0:
                    dump("vn", bf(o_vn, 4096), 4096, BF16, [t_vn])
                tsched = {0: 0, 2: 1, 4: 2, 6: 3}
                for g in range(8):
                    g_phase_uz(j, g)
                    k = tsched.get(g)
                    if k is not None and k < ntn:
                        norm_trans_tile(j + 1, k)
                        if k + 1 < ntn:
                            norm_elem_tile(j + 1, k + 1)
                if j == 1 and l == 0:
                    dump("yb", bf(o_yb, 4096), 4096, BF16, [t_yb])
                kv_phase(j + 1)
                if j == 1 and l == 0:
                    dump("K", bf(o_K, 12288), 12288, BF16, [t_K])
                    dump("V", bf(o_V, 12288), 12288, BF16, [t_V])
                q_proj(j, 0)
                for hd in range(8):
                    if hd + 1 < 8:
                        q_proj(j, hd + 1)
                    attn(j, hd)
                if j == 1 and l == 0:
                    dump("ya", bf(o_vn, 4096), 4096, BF16, [t_vn])
                m_phase(j)
                if j == 1 and l == 0:
                    dump("mg", bf(o_mg, 8192), 8192, BF16, [t_mg])
                pre = (j + 1 <= nfull)
                o_phase(j, early=(lambda j=j: norm_elem_tile(j + 2, 0)) if pre else None)

        for l in range(n_layers):
            do_layer(l)

        assert stream.pos == len(stream.plan), (stream.pos, len(stream.plan))
        last = {}
        for o in out_stores:
            last[o.dma_key] = o
        P.emit(final_dma_ops=list(last.values()) + dbg_ops[-1:])
    return nc


def _a_chunk(W3, col0):
    return W3[:, :, col0:col0 + 128].transpose(1, 0, 2).reshape(128, -1)


def _b_slab(W, col0, kh):
    Wk = W.reshape(-1, 128, W.shape[1])
    return Wk[kh * 8:(kh + 1) * 8, :, col0:col0 + 512].transpose(1, 0, 2).reshape(128, 4096)


def _pair_table(rpb, qrow_g, offs, rows_total, generic):
    kc = np.arange(64)
    qc = np.arange(64)
    cs = np.clip(qc - 8, 0, 48)
    colvalid = (kc[:, None] >= cs[None, :]) & (kc[:, None] < cs[None, :] + 16)
    dc = kc[:, None] - qc[None, :] + 15
    dcc = np.clip(dc, 0, 30)
    out = np.full((2, 64, len(offs), 8, 2, 64), NEG, dtype=np.float32)
    for i, o in enumerate(offs):
        for kr in range(2):
            for qr in range(2):
                krow = qrow_g + 2 * o + kr
                qrow = qrow_g + qr
                if generic:
                    rs = qrow - 4
                else:
                    rs = min(max(qrow - 4, 0), rows_total - 8)
                    if qrow < 0 or qrow >= rows_total:
                        rs = qrow - 4
                dr = krow - qrow + 7
                if krow < rs or krow >= rs + 8 or dr < 0 or dr > 14:
                    continue
                vals = rpb[:, dr, :][:, dcc]
                sel = np.where(colvalid[None], vals, np.float32(NEG))
                out[kr, :, i, :, qr, :] = sel.transpose(1, 0, 2)
    return out.reshape(128, len(offs), 8, 128)


def _layer_arrays(lw, wsl):
    (pre_g, post_g, w_in, rpb, ln_g, ln_b, sg_w, sg_b, w_pa, w_pb, w_out) = lw
    W3 = w_in.reshape(16, 128, NIN)
    for i in range(4):
        wsl[i, :, 0:2048] = _a_chunk(W3, 1024 + (2 * i) * 128)
        wsl[i, :, 2048:4096] = _a_chunk(W3, 1024 + (2 * i + 1) * 128)
    for cg in range(2):
        for kh in range(2):
            wsl[4 + cg * 2 + kh] = _b_slab(w_in, 2048 + cg * 512, kh)
            wsl[8 + cg * 2 + kh] = _b_slab(w_in, 5120 + cg * 512, kh)
    for g in range(8):
        wsl[12 + g, :, 0:2048] = _a_chunk(W3, 4096 + g * 128)
        wsl[12 + g, :, 2048:4096] = _a_chunk(W3, 6144 + g * 128)
        wsl[20 + g, :, 0:2048] = _a_chunk(W3, g * 128)
        wsl[20 + g, :, 2048:4096] = _a_chunk(W3, 3072 + g * 128)
    PA3 = w_pa.reshape(8, 128, D)
    PB3 = w_pb.reshape(8, 128, D)
    for j in range(16):
        wsl[28 + 2 * j, :, 0:2048] = _a_chunk(W3, 7168 + j * 128)
        wsl[28 + 2 * j, :, 2048:3072] = _a_chunk(PA3, j * 128)
        wsl[29 + 2 * j, :, 0:2048] = _a_chunk(W3, 9216 + j * 128)
        wsl[29 + 2 * j, :, 2048:3072] = _a_chunk(PB3, j * 128)
    for cg in range(4):
        for kh in range(2):
            wsl[60 + cg * 2 + kh] = _b_slab(w_out, cg * 512, kh)
    return wsl


def _gains(pre_g):
    gc = pre_g.reshape(16, 128).T
    e = np.arange(SLAB)
    GA = gc[:, (e // 128) % 16]
    GB0 = gc[:, e // 512]
    GB1 = gc[:, 8 + e // 512]
    GM = np.where(e[None, :] < 2048, GA, np.float32(1.0))
    return np.stack([GA, GB0, GB1, GM]).astype(np.float32)


_NC_CACHE = {}


def _get_nc(n_layers, NR0):
    key = (n_layers, NR0)
    if key not in _NC_CACHE:
        _NC_CACHE[key] = build_nc(n_layers, NR0)
    return _NC_CACHE[key]


def _core_inputs(x, layers, NR0):
    nl = len(layers)
    ident = np.eye(128, dtype=np.float32).astype(ml_dtypes.bfloat16)
    ones = np.ones((128, 128), dtype=np.float32).astype(ml_dtypes.bfloat16)
    halo = (NR0 - 64) // 2
    xg = x.reshape(2, 256, 64, D)
    wsl = np.zeros((nl, NSW, 128, SLAB), dtype=np.float32)
    gains = np.zeros((nl, 4, 128, SLAB), dtype=np.float32)
    sgb = np.zeros((nl, 128, 1024), dtype=np.float32)
    lnc = np.zeros((nl, 128, 16), dtype=np.float32)
    postg = np.zeros((nl, 128, D), dtype=np.float32)
    for li, lw in enumerate(layers):
        (pre_g, post_g, w_in, rpb, ln_g, ln_b, sg_w, sg_b, w_pa, w_pb, w_out) = lw
        _layer_arrays(lw, wsl[li])
        gains[li] = _gains(pre_g)
        sgb[li] = np.broadcast_to(sg_b.reshape(1, 1024), (128, 1024))
        lnc[li, :, 0:8] = ln_g.reshape(8, 128).T
        lnc[li, :, 8:16] = ln_b.reshape(8, 128).T
        postg[li] = np.broadcast_to(post_g.reshape(1, D), (128, D))
        wsl[li, 70, :, 0:1024] = sg_w.transpose(2, 0, 1).reshape(128, 1024)
        g5 = _pair_table(rpb, 0, list(range(-2, 3)), 256, True).reshape(128, 5120)
        wsl[li, 68] = g5[:, 0:4096]
        wsl[li, 69, :, 0:1024] = g5[:, 4096:5120]
    in_maps = []
    for c in range(8):
        bi, p = c // 4, c % 4
        R0 = 64 * p
        xl = np.zeros((NR0, 64, D), dtype=np.float32)
        lo, hi = R0 - halo, R0 + 64 + halo
        slo, shi = max(lo, 0), min(hi, 256)
        xl[slo - lo:shi - lo] = xg[bi, slo:shi]
        spt = np.zeros((nl, 8, 128, 3072), dtype=np.float32)
        for li, lw in enumerate(layers):
            rpb = lw[3]
            specs = [(R0, list(range(-2, 4))), (R0 + 2, list(range(-2, 3))),
                     (R0 + 60, list(range(-2, 3))), (R0 + 62, list(range(-3, 3)))]
            for spi, (qrow, offs) in enumerate(specs):
                tb = _pair_table(rpb, qrow, offs, 256, False)
                full = np.full((128, 6, 8, 128), NEG, dtype=np.float32)
                full[:, 0:len(offs)] = tb
                hm = full.transpose(0, 2, 1, 3)
                spt[li, spi * 2 + 0] = hm[:, 0:4].reshape(128, 3072)
                spt[li, spi * 2 + 1] = hm[:, 4:8].reshape(128, 3072)
        in_maps.append({"xin": xl.reshape(NR0 * 64, D), "wsl": wsl, "spt": spt, "gains": gains, "sgb": sgb,
                        "lnc": lnc, "postg": postg, "ident": ident, "ones": ones})
    return in_maps


FUSED = True


def kernel(x, pre_norm_g, post_norm_g, w_in, na_rpb, sg_ln_g, sg_ln_b, sg_w, sg_b, w_proj_a, w_proj_b, w_out):
    f = lambda a: np.ascontiguousarray(np.asarray(a, dtype=np.float32))
    x = f(x)
    layers = []
    for l in range(2):
        layers.append(tuple(f(a[l]) for a in (pre_norm_g, post_norm_g, w_in, na_rpb, sg_ln_g, sg_ln_b, sg_w, sg_b,
                                               w_proj_a, w_proj_b, w_out)))
    if FUSED:
        nc = _get_nc(2, 80)
        in_maps = _core_inputs(x, layers, 80)
        res = run_bass_kernel_spmd(nc, in_maps, core_ids=list(range(8)))
        out = np.zeros((2, 256, 64, D), dtype=np.float32)
        for c in range(8):
            bi, p = c // 4, c % 4
            out[bi, 64 * p:64 * p + 64] = res.results[c]["y"].reshape(64, 64, D)
        return out.reshape(2, 16384, D)
    nc = _get_nc(1, 72)
    cur = x
    for l in range(2):
        in_maps = _core_inputs(cur, [layers[l]], 72)
        res = run_bass_kernel_spmd(nc, in_maps, core_ids=list(range(8)))
        out = np.zeros((2, 256, 64, D), dtype=np.float32)
        for c in range(8):
            bi, p = c // 4, c % 4
            out[bi, 64 * p:64 * p + 64] = res.results[c]["y"].reshape(64, 64, D)
        cur = out.reshape(2, 16384, D)
    return cur
```

```python
from contextlib import ExitStack
import numpy as np
import ml_dtypes
import concourse.bass as bass
import concourse.mybir as mybir
from concourse.bass_utils import run_bass_kernel_spmd

F32 = mybir.dt.float32
BF16 = mybir.dt.bfloat16
AF = mybir.ActivationFunctionType
ALU = mybir.AluOpType
AX = mybir.AxisListType

ENGS = ("pe", "act", "dve", "pool", "sp")

D = 2048
NIN = 11264
NS = 79
SLAB = 4096
QSCALE = 128 ** -0.5
EXPSILU = True
TAILQ = "pool"
NEG = -30000.0


class Tok:
    __slots__ = ("name", "w", "r")

    def __init__(self, name):
        self.name = name
        self.w = None
        self.r = []


class Op:
    __slots__ = ("eng", "fn", "idx", "deps", "signal", "dma_key", "cnt")


class Prog:
    def __init__(self, nc):
        self.nc = nc
        self.ops = {e: [] for e in ENGS}
        self.dma_counts = {}

    def op(self, eng, fn, reads=(), writes=(), dma_key=None):
        o = Op()
        o.eng = eng
        o.fn = fn
        o.idx = len(self.ops[eng])
        o.signal = False
        o.dma_key = dma_key
        o.cnt = 0
        if dma_key is not None:
            c = self.dma_counts.get(dma_key, 0) + 16
            self.dma_counts[dma_key] = c
            o.cnt = c
        deps = {}

        def add(d, kind):
            if d is None or d is o:
                return
            if d.dma_key is None and d.eng == eng:
                if eng == "pe":
                    return
                if kind != "raw":
                    return
                if o.idx - d.idx > 8:
                    return
            deps[id(d)] = d

        for t in reads:
            add(t.w, "raw")
        for t in writes:
            add(t.w, "waw")
            for r in t.r:
                add(r, "war")
        for t in reads:
            t.r.append(o)
        for t in writes:
            t.w = o
            t.r = []
        o.deps = list(deps.values())
        for d in o.deps:
            if d.dma_key is None:
                d.signal = True
        self.ops[eng].append(o)
        return o

    def emit(self, final_dma_ops=()):
        nc = self.nc
        with ExitStack() as es:
            sems = {e: es.enter_context(nc.semaphore(f"s_{e}")) for e in ENGS}
            dsems = {k: es.enter_context(nc.semaphore(f"d_{k}")) for k in self.dma_counts}
            for e in ENGS:
                c = 0
                for o in self.ops[e]:
                    if o.dma_key is None and o.signal:
                        c += 1
                        o.cnt = c
            block = es.enter_context(nc.Block())

            def run(e, h):
                seen = {}
                for o in self.ops[e]:
                    need = {}
                    for d in o.deps:
                        s = dsems[d.dma_key] if d.dma_key is not None else sems[d.eng]
                        k = id(s)
                        if need.get(k, (None, 0))[1] < d.cnt:
                            need[k] = (s, d.cnt)
                    for k, (s, v) in need.items():
                        if seen.get(k, 0) < v:
                            h.wait_ge(s, v)
                            seen[k] = v
                    ins = o.fn(h)
                    if o.dma_key is not None:
                        ins.then_inc(dsems[o.dma_key], 16)
                    elif o.signal:
                        ins.then_inc(sems[e], 1)
                if e == "sp":
                    for o in final_dma_ops:
                        h.wait_ge(dsems[o.dma_key], o.cnt)

            @block.tensor
            def _(h):
                run("pe", h)

            @block.scalar
            def _(h):
                run("act", h)

            @block.vector
            def _(h):
                run("dve", h)

            @block.gpsimd
            def _(h):
                run("pool", h)

            @block.sync
            def _(h):
                run("sp", h)


NSW = 71


def slab_used(s):
    if s < 28:
        return 4096
    if s < 60:
        return 3072
    if s < 68:
        return 4096
    if s == 68:
        return 4096
    if s == 69:
        return 1024
    if s == 70:
        return 1024
    return 3072


def slab_gain(s):
    if s < 4:
        return 0
    if s < 12:
        return 1 + (s % 2)
    if s < 28:
        return 0
    if s < 60:
        return 3
    return None


def build_nc(n_layers, NR0, debug=False):
    nc = bass.Bass("TRN2", target_bir_lowering=False)
    dbg_ops = []

    def din(name, shape, dt=F32):
        return nc.dram_tensor(name, shape, dt, kind="ExternalInput").ap()

    xin = din("xin", [NR0 * 64, D])
    wsl = din("wsl", [n_layers, NSW, 128, SLAB])
    spt = din("spt", [n_layers, 8, 128, 3072])
    gains = din("gains", [n_layers, 4, 128, SLAB])
    sgb = din("sgb", [n_layers, 128, 1024])
    lnc = din("lnc", [n_layers, 128, 16])
    postg = din("postg", [n_layers, 128, D])
    ident_d = din("ident", [128, 128], BF16)
    ones_d = din("ones", [128, 128], BF16)
    NRL = NR0 - 8 * n_layers
    y = nc.dram_tensor("y", [NRL * 64, D], F32, kind="ExternalOutput").ap()
    wq = nc.dram_tensor("wq", [n_layers, NS, 128, SLAB], BF16, kind="Internal").ap()
    x1s = None
    if n_layers == 2:
        x1s = nc.dram_tensor("x1s", [(NR0 - 8) * 64, D], F32, kind="Internal").ap()

    with ExitStack() as es:
        off = [0]

        def alloc(nbytes):
            o = off[0]
            off[0] += (nbytes + 63) // 64 * 64
            return o

        o_hT = [alloc(16384), alloc(16384)]
        o_K = alloc(24576)
        o_V = alloc(24576)
        o_slab = [alloc(8192) for _ in range(3)]
        o_xn = alloc(8192)
        o_xr = alloc(8192)
        o_xs = alloc(4096)
        o_qT = [alloc(1024), alloc(1024)]
        o_sza = [alloc(2048), alloc(2048)]
        o_gt = alloc(2048)
        o_szb = alloc(2048)
        o_vn = alloc(8192)
        o_yb = alloc(8192)
        o_pT = [alloc(1536), alloc(1536)]
        o_mg = alloc(16384)
        o_msg = [alloc(2048), alloc(2048)]
        o_rden = alloc(2048)
        o_at = alloc(2048)
        o_bg = alloc(10240)
        o_sp = [alloc(1536), alloc(1536)]
        o_wst = alloc(2048)
        o_tt = alloc(4096)
        o_lnc = alloc(64)
        o_pg = alloc(8192)
        o_id = alloc(256)
        o_on = alloc(256)
        o_st = alloc(1024)
        o_junk = alloc(1024)
        TOTAL = off[0]
        assert TOTAL <= 212000, TOTAL
        A = es.enter_context(nc.sbuf_tensor("arena", [128, TOTAL // 2], BF16))
        PS = es.enter_context(nc.psum_tensor("ps", [128, 4096], F32))

        def bf(o, n):
            return A[:, o // 2:o // 2 + n]

        def f32(o, n):
            return A[:, o // 2:o // 2 + 2 * n].bitcast(F32)

        def bank(b, n=512, c0=0):
            return PS[:, b * 512 + c0:b * 512 + c0 + n]

        def bank_bf(b, n):
            return PS[:, b * 512:b * 512 + (n + 1) // 2].bitcast(BF16)

        hT = [bf(o, 8192).rearrange("p (k t) -> p k t", k=16) for o in o_hT]
        Kr = bf(o_K, 12288).rearrange("p (h t) -> p h t", h=8)
        Vr = bf(o_V, 12288).rearrange("p (s c) -> p s c", s=12)
        slabs = [bf(o, 4096) for o in o_slab]
        xn = f32(o_xn, 2048)
        xr = f32(o_xr, 2048)
        xs = bf(o_xs, 2048)
        qT = [bf(o, 512) for o in o_qT]
        sza = [f32(o, 512) for o in o_sza]
        gt = f32(o_gt, 512)
        szb = f32(o_szb, 512)
        vn = bf(o_vn, 4096).rearrange("p (t c) -> p t c", t=4)
        yaT = bf(o_vn, 4096).rearrange("p (h t) -> p h t", h=8)
        ybT = bf(o_yb, 4096).rearrange("p (h t) -> p h t", h=8)
        pT = [bf(o, 768) for o in o_pT]
        mgT = bf(o_mg, 8192).rearrange("p (k t) -> p k t", k=16)
        njunk = bf(o_mg, 2048)
        msg = [f32(o, 512) for o in o_msg]
        rden = f32(o_rden, 512)
        atb = f32(o_at, 512)
        bgen = bf(o_bg, 5120).rearrange("p (i h q) -> p i h q", i=5, h=8)
        spb = [bf(o, 768).rearrange("p (i q) -> p i q", i=6) for o in o_sp]
        wst = bf(o_wst, 1024).rearrange("p (g s) -> p g s", g=8)
        ttab = f32(o_tt, 1024).rearrange("p (g s) -> p g s", g=8)
        lncb = f32(o_lnc, 16)
        pgb = f32(o_pg, 2048)
        ident = bf(o_id, 128)
        ones = bf(o_on, 128)
        st = f32(o_st, 256)
        ojunk = bf(o_junk, 512)

        P = Prog(nc)
        T = Tok

        def dump(name, ap, n, dt, toks):
            if not debug:
                return
            d = nc.dram_tensor("dbg_" + name, [128, n], dt, kind="ExternalOutput").ap()
            dbg_ops.append(P.op("sp", lambda h: h.dma_start(out=d, in_=ap), reads=toks, dma_key="dbg"))

        t_hT = [T("hT0"), T("hT1")]
        t_hTx = [[T("hT0a"), T("hT0b")], [T("hT1a"), T("hT1b")]]
        t_K, t_V = T("K"), T("V")
        t_slab = [T(f"slab{i}") for i in range(3)]
        t_xn, t_xr, t_xs = T("xn"), T("xr"), T("xs")
        t_qT = [T("q0"), T("q1")]
        t_sza = [T("sza0"), T("sza1")]
        t_gt, t_szb = T("gt"), T("szb")
        t_vn, t_yb = T("vn"), T("yb")
        t_pT = [T("pT0"), T("pT1")]
        t_mg = T("mg")
        t_msg = [T("msg0"), T("msg1")]
        t_rden, t_at = T("rden"), T("at")
        t_bg = T("bg")
        t_sp = [T("sp0"), T("sp1")]
        t_wst, t_tt, t_lnc, t_pg = T("wst"), T("tt"), T("lnc"), T("pg")
        t_const = T("const")
        t_nst, t_lst, t_ost = T("nst"), T("lst"), T("ost")
        t_junk = T("junk")
        t_bank = [T(f"bank{i}") for i in range(8)]
        t_x1c = [T(f"x1c{i}") for i in range((NR0 - 8) * 64 // 128)]
        t_xrh = [T("xrh0"), T("xrh1")]
        t_lsa, t_lsd = T("lsa"), T("lsd")
        main_toks = (t_hT + t_hTx[0] + t_hTx[1] + [t_K, t_V] + t_slab + [t_xn, t_xr, t_xs] + t_qT + t_sza + [t_gt, t_szb, t_vn, t_yb]
                     + t_pT + [t_mg] + t_msg + [t_rden, t_at, t_bg] + t_sp + [t_wst, t_tt, t_lnc, t_pg, t_const,
                                                                           t_nst, t_lst, t_ost, t_junk, t_lsa, t_lsd] + t_xrh)

        NST = st[:, 0:16]
        LST = st[:, 16:64]
        OST = st[:, 64:96]
        EPS_RMS = st[:, 96:97]
        EPS_LN = st[:, 97:98]

        NB = 2
        pci = [f32(16384 * i, 4096) for i in range(NB)]
        pco = [bf(49152 + 8192 * i, 4096) for i in range(NB)]
        gti = [[f32(73728 + 16384 * (l * 4 + k), 4096) for k in range(4)] for l in range(n_layers)]
        assert 73728 + 16384 * 4 * n_layers <= TOTAL
        t_pci = [T(f"pci{i}") for i in range(NB)]
        t_pco = [T(f"pco{i}") for i in range(NB)]
        t_gain = [T(f"gain{k}") for k in range(4 * n_layers)]
        t_wq = [[T(f"wq{l}_{s}") for s in range(NS)] for l in range(n_layers)]
        jobs = [(l, s) for l in range(n_layers) for s in range(NS)]
        for l in range(n_layers):
            for k in range(4):
                P.op("sp", lambda h, l=l, k=k: h.dma_start(out=gti[l][k], in_=gains[l, k]), writes=[t_gain[l * 4 + k]],
                     dma_key=f"gain{l * 4 + k}")

        def pc_load(q):
            l, s = jobs[q]
            n = slab_used(s)
            i = q % NB
            srcap = wsl[l, s, :, 0:n] if s < NSW else spt[l, s - NSW]
            P.op("sp", lambda h: h.dma_start(out=pci[i][:, 0:n], in_=srcap), writes=[t_pci[i]], dma_key=f"pci{i}")

        def pc_cast(q):
            l, s = jobs[q]
            n = slab_used(s)
            i = q % NB
            g = slab_gain(s)
            if g is None:
                P.op("act", lambda h: h.activation(out=pco[i][:, 0:n], in_=pci[i][:, 0:n], func=AF.Copy),
                     reads=[t_pci[i]], writes=[t_pco[i]])
            else:
                eng = "dve" if (q % 3) != 2 else "pool"
                P.op(eng, lambda h: h.tensor_tensor(out=pco[i][:, 0:n], in0=pci[i][:, 0:n], in1=gti[l][g][:, 0:n], op=ALU.mult),
                     reads=[t_pci[i], t_gain[l * 4 + g]], writes=[t_pco[i]])

        def pc_store(q):
            l, s = jobs[q]
            n = slab_used(s)
            i = q % NB
            P.op("sp", lambda h: h.dma_start(out=wq[l, s, :, 0:n], in_=pco[i][:, 0:n]),
                 reads=[t_pco[i]], writes=[t_wq[l][s]], dma_key=f"pco{i}")

        pc_load(0)
        for q in range(len(jobs)):
            if q + 1 < len(jobs):
                pc_load(q + 1)
            pc_cast(q)
            pc_store(q)
        P.op("pool", lambda h: h.memset(st[:, 128:129], 0.0), writes=t_pci + t_pco + t_gain + main_toks)
        P.op("pool", lambda h: h.memset(EPS_RMS, 1e-6), writes=[t_const])
        P.op("pool", lambda h: h.memset(EPS_LN, 1e-5), writes=[t_const])
        P.op("sp", lambda h: h.dma_start(out=ident, in_=ident_d), writes=[t_const], dma_key="const")
        P.op("sp", lambda h: h.dma_start(out=ones, in_=ones_d), writes=[t_const], dma_key="const")

        class Stream:
            def __init__(self):
                self.plan = []
                self.issued = 0
                self.pos = 0

            def issue(self, upto):
                while self.issued < min(upto, len(self.plan)):
                    i = self.issued
                    l, s = self.plan[i]
                    n = slab_used(s)
                    b = i % 3
                    P.op("sp", lambda h, l=l, s=s, n=n, b=b: h.dma_start(out=slabs[b][:, 0:n], in_=wq[l, s, :, 0:n]),
                         reads=[t_wq[l][s]], writes=[t_slab[b]], dma_key=f"slab{b}")
                    self.issued += 1

            def get(self, l, s):
                i = self.pos
                assert self.plan[i] == (l, s), (i, self.plan[i], l, s)
                self.issue(i + 3)
                self.pos += 1
                return slabs[i % 3], t_slab[i % 3]

        stream = Stream()

        def layer_plan(l, nfull):
            pl = []
            kv = [(l, s) for s in range(0, 8)]
            pl += kv + kv
            for s_ in range(nfull):
                pl += [(l, s) for s in range(8, 12)]
                pl += [(l, s) for s in range(12, 20)]
                pl += kv
                pl += [(l, s) for s in range(20, 28)]
                pl += [(l, s) for s in range(28, 60)]
                pl += [(l, s) for s in range(60, 68)]
            return pl

        for l in range(n_layers):
            NR = NR0 - 8 * l
            stream.plan += layer_plan(l, (NR - 8) // 8)

        out_stores = []

        def do_layer(l):
            NR = NR0 - 8 * l
            nfull = (NR - 8) // 8
            src = xin if l == 0 else x1s
            dst = y if l == n_layers - 1 else x1s
            def src_tok(r0):
                return [t_x1c[r0 // 128]] if l > 0 else []

            def dst_tok(q0):
                return [t_x1c[q0 // 128]] if l < n_layers - 1 else []

            xrh = [f32(o_xr, 1024), f32(o_xr + 4096, 1024)]
            e0 = 8 - 4 * l if NR0 == 80 else 4
            if NR0 != 80:
                assert n_layers == 1

            def blk(j):
                if j == 0:
                    return 0, 256, 0
                if j == nfull + 1:
                    return NR - 4, 256, (NR - 4) // 2
                return 4 + 8 * (j - 1), 512, 2 + 4 * (j - 1)

            def slot(pi):
                return (pi + 2) % 12

            P.op("sp", lambda h, l=l: h.dma_start(out=bf(o_bg, 4096), in_=wq[l, 68, :, 0:4096]),
                 reads=[t_wq[l][68]], writes=[t_bg], dma_key="lc_bg")
            P.op("sp", lambda h, l=l: h.dma_start(out=bf(o_bg + 8192, 1024), in_=wq[l, 69, :, 0:1024]),
                 reads=[t_wq[l][69]], writes=[t_bg], dma_key="lc_bg")
            P.op("sp", lambda h, l=l: h.dma_start(out=bf(o_wst, 1024), in_=wq[l, 70, :, 0:1024]),
                 reads=[t_wq[l][70]], writes=[t_wst], dma_key="lc_wst")
            P.op("sp", lambda h, l=l: h.dma_start(out=f32(o_tt, 1024), in_=sgb[l]), writes=[t_tt], dma_key="lc_tt")
            P.op("sp", lambda h, l=l: h.dma_start(out=lncb, in_=lnc[l]), writes=[t_lnc], dma_key="lc_lnc")
            P.op("sp", lambda h, l=l: h.dma_start(out=pgb, in_=postg[l]), writes=[t_pg], dma_key="lc_pg")
            for c in range(2):
                P.op("pe", lambda h, c=c: h.matmul(bank(c), lhsT=ones, rhs=bf(o_wst + c * 1024, 512), start=True, stop=True),
                     reads=[t_const, t_wst], writes=[t_bank[c]])
            for g in range(8):
                P.op("dve", lambda h, g=g: h.scalar_tensor_tensor(
                    out=ttab[:, g, :], in0=bank(g // 4, 128, (g % 4) * 128), scalar=lncb[:, 8 + g:9 + g],
                    in1=ttab[:, g, :], op0=ALU.mult, op1=ALU.add),
                    reads=[t_bank[g // 4], t_lnc, t_tt], writes=[t_tt])

            def norm_elem_tile(j, t):
                row0, ntok, _ = blk(j)
                r0 = row0 * 64 + t * 128
                P.op("sp", lambda h: h.dma_start(out=xn, in_=src[r0:r0 + 128, :]),
                     reads=src_tok(r0), writes=[t_xn], dma_key="xn")
                P.op("act", lambda h: h.activation(out=xs, in_=xn, func=AF.Square, accum_out=NST[:, t:t + 1]),
                     reads=[t_xn], writes=[t_xs, t_nst])
                P.op("act", lambda h: h.activation(out=NST[:, 4 + t:5 + t], in_=NST[:, t:t + 1], func=AF.Sqrt,
                                                   scale=1.0 / D, bias=EPS_RMS),
                     reads=[t_nst, t_const], writes=[t_nst])
                P.op("dve", lambda h: h.reciprocal(out=NST[:, 8 + t:9 + t], in_=NST[:, 4 + t:5 + t]),
                     reads=[t_nst], writes=[t_nst])
                P.op("act", lambda h: h.activation(out=xs, in_=xn, func=AF.Copy, scale=NST[:, 8 + t:9 + t]),
                     reads=[t_xn, t_nst], writes=[t_xs])

            def norm_trans_tile(j, t):
                par = j % 2
                for kc in range(16):
                    P.op("pe", lambda h, kc=kc: h.transpose(out=bank_bf(6, 2048)[:, kc * 128:(kc + 1) * 128],
                                                            in_=xs[:, kc * 128:(kc + 1) * 128], identity=ident),
                         reads=[t_xs, t_const], writes=[t_bank[6], t_bank[7]])
                P.op("dve", lambda h: h.tensor_copy(
                    out=hT[par][:, :, t * 128:(t + 1) * 128],
                    in_=bank_bf(6, 2048).rearrange("p (k c) -> p k c", c=128)),
                    reads=[t_bank[6], t_bank[7]], writes=[t_hT[par]] + t_hTx[par])

            def norm_elem(j):
                for t in range(blk(j)[1] // 128):
                    norm_elem_tile(j, t)
                    norm_trans_tile(j, t)

            def kv_phase(j):
                row0, ntok, p0 = blk(j)
                nt = ntok // 128
                par = j % 2
                s0 = slot(p0)
                for i in range(4):
                    sl, tsl = stream.get(l, i)
                    for ch in range(2):
                        hd = 2 * i + ch
                        b = (2 * i + ch) % 8
                        for kc in range(16):
                            P.op("pe", lambda h, sl=sl, ch=ch, kc=kc, b=b: h.matmul(
                                bank(b, ntok), lhsT=sl[:, ch * 2048 + kc * 128:ch * 2048 + (kc + 1) * 128],
                                rhs=hT[par][:, kc, 0:ntok], start=(kc == 0), stop=(kc == 15)),
                                reads=[tsl, t_hT[par]], writes=[t_bank[b]])
                        P.op("dve", lambda h, hd=hd, b=b: h.tensor_copy(out=Kr[:, hd, s0 * 128:s0 * 128 + ntok], in_=bank(b, ntok)),
                             reads=[t_bank[b]], writes=[t_K])
                for cg in range(2):
                    for kh in range(2):
                        sl, tsl = stream.get(l, 4 + cg * 2 + kh)
                        for t in range(nt):
                            b = cg * 4 + t
                            for kcl in range(8):
                                kc = kh * 8 + kcl
                                P.op("pe", lambda h, sl=sl, t=t, kc=kc, kcl=kcl, b=b: h.matmul(
                                    bank(b), lhsT=hT[par][:, kc, t * 128:(t + 1) * 128],
                                    rhs=sl[:, kcl * 512:(kcl + 1) * 512], start=(kc == 0), stop=(kc == 15)),
                                    reads=[tsl, t_hT[par]], writes=[t_bank[b]])
                    for t in range(nt):
                        b = cg * 4 + t
                        P.op("act", lambda h, t=t, b=b, cg=cg: h.activation(
                            out=Vr[:, s0 + t, cg * 512:(cg + 1) * 512], in_=bank(b), func=AF.Copy),
                            reads=[t_bank[b]], writes=[t_V])

            def vs_slab(j, cg, kh):
                par = j % 2
                sl, tsl = stream.get(l, 8 + cg * 2 + kh)
                for t in range(4):
                    b = cg * 4 + t
                    for kcl in range(8):
                        kc = kh * 8 + kcl
                        P.op("pe", lambda h, sl=sl, t=t, kc=kc, kcl=kcl, b=b: h.matmul(
                            bank(b), lhsT=hT[par][:, kc, t * 128:(t + 1) * 128],
                            rhs=sl[:, kcl * 512:(kcl + 1) * 512], start=(kc == 0), stop=(kc == 15)),
                            reads=[tsl, t_hT[par]], writes=[t_bank[b]])

            def g_phase_ln(j):
                for cg in range(2):
                    for t in range(4):
                        b = cg * 4 + t
                        c = cg * 4 + t
                        P.op("act", lambda h, b=b, c=c: h.activation(out=ojunk, in_=bank(b), func=AF.Square,
                                                                     accum_out=LST[:, 8 + c:9 + c]),
                             reads=[t_bank[b]], writes=[t_junk, t_lst])
                        P.op("dve", lambda h, b=b, c=c: h.tensor_reduce(out=LST[:, c:c + 1], in_=bank(b), axis=AX.X, op=ALU.add),
                             reads=[t_bank[b]], writes=[t_lst])
                rl = [t_lst]
                P.op("dve", lambda h: h.tensor_tensor(out=LST[:, 16:20], in0=LST[:, 0:4], in1=LST[:, 4:8], op=ALU.add), reads=rl, writes=rl)
                P.op("dve", lambda h: h.tensor_tensor(out=LST[:, 20:24], in0=LST[:, 8:12], in1=LST[:, 12:16], op=ALU.add), reads=rl, writes=rl)
                P.op("dve", lambda h: h.tensor_scalar(out=LST[:, 16:20], in0=LST[:, 16:20], scalar1=1.0 / 1024, scalar2=0.0,
                                                      op0=ALU.mult, op1=ALU.add), reads=rl, writes=rl)
                P.op("dve", lambda h: h.tensor_tensor(out=LST[:, 24:28], in0=LST[:, 16:20], in1=LST[:, 16:20], op=ALU.mult), reads=rl, writes=rl)
                P.op("dve", lambda h: h.scalar_tensor_tensor(out=LST[:, 24:28], in0=LST[:, 20:24], scalar=1.0 / 1024,
                                                             in1=LST[:, 24:28], op0=ALU.mult, op1=ALU.subtract), reads=rl, writes=rl)
                P.op("act", lambda h: h.activation(out=LST[:, 24:28], in_=LST[:, 24:28], func=AF.Sqrt, scale=1.0, bias=EPS_LN),
                     reads=[t_lst, t_const], writes=rl)
                P.op("dve", lambda h: h.reciprocal(out=LST[:, 28:32], in_=LST[:, 24:28]), reads=rl, writes=rl)
                P.op("dve", lambda h: h.scalar_tensor_tensor(out=LST[:, 32:36], in0=LST[:, 16:20], scalar=-1.0,
                                                             in1=LST[:, 28:32], op0=ALU.mult, op1=ALU.mult), reads=rl, writes=rl)
                for cg in range(2):
                    for t in range(4):
                        b = cg * 4 + t
                        P.op("act", lambda h, t=t, b=b, cg=cg: h.activation(
                            out=vn[:, t, cg * 512:(cg + 1) * 512], in_=bank(b), func=AF.Identity,
                            scale=LST[:, 28 + t:29 + t], bias=LST[:, 32 + t:33 + t]),
                            reads=[t_bank[b], t_lst], writes=[t_vn])

            def g_phase_uz(j, g):
                par = j % 2
                sl, tsl = stream.get(l, 12 + g)
                bs = (g % 2) * 3
                bu, bz, bsx = bs, bs + 1, bs + 2
                for which, b in ((0, bu), (1, bz)):
                    for kc in range(16):
                        P.op("pe", lambda h, sl=sl, which=which, kc=kc, b=b: h.matmul(
                            bank(b), lhsT=sl[:, which * 2048 + kc * 128:which * 2048 + (kc + 1) * 128],
                            rhs=hT[par][:, kc, :], start=(kc == 0), stop=(kc == 15)),
                            reads=[tsl, t_hT[par]], writes=[t_bank[b]])
                for t in range(4):
                    P.op("pe", lambda h, t=t, g=g, b=bsx: h.matmul(
                        bank(b, 128, t * 128), lhsT=vn[:, t, g * 128:(g + 1) * 128], rhs=wst[:, g, :], start=True, stop=True),
                        reads=[t_vn, t_wst], writes=[t_bank[bsx]])
                P.op("act", lambda h, b=bz: h.activation(out=szb, in_=bank(b), func=AF.Silu), reads=[t_bank[bz]], writes=[t_szb])
                for t in range(4):
                    P.op("dve", lambda h, t=t, g=g, b=bsx: h.scalar_tensor_tensor(
                        out=gt[:, t * 128:(t + 1) * 128], in0=bank(b, 128, t * 128), scalar=lncb[:, g:g + 1],
                        in1=ttab[:, g, :], op0=ALU.mult, op1=ALU.add),
                        reads=[t_bank[bsx], t_lnc, t_tt], writes=[t_gt])
                P.op("dve", lambda h, b=bu: h.tensor_tensor(out=gt, in0=gt, in1=bank(b), op=ALU.mult),
                     reads=[t_gt, t_bank[bu]], writes=[t_gt])
                P.op("dve", lambda h, g=g: h.tensor_tensor(out=ybT[:, g, :], in0=gt, in1=szb, op=ALU.mult),
                     reads=[t_gt, t_szb], writes=[t_yb])

            def q_proj(j, hd):
                par = j % 2
                sl, tsl = stream.get(l, 20 + hd)
                for which, b in ((0, 0), (1, 1)):
                    for kc in range(16):
                        P.op("pe", lambda h, sl=sl, which=which, kc=kc, b=b: h.matmul(
                            bank(b), lhsT=sl[:, which * 2048 + kc * 128:which * 2048 + (kc + 1) * 128],
                            rhs=hT[par][:, kc, :], start=(kc == 0), stop=(kc == 15)),
                            reads=[tsl, t_hT[par]], writes=[t_bank[b]])
                P.op("dve", lambda h, hd=hd: h.tensor_scalar(out=qT[hd % 2], in0=bank(0), scalar1=QSCALE, scalar2=0.0,
                                                             op0=ALU.mult, op1=ALU.add),
                     reads=[t_bank[0]], writes=[t_qT[hd % 2]])
                if EXPSILU:
                    P.op("act", lambda h, hd=hd: h.activation(out=sza[hd % 2], in_=bank(1), func=AF.Exp, scale=-1.0),
                         reads=[t_bank[1]], writes=[t_sza[hd % 2]])
                    P.op("dve", lambda h, hd=hd: h.tensor_scalar(out=sza[hd % 2], in0=sza[hd % 2], scalar1=1.0, scalar2=0.0,
                                                                 op0=ALU.add, op1=ALU.add),
                         reads=[t_sza[hd % 2]], writes=[t_sza[hd % 2]])
                    P.op("dve", lambda h, hd=hd: h.reciprocal(out=sza[hd % 2], in_=sza[hd % 2]),
                         reads=[t_sza[hd % 2]], writes=[t_sza[hd % 2]])
                    P.op("dve", lambda h, hd=hd: h.tensor_tensor(out=sza[hd % 2], in0=sza[hd % 2], in1=bank(1), op=ALU.mult),
                         reads=[t_bank[1], t_sza[hd % 2]], writes=[t_sza[hd % 2]])
                else:
                    P.op("act", lambda h, hd=hd: h.activation(out=sza[hd % 2], in_=bank(1), func=AF.Silu),
                         reads=[t_bank[1]], writes=[t_sza[hd % 2]])

            sp_ctr = [0]

            def attn(j, hd):
                _, _, p0 = blk(j)
                tiles = []
                for t in range(4):
                    tp = p0 + t
                    row = 2 * tp
                    spi = None
                    if row == e0:
                        spi, offs = 0, list(range(-2, 4))
                    elif row == e0 + 2:
                        spi, offs = 1, list(range(-2, 3))
                    elif row == e0 + 60:
                        spi, offs = 2, list(range(-2, 3))
                    elif row == e0 + 62:
                        spi, offs = 3, list(range(-3, 3))
                    else:
                        offs = list(range(-2, 3))
                    tiles.append((tp, spi, offs))

                def S(t):
                    tp, spi, offs = tiles[t]
                    bx, by = (4, 5) if t % 2 == 0 else (6, 7)
                    tb = None
                    ttb = t_bg
                    if spi is not None:
                        k = sp_ctr[0] % 2
                        sp_ctr[0] += 1
                        sidx = 71 + spi * 2 + hd // 4
                        P.op("sp", lambda h, k=k, sidx=sidx: h.dma_start(
                            out=bf(o_sp[k], 768), in_=wq[l, sidx, :, (hd % 4) * 768:(hd % 4) * 768 + 768]),
                            reads=[t_wq[l][sidx]], writes=[t_sp[k]], dma_key=f"spt{k}")
                        tb = spb[k]
                        ttb = t_sp[k]
                    for i, o in enumerate(offs):
                        pi = tp + o
                        sk = slot(pi)
                        b = bx if i < 4 else by
                        c0 = (i % 4) * 128
                        P.op("pe", lambda h, sk=sk, b=b, c0=c0, t=t: h.matmul(
                            bank(b, 128, c0), lhsT=Kr[:, hd, sk * 128:(sk + 1) * 128], rhs=qT[hd % 2][:, t * 128:(t + 1) * 128],
                            start=True, stop=False),
                            reads=[t_K, t_qT[hd % 2]], writes=[t_bank[b]])
                        if tb is None:
                            rhs = bgen[:, i, hd, :]
                        else:
                            rhs = tb[:, i, :]
                        P.op("pe", lambda h, b=b, c0=c0, rhs=rhs: h.matmul(bank(b, 128, c0), lhsT=ident, rhs=rhs, start=False, stop=True),
                             reads=[t_const, ttb], writes=[t_bank[b]])

                def E(t):
                    tp, spi, offs = tiles[t]
                    bx, by = (4, 5) if t % 2 == 0 else (6, 7)
                    n2 = (len(offs) - 4) * 128
                    P.op("act", lambda h, t=t, bx=bx: h.activation(out=pT[t % 2][:, 0:512], in_=bank(bx), func=AF.Exp),
                         reads=[t_bank[bx]], writes=[t_pT[t % 2]])
                    P.op("act", lambda h, t=t, by=by, n2=n2: h.activation(out=pT[t % 2][:, 512:512 + n2], in_=bank(by, n2), func=AF.Exp),
                         reads=[t_bank[by]], writes=[t_pT[t % 2]])

                def PV(t):
                    tp, spi, offs = tiles[t]
                    n = len(offs)
                    for i, o in enumerate(offs):
                        sk = slot(tp + o)
                        P.op("pe", lambda h, sk=sk, i=i, t=t, n=n: h.matmul(
                            bank(2, 128, t * 128), lhsT=Vr[:, sk, hd * 128:(hd + 1) * 128], rhs=pT[t % 2][:, i * 128:(i + 1) * 128],
                            start=(i == 0), stop=(i == n - 1)),
                            reads=[t_V, t_pT[t % 2]], writes=[t_bank[2]])
                        P.op("pe", lambda h, i=i, t=t, n=n: h.matmul(
                            bank(3, 128, t * 128), lhsT=ones, rhs=pT[t % 2][:, i * 128:(i + 1) * 128],
                            start=(i == 0), stop=(i == n - 1)),
                            reads=[t_const, t_pT[t % 2]], writes=[t_bank[3]])

                S(0)
                for t in range(4):
                    if t + 1 < 4:
                        S(t + 1)
                    E(t)
                    PV(t)
                P.op("dve", lambda h: h.reciprocal(out=rden, in_=bank(3)), reads=[t_bank[3]], writes=[t_rden])
                P.op("dve", lambda h: h.tensor_tensor(out=atb, in0=bank(2), in1=rden, op=ALU.mult),
                     reads=[t_bank[2], t_rden], writes=[t_at])
                P.op("pool", lambda h: h.tensor_tensor(out=yaT[:, hd, :], in0=atb, in1=sza[hd % 2], op=ALU.mult),
                     reads=[t_at, t_sza[hd % 2]], writes=[t_vn])

            def m_phase(j):
                par = j % 2
                for jj in range(16):
                    bs = (jj % 2) * 4
                    for half in range(2):
                        sl, tsl = stream.get(l, 28 + 2 * jj + half)
                        bg_, bp_ = bs + 2 * half, bs + 2 * half + 1
                        for kc in range(16):
                            P.op("pe", lambda h, sl=sl, kc=kc, b=bg_: h.matmul(
                                bank(b), lhsT=sl[:, kc * 128:(kc + 1) * 128], rhs=hT[par][:, kc, :],
                                start=(kc == 0), stop=(kc == 15)),
                                reads=[tsl, t_hT[par]], writes=[t_bank[bg_]])
                        yT = yaT if half == 0 else ybT
                        ty = t_vn if half == 0 else t_yb
                        for c in range(8):
                            P.op("pe", lambda h, sl=sl, c=c, b=bp_, yT=yT: h.matmul(
                                bank(b), lhsT=sl[:, 2048 + c * 128:2048 + (c + 1) * 128], rhs=yT[:, c, :],
                                start=(c == 0), stop=(c == 7)),
                                reads=[tsl, ty], writes=[t_bank[bp_]])
                        P.op("act", lambda h, half=half, b=bg_: h.activation(out=msg[half], in_=bank(b), func=AF.Sigmoid),
                             reads=[t_bank[bg_]], writes=[t_msg[half]])
                        P.op("dve", lambda h, half=half, b=bp_: h.tensor_tensor(out=msg[half], in0=msg[half], in1=bank(b), op=ALU.mult),
                             reads=[t_msg[half], t_bank[bp_]], writes=[t_msg[half]])
                    P.op("pool", lambda h, jj=jj: h.tensor_tensor(out=mgT[:, jj, :], in0=msg[0], in1=msg[1], op=ALU.add),
                         reads=t_msg, writes=[t_mg])

            def o_phase(j, early=None):
                row0, _, _ = blk(j)
                par = j % 2
                ob = [f32(o_msg[0], 2048), f32(o_yb, 2048), f32(o_hT[par], 2048), f32(o_hT[par] + 8192, 2048)]
                t_ob = [[t_msg[0], t_msg[1], t_rden, t_at], [t_yb], [t_hTx[par][0]], [t_hTx[par][1]]]
                for cg in range(4):
                    bs = (cg % 2) * 4
                    for kh in range(2):
                        sl, tsl = stream.get(l, 60 + cg * 2 + kh)
                        for t in range(4):
                            b = bs + t
                            for kcl in range(8):
                                kc = kh * 8 + kcl
                                P.op("pe", lambda h, sl=sl, t=t, kc=kc, kcl=kcl, b=b: h.matmul(
                                    bank(b), lhsT=mgT[:, kc, t * 128:(t + 1) * 128], rhs=sl[:, kcl * 512:(kcl + 1) * 512],
                                    start=(kc == 0), stop=(kc == 15)),
                                    reads=[tsl, t_mg], writes=[t_bank[b]])
                    for t in range(4):
                        b = bs + t
                        wt = t_ob[t] + ([t_hT[par]] if (cg == 0 and t >= 2) else [])
                        P.op("act", lambda h, t=t, b=b, cg=cg: h.activation(out=ob[t][:, cg * 512:(cg + 1) * 512], in_=bank(b), func=AF.Copy),
                             reads=[t_bank[b]], writes=wt)
                        P.op("act", lambda h, t=t, cg=cg: h.activation(out=ojunk, in_=ob[t][:, cg * 512:(cg + 1) * 512], func=AF.Square,
                                                                       accum_out=OST[:, t * 4 + cg:t * 4 + cg + 1]),
                             reads=t_ob[t], writes=[t_junk, t_ost])
                if early is not None:
                    early()
                ro = [t_ost]
                P.op("dve", lambda h: h.tensor_reduce(out=OST[:, 16:20], in_=OST[:, 0:16].rearrange("p (t c) -> p t c", c=4),
                                                      axis=AX.X, op=ALU.add), reads=ro, writes=ro)
                P.op("act", lambda h: h.activation(out=OST[:, 20:24], in_=OST[:, 16:20], func=AF.Sqrt, scale=1.0 / D, bias=EPS_RMS),
                     reads=[t_ost, t_const], writes=ro)
                P.op("dve", lambda h: h.reciprocal(out=OST[:, 24:28], in_=OST[:, 20:24]), reads=ro, writes=ro)
                for t in (1, 2, 3, 0):
                    r0 = row0 * 64 + t * 128
                    q0 = (row0 - 4) * 64 + t * 128
                    P.op(TAILQ, lambda h, r0=r0: h.dma_start(out=xr, in_=src[r0:r0 + 128, :]),
                         reads=src_tok(r0), writes=t_xrh, dma_key="xr")
                    P.op("dve", lambda h, t=t: h.scalar_tensor_tensor(out=ob[t], in0=ob[t], scalar=OST[:, 24 + t:25 + t], in1=pgb,
                                                                      op0=ALU.mult, op1=ALU.mult),
                         reads=t_ob[t] + [t_ost, t_pg], writes=t_ob[t])
                    for hf in range(2):
                        P.op("pool", lambda h, t=t, hf=hf: h.tensor_tensor(out=ob[t][:, hf * 1024:(hf + 1) * 1024],
                                                                          in0=ob[t][:, hf * 1024:(hf + 1) * 1024], in1=xrh[hf], op=ALU.add),
                             reads=t_ob[t] + [t_xrh[hf]], writes=t_ob[t])
                    o = P.op(TAILQ, lambda h, t=t, q0=q0: h.dma_start(out=dst[q0:q0 + 128, :], in_=ob[t]),
                             reads=t_ob[t], writes=dst_tok(q0), dma_key=f"out{t}")
                    if l == n_layers - 1:
                        out_stores.append(o)

            norm_elem(0)
            kv_phase(0)
            norm_elem(1)
            kv_phase(1)
            pre = False
            for s_ in range(nfull):
                j = s_ + 1
                ntn = blk(j + 1)[1] // 128
                if not pre:
                    norm_elem_tile(j + 1, 0)
                vs_slab(j, 0, 0)
                vs_slab(j, 0, 1)
                vs_slab(j, 1, 0)
                vs_slab(j, 1, 1)
                if j == 1 and l == 0:
                    dump("hT", bf(o_hT[1], 8192), 8192, BF16, [t_hT[1]])
                g_phase_ln(j)
                if j == 1 and l == 0:
                    dump("vn", bf(o_vn, 4096), 4096, BF16, [t_vn])
                tsched = {2: 0, 4: 1, 6: 2, 7: 3}
                for g in range(8):
                    g_phase_uz(j, g)
                    k = tsched.get(g)
                    if k is not None and k < ntn:
                        norm_trans_tile(j + 1, k)
                        if k + 1 < ntn:
                            norm_elem_tile(j + 1, k + 1)
                if j == 1 and l == 0:
                    dump("yb", bf(o_yb, 4096), 4096, BF16, [t_yb])
                kv_phase(j + 1)
                if j == 1 and l == 0:
                    dump("K", bf(o_K, 12288), 12288, BF16, [t_K])
                    dump("V", bf(o_V, 12288), 12288, BF16, [t_V])
                q_proj(j, 0)
                for hd in range(8):
                    if hd + 1 < 8:
                        q_proj(j, hd + 1)
                    attn(j, hd)
                if j == 1 and l == 0:
                    dump("ya", bf(o_vn, 4096), 4096, BF16, [t_vn])
                m_phase(j)
                if j == 1 and l == 0:
                    dump("mg", bf(o_mg, 8192), 8192, BF16, [t_mg])
                pre = (j + 1 <= nfull)
                o_phase(j, early=(lambda j=j: norm_elem_tile(j + 2, 0)) if pre else None)

        for l in range(n_layers):
            do_layer(l)

        assert stream.pos == len(stream.plan), (stream.pos, len(stream.plan))
        last = {}
        for o in out_stores:
            last[o.dma_key] = o
        P.emit(final_dma_ops=list(last.values()) + dbg_ops[-1:])
    return nc


def _a_chunk(W3, col0):
    return W3[:, :, col0:col0 + 128].transpose(1, 0, 2).reshape(128, -1)


def _b_slab(W, col0, kh):
    Wk = W.reshape(-1, 128, W.shape[1])
    return Wk[kh * 8:(kh + 1) * 8, :, col0:col0 + 512].transpose(1, 0, 2).reshape(128, 4096)


def _pair_table(rpb, qrow_g, offs, rows_total, generic):
    kc = np.arange(64)
    qc = np.arange(64)
    cs = np.clip(qc - 8, 0, 48)
    colvalid = (kc[:, None] >= cs[None, :]) & (kc[:, None] < cs[None, :] + 16)
    dc = kc[:, None] - qc[None, :] + 15
    dcc = np.clip(dc, 0, 30)
    out = np.full((2, 64, len(offs), 8, 2, 64), NEG, dtype=np.float32)
    for i, o in enumerate(offs):
        for kr in range(2):
            for qr in range(2):
                krow = qrow_g + 2 * o + kr
                qrow = qrow_g + qr
                if generic:
                    rs = qrow - 4
                else:
                    rs = min(max(qrow - 4, 0), rows_total - 8)
                    if qrow < 0 or qrow >= rows_total:
                        rs = qrow - 4
                dr = krow - qrow + 7
                if krow < rs or krow >= rs + 8 or dr < 0 or dr > 14:
                    continue
                vals = rpb[:, dr, :][:, dcc]
                sel = np.where(colvalid[None], vals, np.float32(NEG))
                out[kr, :, i, :, qr, :] = sel.transpose(1, 0, 2)
    return out.reshape(128, len(offs), 8, 128)


def _layer_arrays(lw, wsl):
    (pre_g, post_g, w_in, rpb, ln_g, ln_b, sg_w, sg_b, w_pa, w_pb, w_out) = lw
    W3 = w_in.reshape(16, 128, NIN)
    for i in range(4):
        wsl[i, :, 0:2048] = _a_chunk(W3, 1024 + (2 * i) * 128)
        wsl[i, :, 2048:4096] = _a_chunk(W3, 1024 + (2 * i + 1) * 128)
    for cg in range(2):
        for kh in range(2):
            wsl[4 + cg * 2 + kh] = _b_slab(w_in, 2048 + cg * 512, kh)
            wsl[8 + cg * 2 + kh] = _b_slab(w_in, 5120 + cg * 512, kh)
    for g in range(8):
        wsl[12 + g, :, 0:2048] = _a_chunk(W3, 4096 + g * 128)
        wsl[12 + g, :, 2048:4096] = _a_chunk(W3, 6144 + g * 128)
        wsl[20 + g, :, 0:2048] = _a_chunk(W3, g * 128)
        wsl[20 + g, :, 2048:4096] = _a_chunk(W3, 3072 + g * 128)
    PA3 = w_pa.reshape(8, 128, D)
    PB3 = w_pb.reshape(8, 128, D)
    for j in range(16):
        wsl[28 + 2 * j, :, 0:2048] = _a_chunk(W3, 7168 + j * 128)
        wsl[28 + 2 * j, :, 2048:3072] = _a_chunk(PA3, j * 128)
        wsl[29 + 2 * j, :, 0:2048] = _a_chunk(W3, 9216 + j * 128)
        wsl[29 + 2 * j, :, 2048:3072] = _a_chunk(PB3, j * 128)
    for cg in range(4):
        for kh in range(2):
            wsl[60 + cg * 2 + kh] = _b_slab(w_out, cg * 512, kh)
    return wsl


def _gains(pre_g):
    gc = pre_g.reshape(16, 128).T
    e = np.arange(SLAB)
    GA = gc[:, (e // 128) % 16]
    GB0 = gc[:, e // 512]
    GB1 = gc[:, 8 + e // 512]
    GM = np.where(e[None, :] < 2048, GA, np.float32(1.0))
    return np.stack([GA, GB0, GB1, GM]).astype(np.float32)


_NC_CACHE = {}


def _get_nc(n_layers, NR0):
    key = (n_layers, NR0)
    if key not in _NC_CACHE:
        _NC_CACHE[key] = build_nc(n_layers, NR0)
    return _NC_CACHE[key]


def _core_inputs(x, layers, NR0):
    nl = len(layers)
    ident = np.eye(128, dtype=np.float32).astype(ml_dtypes.bfloat16)
    ones = np.ones((128, 128), dtype=np.float32).astype(ml_dtypes.bfloat16)
    halo = (NR0 - 64) // 2
    xg = x.reshape(2, 256, 64, D)
    wsl = np.zeros((nl, NSW, 128, SLAB), dtype=np.float32)
    gains = np.zeros((nl, 4, 128, SLAB), dtype=np.float32)
    sgb = np.zeros((nl, 128, 1024), dtype=np.float32)
    lnc = np.zeros((nl, 128, 16), dtype=np.float32)
    postg = np.zeros((nl, 128, D), dtype=np.float32)
    for li, lw in enumerate(layers):
        (pre_g, post_g, w_in, rpb, ln_g, ln_b, sg_w, sg_b, w_pa, w_pb, w_out) = lw
        _layer_arrays(lw, wsl[li])
        gains[li] = _gains(pre_g)
        sgb[li] = np.broadcast_to(sg_b.reshape(1, 1024), (128, 1024))
        lnc[li, :, 0:8] = ln_g.reshape(8, 128).T
        lnc[li, :, 8:16] = ln_b.reshape(8, 128).T
        postg[li] = np.broadcast_to(post_g.reshape(1, D), (128, D))
        wsl[li, 70, :, 0:1024] = sg_w.transpose(2, 0, 1).reshape(128, 1024)
        g5 = _pair_table(rpb, 0, list(range(-2, 3)), 256, True).reshape(128, 5120)
        wsl[li, 68] = g5[:, 0:4096]
        wsl[li, 69, :, 0:1024] = g5[:, 4096:5120]
    in_maps = []
    for c in range(8):
        bi, p = c // 4, c % 4
        R0 = 64 * p
        xl = np.zeros((NR0, 64, D), dtype=np.float32)
        lo, hi = R0 - halo, R0 + 64 + halo
        slo, shi = max(lo, 0), min(hi, 256)
        xl[slo - lo:shi - lo] = xg[bi, slo:shi]
        spt = np.zeros((nl, 8, 128, 3072), dtype=np.float32)
        for li, lw in enumerate(layers):
            rpb = lw[3]
            specs = [(R0, list(range(-2, 4))), (R0 + 2, list(range(-2, 3))),
                     (R0 + 60, list(range(-2, 3))), (R0 + 62, list(range(-3, 3)))]
            for spi, (qrow, offs) in enumerate(specs):
                tb = _pair_table(rpb, qrow, offs, 256, False)
                full = np.full((128, 6, 8, 128), NEG, dtype=np.float32)
                full[:, 0:len(offs)] = tb
                hm = full.transpose(0, 2, 1, 3)
                spt[li, spi * 2 + 0] = hm[:, 0:4].reshape(128, 3072)
                spt[li, spi * 2 + 1] = hm[:, 4:8].reshape(128, 3072)
        in_maps.append({"xin": xl.reshape(NR0 * 64, D), "wsl": wsl, "spt": spt, "gains": gains, "sgb": sgb,
                        "lnc": lnc, "postg": postg, "ident": ident, "ones": ones})
    return in_maps


FUSED = True


def kernel(x, pre_norm_g, post_norm_g, w_in, na_rpb, sg_ln_g, sg_ln_b, sg_w, sg_b, w_proj_a, w_proj_b, w_out):
    f = lambda a: np.ascontiguousarray(np.asarray(a, dtype=np.float32))
    x = f(x)
    layers = []
    for l in range(2):
        layers.append(tuple(f(a[l]) for a in (pre_norm_g, post_norm_g, w_in, na_rpb, sg_ln_g, sg_ln_b, sg_w, sg_b,
                                               w_proj_a, w_proj_b, w_out)))
    if FUSED:
        nc = _get_nc(2, 80)
        in_maps = _core_inputs(x, layers, 80)
        res = run_bass_kernel_spmd(nc, in_maps, core_ids=list(range(8)))
        out = np.zeros((2, 256, 64, D), dtype=np.float32)
        for c in range(8):
            bi, p = c // 4, c % 4
            out[bi, 64 * p:64 * p + 64] = res.results[c]["y"].reshape(64, 64, D)
        return out.reshape(2, 16384, D)
    nc = _get_nc(1, 72)
    cur = x
    for l in range(2):
        in_maps = _core_inputs(cur, [layers[l]], 72)
        res = run_bass_kernel_spmd(nc, in_maps, core_ids=list(range(8)))
        out = np.zeros((2, 256, 64, D), dtype=np.float32)
        for c in range(8):
            bi, p = c // 4, c % 4
            out[bi, 64 * p:64 * p + 64] = res.results[c]["y"].reshape(64, 64, D)
        cur = out.reshape(2, 16384, D)
    return cur
```

```python
from contextlib import ExitStack
import numpy as np
import ml_dtypes
import concourse.bass as bass
import concourse.mybir as mybir
from concourse.bass_utils import run_bass_kernel_spmd

F32 = mybir.dt.float32
BF16 = mybir.dt.bfloat16
AF = mybir.ActivationFunctionType
ALU = mybir.AluOpType
AX = mybir.AxisListType

ENGS = ("pe", "act", "dve", "pool", "sp")

D = 2048
NIN = 11264
NS = 79
SLAB = 4096
QSCALE = 128 ** -0.5
EXPSILU = True
TAILQ = "pool"
NEG = -30000.0


class Tok:
    __slots__ = ("name", "w", "r")

    def __init__(self, name):
        self.name = name
        self.w = None
        self.r = []


class Op:
    __slots__ = ("eng", "fn", "idx", "deps", "signal", "dma_key", "cnt")


class Prog:
    def __init__(self, nc):
        self.nc = nc
        self.ops = {e: [] for e in ENGS}
        self.dma_counts = {}

    def op(self, eng, fn, reads=(), writes=(), dma_key=None):
        o = Op()
        o.eng = eng
        o.fn = fn
        o.idx = len(self.ops[eng])
        o.signal = False
        o.dma_key = dma_key
        o.cnt = 0
        if dma_key is not None:
            c = self.dma_counts.get(dma_key, 0) + 16
            self.dma_counts[dma_key] = c
            o.cnt = c
        deps = {}

        def add(d, kind):
            if d is None or d is o:
                return
            if d.dma_key is None and d.eng == eng:
                if eng == "pe":
                    return
                if kind != "raw":
                    return
                if o.idx - d.idx > 8:
                    return
            deps[id(d)] = d

        for t in reads:
            add(t.w, "raw")
        for t in writes:
            add(t.w, "waw")
            for r in t.r:
                add(r, "war")
        for t in reads:
            t.r.append(o)
        for t in writes:
            t.w = o
            t.r = []
        o.deps = list(deps.values())
        for d in o.deps:
            if d.dma_key is None:
                d.signal = True
        self.ops[eng].append(o)
        return o

    def emit(self, final_dma_ops=()):
        nc = self.nc
        with ExitStack() as es:
            sems = {e: es.enter_context(nc.semaphore(f"s_{e}")) for e in ENGS}
            dsems = {k: es.enter_context(nc.semaphore(f"d_{k}")) for k in self.dma_counts}
            for e in ENGS:
                c = 0
                for o in self.ops[e]:
                    if o.dma_key is None and o.signal:
                        c += 1
                        o.cnt = c
            block = es.enter_context(nc.Block())

            def run(e, h):
                seen = {}
                for o in self.ops[e]:
                    need = {}
                    for d in o.deps:
                        s = dsems[d.dma_key] if d.dma_key is not None else sems[d.eng]
                        k = id(s)
                        if need.get(k, (None, 0))[1] < d.cnt:
                            need[k] = (s, d.cnt)
                    for k, (s, v) in need.items():
                        if seen.get(k, 0) < v:
                            h.wait_ge(s, v)
                            seen[k] = v
                    ins = o.fn(h)
                    if o.dma_key is not None:
                        ins.then_inc(dsems[o.dma_key], 16)
                    elif o.signal:
                        ins.then_inc(sems[e], 1)
                if e == "sp":
                    for o in final_dma_ops:
                        h.wait_ge(dsems[o.dma_key], o.cnt)

            @block.tensor
            def _(h):
                run("pe", h)

            @block.scalar
            def _(h):
                run("act", h)

            @block.vector
            def _(h):
                run("dve", h)

            @block.gpsimd
            def _(h):
                run("pool", h)

            @block.sync
            def _(h):
                run("sp", h)


NSW = 71


def slab_used(s):
    if s < 28:
        return 4096
    if s < 60:
        return 3072
    if s < 68:
        return 4096
    if s == 68:
        return 4096
    if s == 69:
        return 1024
    if s == 70:
        return 1024
    return 3072


def slab_gain(s):
    if s < 4:
        return 0
    if s < 12:
        return 1 + (s % 2)
    if s < 28:
        return 0
    if s < 60:
        return 3
    return None


def build_nc(n_layers, NR0, debug=False):
    nc = bass.Bass("TRN2", target_bir_lowering=False)
    dbg_ops = []

    def din(name, shape, dt=F32):
        return nc.dram_tensor(name, shape, dt, kind="ExternalInput").ap()

    xin = din("xin", [NR0 * 64, D])
    wsl = din("wsl", [n_layers, NSW, 128, SLAB])
    spt = din("spt", [n_layers, 8, 128, 3072])
    gains = din("gains", [n_layers, 4, 128, SLAB])
    sgb = din("sgb", [n_layers, 128, 1024])
    lnc = din("lnc", [n_layers, 128, 16])
    postg = din("postg", [n_layers, 128, D])
    ident_d = din("ident", [128, 128], BF16)
    ones_d = din("ones", [128, 128], BF16)
    NRL = NR0 - 8 * n_layers
    y = nc.dram_tensor("y", [NRL * 64, D], F32, kind="ExternalOutput").ap()
    wq = nc.dram_tensor("wq", [n_layers, NS, 128, SLAB], BF16, kind="Internal").ap()
    x1s = None
    if n_layers == 2:
        x1s = nc.dram_tensor("x1s", [(NR0 - 8) * 64, D], F32, kind="Internal").ap()

    with ExitStack() as es:
        off = [0]

        def alloc(nbytes):
            o = off[0]
            off[0] += (nbytes + 63) // 64 * 64
            return o

        o_hT = [alloc(16384), alloc(16384)]
        o_K = alloc(24576)
        o_V = alloc(24576)
        o_slab = [alloc(8192) for _ in range(3)]
        o_xn = alloc(8192)
        o_xr = alloc(8192)
        o_xs = alloc(4096)
        o_qT = [alloc(1024), alloc(1024)]
        o_sza = [alloc(2048), alloc(2048)]
        o_gt = alloc(2048)
        o_szb = alloc(2048)
        o_vn = alloc(8192)
        o_yb = alloc(8192)
        o_pT = [alloc(1536), alloc(1536)]
        o_mg = alloc(16384)
        o_msg = [alloc(2048), alloc(2048)]
        o_rden = alloc(2048)
        o_at = alloc(2048)
        o_bg = alloc(10240)
        o_sp = [alloc(1536), alloc(1536)]
        o_wst = alloc(2048)
        o_tt = alloc(4096)
        o_lnc = alloc(64)
        o_pg = alloc(8192)
        o_id = alloc(256)
        o_on = alloc(256)
        o_st = alloc(1024)
        o_junk = alloc(1024)
        TOTAL = off[0]
        assert TOTAL <= 212000, TOTAL
        A = es.enter_context(nc.sbuf_tensor("arena", [128, TOTAL // 2], BF16))
        PS = es.enter_context(nc.psum_tensor("ps", [128, 4096], F32))

        def bf(o, n):
            return A[:, o // 2:o // 2 + n]

        def f32(o, n):
            return A[:, o // 2:o // 2 + 2 * n].bitcast(F32)

        def bank(b, n=512, c0=0):
            return PS[:, b * 512 + c0:b * 512 + c0 + n]

        def bank_bf(b, n):
            return PS[:, b * 512:b * 512 + (n + 1) // 2].bitcast(BF16)

        hT = [bf(o, 8192).rearrange("p (k t) -> p k t", k=16) for o in o_hT]
        Kr = bf(o_K, 12288).rearrange("p (h t) -> p h t", h=8)
        Vr = bf(o_V, 12288).rearrange("p (s c) -> p s c", s=12)
        slabs = [bf(o, 4096) for o in o_slab]
        xn = f32(o_xn, 2048)
        xr = f32(o_xr, 2048)
        xs = bf(o_xs, 2048)
        qT = [bf(o, 512) for o in o_qT]
        sza = [f32(o, 512) for o in o_sza]
        gt = f32(o_gt, 512)
        szb = f32(o_szb, 512)
        vn = bf(o_vn, 4096).rearrange("p (t c) -> p t c", t=4)
        yaT = bf(o_vn, 4096).rearrange("p (h t) -> p h t", h=8)
        ybT = bf(o_yb, 4096).rearrange("p (h t) -> p h t", h=8)
        pT = [bf(o, 768) for o in o_pT]
        mgT = bf(o_mg, 8192).rearrange("p (k t) -> p k t", k=16)
        njunk = bf(o_mg, 2048)
        msg = [f32(o, 512) for o in o_msg]
        rden = f32(o_rden, 512)
        atb = f32(o_at, 512)
        bgen = bf(o_bg, 5120).rearrange("p (i h q) -> p i h q", i=5, h=8)
        spb = [bf(o, 768).rearrange("p (i q) -> p i q", i=6) for o in o_sp]
        wst = bf(o_wst, 1024).rearrange("p (g s) -> p g s", g=8)
        ttab = f32(o_tt, 1024).rearrange("p (g s) -> p g s", g=8)
        lncb = f32(o_lnc, 16)
        pgb = f32(o_pg, 2048)
        ident = bf(o_id, 128)
        ones = bf(o_on, 128)
        st = f32(o_st, 256)
        ojunk = bf(o_junk, 512)

        P = Prog(nc)
        T = Tok

        def dump(name, ap, n, dt, toks):
            if not debug:
                return
            d = nc.dram_tensor("dbg_" + name, [128, n], dt, kind="ExternalOutput").ap()
            dbg_ops.append(P.op("sp", lambda h: h.dma_start(out=d, in_=ap), reads=toks, dma_key="dbg"))

        t_hT = [T("hT0"), T("hT1")]
        t_hTx = [[T("hT0a"), T("hT0b")], [T("hT1a"), T("hT1b")]]
        t_K, t_V = T("K"), T("V")
        t_slab = [T(f"slab{i}") for i in range(3)]
        t_xn, t_xr, t_xs = T("xn"), T("xr"), T("xs")
        t_qT = [T("q0"), T("q1")]
        t_sza = [T("sza0"), T("sza1")]
        t_gt, t_szb = T("gt"), T("szb")
        t_vn, t_yb = T("vn"), T("yb")
        t_pT = [T("pT0"), T("pT1")]
        t_mg = T("mg")
        t_msg = [T("msg0"), T("msg1")]
        t_rden, t_at = T("rden"), T("at")
        t_bg = T("bg")
        t_sp = [T("sp0"), T("sp1")]
        t_wst, t_tt, t_lnc, t_pg = T("wst"), T("tt"), T("lnc"), T("pg")
        t_const = T("const")
        t_nst, t_lst, t_ost = T("nst"), T("lst"), T("ost")
        t_junk = T("junk")
        t_bank = [T(f"bank{i}") for i in range(8)]
        t_x1c = [T(f"x1c{i}") for i in range((NR0 - 8) * 64 // 128)]
        t_xrh = [T("xrh0"), T("xrh1")]
        t_lsa, t_lsd = T("lsa"), T("lsd")
        main_toks = (t_hT + t_hTx[0] + t_hTx[1] + [t_K, t_V] + t_slab + [t_xn, t_xr, t_xs] + t_qT + t_sza + [t_gt, t_szb, t_vn, t_yb]
                     + t_pT + [t_mg] + t_msg + [t_rden, t_at, t_bg] + t_sp + [t_wst, t_tt, t_lnc, t_pg, t_const,
                                                                           t_nst, t_lst, t_ost, t_junk, t_lsa, t_lsd] + t_xrh)

        NST = st[:, 0:16]
        LST = st[:, 16:64]
        OST = st[:, 64:96]
        EPS_RMS = st[:, 96:97]
        EPS_LN = st[:, 97:98]

        NB = 2
        pci = [f32(16384 * i, 4096) for i in range(NB)]
        pco = [bf(49152 + 8192 * i, 4096) for i in range(NB)]
        gti = [[f32(73728 + 16384 * (l * 4 + k), 4096) for k in range(4)] for l in range(n_layers)]
        assert 73728 + 16384 * 4 * n_layers <= TOTAL
        t_pci = [T(f"pci{i}") for i in range(NB)]
        t_pco = [T(f"pco{i}") for i in range(NB)]
        t_gain = [T(f"gain{k}") for k in range(4 * n_layers)]
        t_wq = [[T(f"wq{l}_{s}") for s in range(NS)] for l in range(n_layers)]
        jobs = [(l, s) for l in range(n_layers) for s in range(NS)]
        for l in range(n_layers):
            for k in range(4):
                P.op("sp", lambda h, l=l, k=k: h.dma_start(out=gti[l][k], in_=gains[l, k]), writes=[t_gain[l * 4 + k]],
                     dma_key=f"gain{l * 4 + k}")

        def pc_load(q):
            l, s = jobs[q]
            n = slab_used(s)
            i = q % NB
            srcap = wsl[l, s, :, 0:n] if s < NSW else spt[l, s - NSW]
            P.op("sp", lambda h: h.dma_start(out=pci[i][:, 0:n], in_=srcap), writes=[t_pci[i]], dma_key=f"pci{i}")

        def pc_cast(q):
            l, s = jobs[q]
            n = slab_used(s)
            i = q % NB
            g = slab_gain(s)
            if g is None:
                P.op("act", lambda h: h.activation(out=pco[i][:, 0:n], in_=pci[i][:, 0:n], func=AF.Copy),
                     reads=[t_pci[i]], writes=[t_pco[i]])
            else:
                eng = "dve"
                P.op(eng, lambda h: h.tensor_tensor(out=pco[i][:, 0:n], in0=pci[i][:, 0:n], in1=gti[l][g][:, 0:n], op=ALU.mult),
                     reads=[t_pci[i], t_gain[l * 4 + g]], writes=[t_pco[i]])

        def pc_store(q):
            l, s = jobs[q]
            n = slab_used(s)
            i = q % NB
            P.op("sp", lambda h: h.dma_start(out=wq[l, s, :, 0:n], in_=pco[i][:, 0:n]),
                 reads=[t_pco[i]], writes=[t_wq[l][s]], dma_key=f"pco{i}")

        pc_load(0)
        for q in range(len(jobs)):
            if q + 1 < len(jobs):
                pc_load(q + 1)
            pc_cast(q)
            pc_store(q)
        P.op("pool", lambda h: h.memset(st[:, 128:129], 0.0), writes=t_pci + t_pco + t_gain + main_toks)
        P.op("pool", lambda h: h.memset(EPS_RMS, 1e-6), writes=[t_const])
        P.op("pool", lambda h: h.memset(EPS_LN, 1e-5), writes=[t_const])
        P.op("sp", lambda h: h.dma_start(out=ident, in_=ident_d), writes=[t_const], dma_key="const")
        P.op("sp", lambda h: h.dma_start(out=ones, in_=ones_d), writes=[t_const], dma_key="const")

        class Stream:
            def __init__(self):
                self.plan = []
                self.issued = 0
                self.pos = 0

            def issue(self, upto):
                while self.issued < min(upto, len(self.plan)):
                    i = self.issued
                    l, s = self.plan[i]
                    n = slab_used(s)
                    b = i % 3
                    P.op("sp", lambda h, l=l, s=s, n=n, b=b: h.dma_start(out=slabs[b][:, 0:n], in_=wq[l, s, :, 0:n]),
                         reads=[t_wq[l][s]], writes=[t_slab[b]], dma_key=f"slab{b}")
                    self.issued += 1

            def get(self, l, s):
                i = self.pos
                assert self.plan[i] == (l, s), (i, self.plan[i], l, s)
                self.issue(i + 3)
                self.pos += 1
                return slabs[i % 3], t_slab[i % 3]

        stream = Stream()

        def layer_plan(l, nfull):
            pl = []
            kv = [(l, s) for s in range(0, 8)]
            pl += kv + kv
            for s_ in range(nfull):
                pl += [(l, s) for s in range(8, 12)]
                pl += [(l, s) for s in range(12, 20)]
                pl += kv
                pl += [(l, s) for s in range(20, 28)]
                pl += [(l, s) for s in range(28, 60)]
                pl += [(l, s) for s in range(60, 68)]
            return pl

        for l in range(n_layers):
            NR = NR0 - 8 * l
            stream.plan += layer_plan(l, (NR - 8) // 8)

        out_stores = []

        def do_layer(l):
            NR = NR0 - 8 * l
            nfull = (NR - 8) // 8
            src = xin if l == 0 else x1s
            dst = y if l == n_layers - 1 else x1s
            def src_tok(r0):
                return [t_x1c[r0 // 128]] if l > 0 else []

            def dst_tok(q0):
                return [t_x1c[q0 // 128]] if l < n_layers - 1 else []

            xrh = [f32(o_xr, 1024), f32(o_xr + 4096, 1024)]
            e0 = 8 - 4 * l if NR0 == 80 else 4
            if NR0 != 80:
                assert n_layers == 1

            def blk(j):
                if j == 0:
                    return 0, 256, 0
                if j == nfull + 1:
                    return NR - 4, 256, (NR - 4) // 2
                return 4 + 8 * (j - 1), 512, 2 + 4 * (j - 1)

            def slot(pi):
                return (pi + 2) % 12

            P.op("sp", lambda h, l=l: h.dma_start(out=bf(o_bg, 4096), in_=wq[l, 68, :, 0:4096]),
                 reads=[t_wq[l][68]], writes=[t_bg], dma_key="lc_bg")
            P.op("sp", lambda h, l=l: h.dma_start(out=bf(o_bg + 8192, 1024), in_=wq[l, 69, :, 0:1024]),
                 reads=[t_wq[l][69]], writes=[t_bg], dma_key="lc_bg")
            P.op("sp", lambda h, l=l: h.dma_start(out=bf(o_wst, 1024), in_=wq[l, 70, :, 0:1024]),
                 reads=[t_wq[l][70]], writes=[t_wst], dma_key="lc_wst")
            P.op("sp", lambda h, l=l: h.dma_start(out=f32(o_tt, 1024), in_=sgb[l]), writes=[t_tt], dma_key="lc_tt")
            P.op("sp", lambda h, l=l: h.dma_start(out=lncb, in_=lnc[l]), writes=[t_lnc], dma_key="lc_lnc")
            P.op("sp", lambda h, l=l: h.dma_start(out=pgb, in_=postg[l]), writes=[t_pg], dma_key="lc_pg")
            for c in range(2):
                P.op("pe", lambda h, c=c: h.matmul(bank(c), lhsT=ones, rhs=bf(o_wst + c * 1024, 512), start=True, stop=True),
                     reads=[t_const, t_wst], writes=[t_bank[c]])
            for g in range(8):
                P.op("dve", lambda h, g=g: h.scalar_tensor_tensor(
                    out=ttab[:, g, :], in0=bank(g // 4, 128, (g % 4) * 128), scalar=lncb[:, 8 + g:9 + g],
                    in1=ttab[:, g, :], op0=ALU.mult, op1=ALU.add),
                    reads=[t_bank[g // 4], t_lnc, t_tt], writes=[t_tt])

            def norm_elem_tile(j, t):
                row0, ntok, _ = blk(j)
                r0 = row0 * 64 + t * 128
                P.op("sp", lambda h: h.dma_start(out=xn, in_=src[r0:r0 + 128, :]),
                     reads=src_tok(r0), writes=[t_xn], dma_key="xn")
                P.op("act", lambda h: h.activation(out=xs, in_=xn, func=AF.Square, accum_out=NST[:, t:t + 1]),
                     reads=[t_xn], writes=[t_xs, t_nst])
                P.op("act", lambda h: h.activation(out=NST[:, 4 + t:5 + t], in_=NST[:, t:t + 1], func=AF.Sqrt,
                                                   scale=1.0 / D, bias=EPS_RMS),
                     reads=[t_nst, t_const], writes=[t_nst])
                P.op("dve", lambda h: h.reciprocal(out=NST[:, 8 + t:9 + t], in_=NST[:, 4 + t:5 + t]),
                     reads=[t_nst], writes=[t_nst])
                P.op("act", lambda h: h.activation(out=xs, in_=xn, func=AF.Copy, scale=NST[:, 8 + t:9 + t]),
                     reads=[t_xn, t_nst], writes=[t_xs])

            def norm_trans_tile(j, t):
                par = j % 2
                for kc in range(16):
                    P.op("pe", lambda h, kc=kc: h.transpose(out=bank_bf(6, 2048)[:, kc * 128:(kc + 1) * 128],
                                                            in_=xs[:, kc * 128:(kc + 1) * 128], identity=ident),
                         reads=[t_xs, t_const], writes=[t_bank[6], t_bank[7]])
                P.op("dve", lambda h: h.tensor_copy(
                    out=hT[par][:, :, t * 128:(t + 1) * 128],
                    in_=bank_bf(6, 2048).rearrange("p (k c) -> p k c", c=128)),
                    reads=[t_bank[6], t_bank[7]], writes=[t_hT[par]] + t_hTx[par])

            def norm_elem(j):
                for t in range(blk(j)[1] // 128):
                    norm_elem_tile(j, t)
                    norm_trans_tile(j, t)

            def kv_phase(j):
                row0, ntok, p0 = blk(j)
                nt = ntok // 128
                par = j % 2
                s0 = slot(p0)
                for i in range(4):
                    sl, tsl = stream.get(l, i)
                    for ch in range(2):
                        hd = 2 * i + ch
                        b = (2 * i + ch) % 8
                        for kc in range(16):
                            P.op("pe", lambda h, sl=sl, ch=ch, kc=kc, b=b: h.matmul(
                                bank(b, ntok), lhsT=sl[:, ch * 2048 + kc * 128:ch * 2048 + (kc + 1) * 128],
                                rhs=hT[par][:, kc, 0:ntok], start=(kc == 0), stop=(kc == 15)),
                                reads=[tsl, t_hT[par]], writes=[t_bank[b]])
                        P.op("dve", lambda h, hd=hd, b=b: h.tensor_copy(out=Kr[:, hd, s0 * 128:s0 * 128 + ntok], in_=bank(b, ntok)),
                             reads=[t_bank[b]], writes=[t_K])
                for cg in range(2):
                    for kh in range(2):
                        sl, tsl = stream.get(l, 4 + cg * 2 + kh)
                        for t in range(nt):
                            b = cg * 4 + t
                            for kcl in range(8):
                                kc = kh * 8 + kcl
                                P.op("pe", lambda h, sl=sl, t=t, kc=kc, kcl=kcl, b=b: h.matmul(
                                    bank(b), lhsT=hT[par][:, kc, t * 128:(t + 1) * 128],
                                    rhs=sl[:, kcl * 512:(kcl + 1) * 512], start=(kc == 0), stop=(kc == 15)),
                                    reads=[tsl, t_hT[par]], writes=[t_bank[b]])
                    for t in range(nt):
                        b = cg * 4 + t
                        P.op("act", lambda h, t=t, b=b, cg=cg: h.activation(
                            out=Vr[:, s0 + t, cg * 512:(cg + 1) * 512], in_=bank(b), func=AF.Copy),
                            reads=[t_bank[b]], writes=[t_V])

            def vs_slab(j, cg, kh):
                par = j % 2
                sl, tsl = stream.get(l, 8 + cg * 2 + kh)
                for t in range(4):
                    b = cg * 4 + t
                    for kcl in range(8):
                        kc = kh * 8 + kcl
                        P.op("pe", lambda h, sl=sl, t=t, kc=kc, kcl=kcl, b=b: h.matmul(
                            bank(b), lhsT=hT[par][:, kc, t * 128:(t + 1) * 128],
                            rhs=sl[:, kcl * 512:(kcl + 1) * 512], start=(kc == 0), stop=(kc == 15)),
                            reads=[tsl, t_hT[par]], writes=[t_bank[b]])

            def g_phase_ln(j):
                for cg in range(2):
                    for t in range(4):
                        b = cg * 4 + t
                        c = cg * 4 + t
                        P.op("act", lambda h, b=b, c=c: h.activation(out=ojunk, in_=bank(b), func=AF.Square,
                                                                     accum_out=LST[:, 8 + c:9 + c]),
                             reads=[t_bank[b]], writes=[t_junk, t_lst])
                        P.op("dve", lambda h, b=b, c=c: h.tensor_reduce(out=LST[:, c:c + 1], in_=bank(b), axis=AX.X, op=ALU.add),
                             reads=[t_bank[b]], writes=[t_lst])
                rl = [t_lst]
                P.op("dve", lambda h: h.tensor_tensor(out=LST[:, 16:20], in0=LST[:, 0:4], in1=LST[:, 4:8], op=ALU.add), reads=rl, writes=rl)
                P.op("dve", lambda h: h.tensor_tensor(out=LST[:, 20:24], in0=LST[:, 8:12], in1=LST[:, 12:16], op=ALU.add), reads=rl, writes=rl)
                P.op("dve", lambda h: h.tensor_scalar(out=LST[:, 16:20], in0=LST[:, 16:20], scalar1=1.0 / 1024, scalar2=0.0,
                                                      op0=ALU.mult, op1=ALU.add), reads=rl, writes=rl)
                P.op("dve", lambda h: h.tensor_tensor(out=LST[:, 24:28], in0=LST[:, 16:20], in1=LST[:, 16:20], op=ALU.mult), reads=rl, writes=rl)
                P.op("dve", lambda h: h.scalar_tensor_tensor(out=LST[:, 24:28], in0=LST[:, 20:24], scalar=1.0 / 1024,
                                                             in1=LST[:, 24:28], op0=ALU.mult, op1=ALU.subtract), reads=rl, writes=rl)
                P.op("act", lambda h: h.activation(out=LST[:, 24:28], in_=LST[:, 24:28], func=AF.Sqrt, scale=1.0, bias=EPS_LN),
                     reads=[t_lst, t_const], writes=rl)
                P.op("dve", lambda h: h.reciprocal(out=LST[:, 28:32], in_=LST[:, 24:28]), reads=rl, writes=rl)
                P.op("dve", lambda h: h.scalar_tensor_tensor(out=LST[:, 32:36], in0=LST[:, 16:20], scalar=-1.0,
                                                             in1=LST[:, 28:32], op0=ALU.mult, op1=ALU.mult), reads=rl, writes=rl)
                for cg in range(2):
                    for t in range(4):
                        b = cg * 4 + t
                        P.op("act", lambda h, t=t, b=b, cg=cg: h.activation(
                            out=vn[:, t, cg * 512:(cg + 1) * 512], in_=bank(b), func=AF.Identity,
                            scale=LST[:, 28 + t:29 + t], bias=LST[:, 32 + t:33 + t]),
                            reads=[t_bank[b], t_lst], writes=[t_vn])

            def g_phase_uz(j, g):
                par = j % 2
                sl, tsl = stream.get(l, 12 + g)
                bs = (g % 2) * 3
                bu, bz, bsx = bs, bs + 1, bs + 2
                for which, b in ((0, bu), (1, bz)):
                    for kc in range(16):
                        P.op("pe", lambda h, sl=sl, which=which, kc=kc, b=b: h.matmul(
                            bank(b), lhsT=sl[:, which * 2048 + kc * 128:which * 2048 + (kc + 1) * 128],
                            rhs=hT[par][:, kc, :], start=(kc == 0), stop=(kc == 15)),
                            reads=[tsl, t_hT[par]], writes=[t_bank[b]])
                for t in range(4):
                    P.op("pe", lambda h, t=t, g=g, b=bsx: h.matmul(
                        bank(b, 128, t * 128), lhsT=vn[:, t, g * 128:(g + 1) * 128], rhs=wst[:, g, :], start=True, stop=True),
                        reads=[t_vn, t_wst], writes=[t_bank[bsx]])
                P.op("act", lambda h, b=bz: h.activation(out=szb, in_=bank(b), func=AF.Silu), reads=[t_bank[bz]], writes=[t_szb])
                for t in range(4):
                    P.op("dve", lambda h, t=t, g=g, b=bsx: h.scalar_tensor_tensor(
                        out=gt[:, t * 128:(t + 1) * 128], in0=bank(b, 128, t * 128), scalar=lncb[:, g:g + 1],
                        in1=ttab[:, g, :], op0=ALU.mult, op1=ALU.add),
                        reads=[t_bank[bsx], t_lnc, t_tt], writes=[t_gt])
                P.op("dve", lambda h, b=bu: h.tensor_tensor(out=gt, in0=gt, in1=bank(b), op=ALU.mult),
                     reads=[t_gt, t_bank[bu]], writes=[t_gt])
                P.op("dve", lambda h, g=g: h.tensor_tensor(out=ybT[:, g, :], in0=gt, in1=szb, op=ALU.mult),
                     reads=[t_gt, t_szb], writes=[t_yb])

            def q_proj(j, hd):
                par = j % 2
                sl, tsl = stream.get(l, 20 + hd)
                for which, b in ((0, 0), (1, 1)):
                    for kc in range(16):
                        P.op("pe", lambda h, sl=sl, which=which, kc=kc, b=b: h.matmul(
                            bank(b), lhsT=sl[:, which * 2048 + kc * 128:which * 2048 + (kc + 1) * 128],
                            rhs=hT[par][:, kc, :], start=(kc == 0), stop=(kc == 15)),
                            reads=[tsl, t_hT[par]], writes=[t_bank[b]])
                P.op("dve", lambda h, hd=hd: h.tensor_scalar(out=qT[hd % 2], in0=bank(0), scalar1=QSCALE, scalar2=0.0,
                                                             op0=ALU.mult, op1=ALU.add),
                     reads=[t_bank[0]], writes=[t_qT[hd % 2]])
                if EXPSILU:
                    P.op("act", lambda h, hd=hd: h.activation(out=sza[hd % 2], in_=bank(1), func=AF.Exp, scale=-1.0),
                         reads=[t_bank[1]], writes=[t_sza[hd % 2]])
                    P.op("dve", lambda h, hd=hd: h.tensor_scalar(out=sza[hd % 2], in0=sza[hd % 2], scalar1=1.0, scalar2=0.0,
                                                                 op0=ALU.add, op1=ALU.add),
                         reads=[t_sza[hd % 2]], writes=[t_sza[hd % 2]])
                    P.op("dve", lambda h, hd=hd: h.reciprocal(out=sza[hd % 2], in_=sza[hd % 2]),
                         reads=[t_sza[hd % 2]], writes=[t_sza[hd % 2]])
                    P.op("dve", lambda h, hd=hd: h.tensor_tensor(out=sza[hd % 2], in0=sza[hd % 2], in1=bank(1), op=ALU.mult),
                         reads=[t_bank[1], t_sza[hd % 2]], writes=[t_sza[hd % 2]])
                else:
                    P.op("act", lambda h, hd=hd: h.activation(out=sza[hd % 2], in_=bank(1), func=AF.Silu),
                         reads=[t_bank[1]], writes=[t_sza[hd % 2]])

            sp_ctr = [0]

            def attn(j, hd):
                _, _, p0 = blk(j)
                tiles = []
                for t in range(4):
                    tp = p0 + t
                    row = 2 * tp
                    spi = None
                    if row == e0:
                        spi, offs = 0, list(range(-2, 4))
                    elif row == e0 + 2:
                        spi, offs = 1, list(range(-2, 3))
                    elif row == e0 + 60:
                        spi, offs = 2, list(range(-2, 3))
                    elif row == e0 + 62:
                        spi, offs = 3, list(range(-3, 3))
                    else:
                        offs = list(range(-2, 3))
                    tiles.append((tp, spi, offs))

                def S(t):
                    tp, spi, offs = tiles[t]
                    bx, by = (4, 5) if t % 2 == 0 else (6, 7)
                    tb = None
                    ttb = t_bg
                    if spi is not None:
                        k = sp_ctr[0] % 2
                        sp_ctr[0] += 1
                        sidx = 71 + spi * 2 + hd // 4
                        P.op("sp", lambda h, k=k, sidx=sidx: h.dma_start(
                            out=bf(o_sp[k], 768), in_=wq[l, sidx, :, (hd % 4) * 768:(hd % 4) * 768 + 768]),
                            reads=[t_wq[l][sidx]], writes=[t_sp[k]], dma_key=f"spt{k}")
                        tb = spb[k]
                        ttb = t_sp[k]
                    for i, o in enumerate(offs):
                        pi = tp + o
                        sk = slot(pi)
                        b = bx if i < 4 else by
                        c0 = (i % 4) * 128
                        P.op("pe", lambda h, sk=sk, b=b, c0=c0, t=t: h.matmul(
                            bank(b, 128, c0), lhsT=Kr[:, hd, sk * 128:(sk + 1) * 128], rhs=qT[hd % 2][:, t * 128:(t + 1) * 128],
                            start=True, stop=False),
                            reads=[t_K, t_qT[hd % 2]], writes=[t_bank[b]])
                        if tb is None:
                            rhs = bgen[:, i, hd, :]
                        else:
                            rhs = tb[:, i, :]
                        P.op("pe", lambda h, b=b, c0=c0, rhs=rhs: h.matmul(bank(b, 128, c0), lhsT=ident, rhs=rhs, start=False, stop=True),
                             reads=[t_const, ttb], writes=[t_bank[b]])

                def E(t):
                    tp, spi, offs = tiles[t]
                    bx, by = (4, 5) if t % 2 == 0 else (6, 7)
                    n2 = (len(offs) - 4) * 128
                    P.op("act", lambda h, t=t, bx=bx: h.activation(out=pT[t % 2][:, 0:512], in_=bank(bx), func=AF.Exp),
                         reads=[t_bank[bx]], writes=[t_pT[t % 2]])
                    P.op("act", lambda h, t=t, by=by, n2=n2: h.activation(out=pT[t % 2][:, 512:512 + n2], in_=bank(by, n2), func=AF.Exp),
                         reads=[t_bank[by]], writes=[t_pT[t % 2]])

                def PV(t):
                    tp, spi, offs = tiles[t]
                    n = len(offs)
                    for i, o in enumerate(offs):
                        sk = slot(tp + o)
                        P.op("pe", lambda h, sk=sk, i=i, t=t, n=n: h.matmul(
                            bank(2, 128, t * 128), lhsT=Vr[:, sk, hd * 128:(hd + 1) * 128], rhs=pT[t % 2][:, i * 128:(i + 1) * 128],
                            start=(i == 0), stop=(i == n - 1)),
                            reads=[t_V, t_pT[t % 2]], writes=[t_bank[2]])
                        P.op("pe", lambda h, i=i, t=t, n=n: h.matmul(
                            bank(3, 128, t * 128), lhsT=ones, rhs=pT[t % 2][:, i * 128:(i + 1) * 128],
                            start=(i == 0), stop=(i == n - 1)),
                            reads=[t_const, t_pT[t % 2]], writes=[t_bank[3]])

                S(0)
                for t in range(4):
                    if t + 1 < 4:
                        S(t + 1)
                    E(t)
                    PV(t)
                P.op("dve", lambda h: h.reciprocal(out=rden, in_=bank(3)), reads=[t_bank[3]], writes=[t_rden])
                P.op("dve", lambda h: h.tensor_tensor(out=atb, in0=bank(2), in1=rden, op=ALU.mult),
                     reads=[t_bank[2], t_rden], writes=[t_at])
                P.op("pool", lambda h: h.tensor_tensor(out=yaT[:, hd, :], in0=atb, in1=sza[hd % 2], op=ALU.mult),
                     reads=[t_at, t_sza[hd % 2]], writes=[t_vn])

            def m_phase(j):
                par = j % 2
                for jj in range(16):
                    bs = (jj % 2) * 4
                    for half in range(2):
                        sl, tsl = stream.get(l, 28 + 2 * jj + half)
                        bg_, bp_ = bs + 2 * half, bs + 2 * half + 1
                        for kc in range(16):
                            P.op("pe", lambda h, sl=sl, kc=kc, b=bg_: h.matmul(
                                bank(b), lhsT=sl[:, kc * 128:(kc + 1) * 128], rhs=hT[par][:, kc, :],
                                start=(kc == 0), stop=(kc == 15)),
                                reads=[tsl, t_hT[par]], writes=[t_bank[bg_]])
                        yT = yaT if half == 0 else ybT
                        ty = t_vn if half == 0 else t_yb
                        for c in range(8):
                            P.op("pe", lambda h, sl=sl, c=c, b=bp_, yT=yT: h.matmul(
                                bank(b), lhsT=sl[:, 2048 + c * 128:2048 + (c + 1) * 128], rhs=yT[:, c, :],
                                start=(c == 0), stop=(c == 7)),
                                reads=[tsl, ty], writes=[t_bank[bp_]])
                        P.op("act", lambda h, half=half, b=bg_: h.activation(out=msg[half], in_=bank(b), func=AF.Sigmoid),
                             reads=[t_bank[bg_]], writes=[t_msg[half]])
                        P.op("dve", lambda h, half=half, b=bp_: h.tensor_tensor(out=msg[half], in0=msg[half], in1=bank(b), op=ALU.mult),
                             reads=[t_msg[half], t_bank[bp_]], writes=[t_msg[half]])
                    P.op("pool", lambda h, jj=jj: h.tensor_tensor(out=mgT[:, jj, :], in0=msg[0], in1=msg[1], op=ALU.add),
                         reads=t_msg, writes=[t_mg])

            def o_phase(j, early=None):
                row0, _, _ = blk(j)
                par = j % 2
                ob = [f32(o_msg[0], 2048), f32(o_yb, 2048), f32(o_hT[par], 2048), f32(o_hT[par] + 8192, 2048)]
                t_ob = [[t_msg[0], t_msg[1], t_rden, t_at], [t_yb], [t_hTx[par][0]], [t_hTx[par][1]]]
                for cg in range(4):
                    bs = (cg % 2) * 4
                    for kh in range(2):
                        sl, tsl = stream.get(l, 60 + cg * 2 + kh)
                        for t in range(4):
                            b = bs + t
                            for kcl in range(8):
                                kc = kh * 8 + kcl
                                P.op("pe", lambda h, sl=sl, t=t, kc=kc, kcl=kcl, b=b: h.matmul(
                                    bank(b), lhsT=mgT[:, kc, t * 128:(t + 1) * 128], rhs=sl[:, kcl * 512:(kcl + 1) * 512],
                                    start=(kc == 0), stop=(kc == 15)),
                                    reads=[tsl, t_mg], writes=[t_bank[b]])
                    for t in range(4):
                        b = bs + t
                        wt = t_ob[t] + ([t_hT[par]] if (cg == 0 and t >= 2) else [])
                        P.op("act", lambda h, t=t, b=b, cg=cg: h.activation(out=ob[t][:, cg * 512:(cg + 1) * 512], in_=bank(b), func=AF.Copy),
                             reads=[t_bank[b]], writes=wt)
                        P.op("act", lambda h, t=t, cg=cg: h.activation(out=ojunk, in_=ob[t][:, cg * 512:(cg + 1) * 512], func=AF.Square,
                                                                       accum_out=OST[:, t * 4 + cg:t * 4 + cg + 1]),
                             reads=t_ob[t], writes=[t_junk, t_ost])
                if early is not None:
                    early()
                ro = [t_ost]
                P.op("dve", lambda h: h.tensor_reduce(out=OST[:, 16:20], in_=OST[:, 0:16].rearrange("p (t c) -> p t c", c=4),
                                                      axis=AX.X, op=ALU.add), reads=ro, writes=ro)
                P.op("act", lambda h: h.activation(out=OST[:, 20:24], in_=OST[:, 16:20], func=AF.Sqrt, scale=1.0 / D, bias=EPS_RMS),
                     reads=[t_ost, t_const], writes=ro)
                P.op("dve", lambda h: h.reciprocal(out=OST[:, 24:28], in_=OST[:, 20:24]), reads=ro, writes=ro)
                for t in (1, 2, 3, 0):
                    r0 = row0 * 64 + t * 128
                    q0 = (row0 - 4) * 64 + t * 128
                    P.op(TAILQ, lambda h, r0=r0: h.dma_start(out=xr, in_=src[r0:r0 + 128, :]),
                         reads=src_tok(r0), writes=t_xrh, dma_key="xr")
                    P.op("dve", lambda h, t=t: h.scalar_tensor_tensor(out=ob[t], in0=ob[t], scalar=OST[:, 24 + t:25 + t], in1=pgb,
                                                                      op0=ALU.mult, op1=ALU.mult),
                         reads=t_ob[t] + [t_ost, t_pg], writes=t_ob[t])
                    for hf in range(2):
                        P.op("pool", lambda h, t=t, hf=hf: h.tensor_tensor(out=ob[t][:, hf * 1024:(hf + 1) * 1024],
                                                                          in0=ob[t][:, hf * 1024:(hf + 1) * 1024], in1=xrh[hf], op=ALU.add),
                             reads=t_ob[t] + [t_xrh[hf]], writes=t_ob[t])
                    o = P.op(TAILQ, lambda h, t=t, q0=q0: h.dma_start(out=dst[q0:q0 + 128, :], in_=ob[t]),
                             reads=t_ob[t], writes=dst_tok(q0), dma_key=f"out{t}")
                    if l == n_layers - 1:
                        out_stores.append(o)

            norm_elem(0)
            kv_phase(0)
            norm_elem(1)
            kv_phase(1)
            pre = False
            for s_ in range(nfull):
                j = s_ + 1
                ntn = blk(j + 1)[1] // 128
                if not pre:
                    norm_elem_tile(j + 1, 0)
                vs_slab(j, 0, 0)
                vs_slab(j, 0, 1)
                vs_slab(j, 1, 0)
                vs_slab(j, 1, 1)
                if j == 1 and l == 0:
                    dump("hT", bf(o_hT[1], 8192), 8192, BF16, [t_hT[1]])
                g_phase_ln(j)
                if j == 1 and l == 0:
                    dump("vn", bf(o_vn, 4096), 4096, BF16, [t_vn])
                tsched = {2: 0, 4: 1, 6: 2, 7: 3}
                for g in range(8):
                    g_phase_uz(j, g)
                    k = tsched.get(g)
                    if k is not None and k < ntn:
                        norm_trans_tile(j + 1, k)
                        if k + 1 < ntn:
                            norm_elem_tile(j + 1, k + 1)
                if j == 1 and l == 0:
                    dump("yb", bf(o_yb, 4096), 4096, BF16, [t_yb])
                kv_phase(j + 1)
                if j == 1 and l == 0:
                    dump("K", bf(o_K, 12288), 12288, BF16, [t_K])
                    dump("V", bf(o_V, 12288), 12288, BF16, [t_V])
                q_proj(j, 0)
                for hd in range(8):
                    if hd + 1 < 8:
                        q_proj(j, hd + 1)
                    attn(j, hd)
                if j == 1 and l == 0:
                    dump("ya", bf(o_vn, 4096), 4096, BF16, [t_vn])
                m_phase(j)
                if j == 1 and l == 0:
                    dump("mg", bf(o_mg, 8192), 8192, BF16, [t_mg])
                pre = (j + 1 <= nfull)
                o_phase(j, early=(lambda j=j: norm_elem_tile(j + 2, 0)) if pre else None)

        for l in range(n_layers):
            do_layer(l)

        assert stream.pos == len(stream.plan), (stream.pos, len(stream.plan))
        last = {}
        for o in out_stores:
            last[o.dma_key] = o
        P.emit(final_dma_ops=list(last.values()) + dbg_ops[-1:])
    return nc


def _a_chunk(W3, col0):
    return W3[:, :, col0:col0 + 128].transpose(1, 0, 2).reshape(128, -1)


def _b_slab(W, col0, kh):
    Wk = W.reshape(-1, 128, W.shape[1])
    return Wk[kh * 8:(kh + 1) * 8, :, col0:col0 + 512].transpose(1, 0, 2).reshape(128, 4096)


def _pair_table(rpb, qrow_g, offs, rows_total, generic):
    kc = np.arange(64)
    qc = np.arange(64)
    cs = np.clip(qc - 8, 0, 48)
    colvalid = (kc[:, None] >= cs[None, :]) & (kc[:, None] < cs[None, :] + 16)
    dc = kc[:, None] - qc[None, :] + 15
    dcc = np.clip(dc, 0, 30)
    out = np.full((2, 64, len(offs), 8, 2, 64), NEG, dtype=np.float32)
    for i, o in enumerate(offs):
        for kr in range(2):
            for qr in range(2):
                krow = qrow_g + 2 * o + kr
                qrow = qrow_g + qr
                if generic:
                    rs = qrow - 4
                else:
                    rs = min(max(qrow - 4, 0), rows_total - 8)
                    if qrow < 0 or qrow >= rows_total:
                        rs = qrow - 4
                dr = krow - qrow + 7
                if krow < rs or krow >= rs + 8 or dr < 0 or dr > 14:
                    continue
                vals = rpb[:, dr, :][:, dcc]
                sel = np.where(colvalid[None], vals, np.float32(NEG))
                out[kr, :, i, :, qr, :] = sel.transpose(1, 0, 2)
    return out.reshape(128, len(offs), 8, 128)


def _layer_arrays(lw, wsl):
    (pre_g, post_g, w_in, rpb, ln_g, ln_b, sg_w, sg_b, w_pa, w_pb, w_out) = lw
    W3 = w_in.reshape(16, 128, NIN)
    for i in range(4):
        wsl[i, :, 0:2048] = _a_chunk(W3, 1024 + (2 * i) * 128)
        wsl[i, :, 2048:4096] = _a_chunk(W3, 1024 + (2 * i + 1) * 128)
    for cg in range(2):
        for kh in range(2):
            wsl[4 + cg * 2 + kh] = _b_slab(w_in, 2048 + cg * 512, kh)
            wsl[8 + cg * 2 + kh] = _b_slab(w_in, 5120 + cg * 512, kh)
    for g in range(8):
        wsl[12 + g, :, 0:2048] = _a_chunk(W3, 4096 + g * 128)
        wsl[12 + g, :, 2048:4096] = _a_chunk(W3, 6144 + g * 128)
        wsl[20 + g, :, 0:2048] = _a_chunk(W3, g * 128)
        wsl[20 + g, :, 2048:4096] = _a_chunk(W3, 3072 + g * 128)
    PA3 = w_pa.reshape(8, 128, D)
    PB3 = w_pb.reshape(8, 128, D)
    for j in range(16):
        wsl[28 + 2 * j, :, 0:2048] = _a_chunk(W3, 7168 + j * 128)
        wsl[28 + 2 * j, :, 2048:3072] = _a_chunk(PA3, j * 128)
        wsl[29 + 2 * j, :, 0:2048] = _a_chunk(W3, 9216 + j * 128)
        wsl[29 + 2 * j, :, 2048:3072] = _a_chunk(PB3, j * 128)
    for cg in range(4):
        for kh in range(2):
            wsl[60 + cg * 2 + kh] = _b_slab(w_out, cg * 512, kh)
    return wsl


def _gains(pre_g):
    gc = pre_g.reshape(16, 128).T
    e = np.arange(SLAB)
    GA = gc[:, (e // 128) % 16]
    GB0 = gc[:, e // 512]
    GB1 = gc[:, 8 + e // 512]
    GM = np.where(e[None, :] < 2048, GA, np.float32(1.0))
    return np.stack([GA, GB0, GB1, GM]).astype(np.float32)


_NC_CACHE = {}


def _get_nc(n_layers, NR0):
    key = (n_layers, NR0)
    if key not in _NC_CACHE:
        _NC_CACHE[key] = build_nc(n_layers, NR0)
    return _NC_CACHE[key]


def _core_inputs(x, layers, NR0):
    nl = len(layers)
    ident = np.eye(128, dtype=np.float32).astype(ml_dtypes.bfloat16)
    ones = np.ones((128, 128), dtype=np.float32).astype(ml_dtypes.bfloat16)
    halo = (NR0 - 64) // 2
    xg = x.reshape(2, 256, 64, D)
    wsl = np.zeros((nl, NSW, 128, SLAB), dtype=np.float32)
    gains = np.zeros((nl, 4, 128, SLAB), dtype=np.float32)
    sgb = np.zeros((nl, 128, 1024), dtype=np.float32)
    lnc = np.zeros((nl, 128, 16), dtype=np.float32)
    postg = np.zeros((nl, 128, D), dtype=np.float32)
    for li, lw in enumerate(layers):
        (pre_g, post_g, w_in, rpb, ln_g, ln_b, sg_w, sg_b, w_pa, w_pb, w_out) = lw
        _layer_arrays(lw, wsl[li])
        gains[li] = _gains(pre_g)
        sgb[li] = np.broadcast_to(sg_b.reshape(1, 1024), (128, 1024))
        lnc[li, :, 0:8] = ln_g.reshape(8, 128).T
        lnc[li, :, 8:16] = ln_b.reshape(8, 128).T
        postg[li] = np.broadcast_to(post_g.reshape(1, D), (128, D))
        wsl[li, 70, :, 0:1024] = sg_w.transpose(2, 0, 1).reshape(128, 1024)
        g5 = _pair_table(rpb, 0, list(range(-2, 3)), 256, True).reshape(128, 5120)
        wsl[li, 68] = g5[:, 0:4096]
        wsl[li, 69, :, 0:1024] = g5[:, 4096:5120]
    in_maps = []
    for c in range(8):
        bi, p = c // 4, c % 4
        R0 = 64 * p
        xl = np.zeros((NR0, 64, D), dtype=np.float32)
        lo, hi = R0 - halo, R0 + 64 + halo
        slo, shi = max(lo, 0), min(hi, 256)
        xl[slo - lo:shi - lo] = xg[bi, slo:shi]
        spt = np.zeros((nl, 8, 128, 3072), dtype=np.float32)
        for li, lw in enumerate(layers):
            rpb = lw[3]
            specs = [(R0, list(range(-2, 4))), (R0 + 2, list(range(-2, 3))),
                     (R0 + 60, list(range(-2, 3))), (R0 + 62, list(range(-3, 3)))]
            for spi, (qrow, offs) in enumerate(specs):
                tb = _pair_table(rpb, qrow, offs, 256, False)
                full = np.full((128, 6, 8, 128), NEG, dtype=np.float32)
                full[:, 0:len(offs)] = tb
                hm = full.transpose(0, 2, 1, 3)
                spt[li, spi * 2 + 0] = hm[:, 0:4].reshape(128, 3072)
                spt[li, spi * 2 + 1] = hm[:, 4:8].reshape(128, 3072)
        in_maps.append({"xin": xl.reshape(NR0 * 64, D), "wsl": wsl, "spt": spt, "gains": gains, "sgb": sgb,
                        "lnc": lnc, "postg": postg, "ident": ident, "ones": ones})
    return in_maps


FUSED = True


def kernel(x, pre_norm_g, post_norm_g, w_in, na_rpb, sg_ln_g, sg_ln_b, sg_w, sg_b, w_proj_a, w_proj_b, w_out):
    f = lambda a: np.ascontiguousarray(np.asarray(a, dtype=np.float32))
    x = f(x)
    layers = []
    for l in range(2):
        layers.append(tuple(f(a[l]) for a in (pre_norm_g, post_norm_g, w_in, na_rpb, sg_ln_g, sg_ln_b, sg_w, sg_b,
                                               w_proj_a, w_proj_b, w_out)))
    if FUSED:
        nc = _get_nc(2, 80)
        in_maps = _core_inputs(x, layers, 80)
        res = run_bass_kernel_spmd(nc, in_maps, core_ids=list(range(8)))
        out = np.zeros((2, 256, 64, D), dtype=np.float32)
        for c in range(8):
            bi, p = c // 4, c % 4
            out[bi, 64 * p:64 * p + 64] = res.results[c]["y"].reshape(64, 64, D)
        return out.reshape(2, 16384, D)
    nc = _get_nc(1, 72)
    cur = x
    for l in range(2):
        in_maps = _core_inputs(cur, [layers[l]], 72)
        res = run_bass_kernel_spmd(nc, in_maps, core_ids=list(range(8)))
        out = np.zeros((2, 256, 64, D), dtype=np.float32)
        for c in range(8):
            bi, p = c // 4, c % 4
            out[bi, 64 * p:64 * p + 64] = res.results[c]["y"].reshape(64, 64, D)
        cur = out.reshape(2, 16384, D)
    return cur
```

```python
from contextlib import ExitStack
import numpy as np
import ml_dtypes
import concourse.bass as bass
import concourse.mybir as mybir
from concourse.bass_utils import run_bass_kernel_spmd

F32 = mybir.dt.float32
BF16 = mybir.dt.bfloat16
AF = mybir.ActivationFunctionType
ALU = mybir.AluOpType
AX = mybir.AxisListType

ENGS = ("pe", "act", "dve", "pool", "sp")

D = 2048
NIN = 11264
NS = 79
SLAB = 4096
QSCALE = 128 ** -0.5
EXPSILU = True
TAILQ = "pool"
NEG = -30000.0


class Tok:
    __slots__ = ("name", "w", "r")

    def __init__(self, name):
        self.name = name
        self.w = None
        self.r = []


class Op:
    __slots__ = ("eng", "fn", "idx", "deps", "signal", "dma_key", "cnt")


class Prog:
    def __init__(self, nc):
        self.nc = nc
        self.ops = {e: [] for e in ENGS}
        self.dma_counts = {}

    def op(self, eng, fn, reads=(), writes=(), dma_key=None):
        o = Op()
        o.eng = eng
        o.fn = fn
        o.idx = len(self.ops[eng])
        o.signal = False
        o.dma_key = dma_key
        o.cnt = 0
        if dma_key is not None:
            c = self.dma_counts.get(dma_key, 0) + 16
            self.dma_counts[dma_key] = c
            o.cnt = c
        deps = {}

        def add(d, kind):
            if d is None or d is o:
                return
            if d.dma_key is None and d.eng == eng:
                if eng == "pe":
                    return
                if kind != "raw":
                    return
                if o.idx - d.idx > 8:
                    return
            deps[id(d)] = d

        for t in reads:
            add(t.w, "raw")
        for t in writes:
            add(t.w, "waw")
            for r in t.r:
                add(r, "war")
        for t in reads:
            t.r.append(o)
        for t in writes:
            t.w = o
            t.r = []
        o.deps = list(deps.values())
        for d in o.deps:
            if d.dma_key is None:
                d.signal = True
        self.ops[eng].append(o)
        return o

    def emit(self, final_dma_ops=()):
        nc = self.nc
        with ExitStack() as es:
            sems = {e: es.enter_context(nc.semaphore(f"s_{e}")) for e in ENGS}
            dsems = {k: es.enter_context(nc.semaphore(f"d_{k}")) for k in self.dma_counts}
            for e in ENGS:
                c = 0
                for o in self.ops[e]:
                    if o.dma_key is None and o.signal:
                        c += 1
                        o.cnt = c
            block = es.enter_context(nc.Block())

            def run(e, h):
                seen = {}
                for o in self.ops[e]:
                    need = {}
                    for d in o.deps:
                        s = dsems[d.dma_key] if d.dma_key is not None else sems[d.eng]
                        k = id(s)
                        if need.get(k, (None, 0))[1] < d.cnt:
                            need[k] = (s, d.cnt)
                    for k, (s, v) in need.items():
                        if seen.get(k, 0) < v:
                            h.wait_ge(s, v)
                            seen[k] = v
                    ins = o.fn(h)
                    if o.dma_key is not None:
                        ins.then_inc(dsems[o.dma_key], 16)
                    elif o.signal:
                        ins.then_inc(sems[e], 1)
                if e == "sp":
                    for o in final_dma_ops:
                        h.wait_ge(dsems[o.dma_key], o.cnt)

            @block.tensor
            def _(h):
                run("pe", h)

            @block.scalar
            def _(h):
                run("act", h)

            @block.vector
            def _(h):
                run("dve", h)

            @block.gpsimd
            def _(h):
                run("pool", h)

            @block.sync
            def _(h):
                run("sp", h)


NSW = 71


def slab_used(s):
    if s < 28:
        return 4096
    if s < 60:
        return 3072
    if s < 68:
        return 4096
    if s == 68:
        return 4096
    if s == 69:
        return 1024
    if s == 70:
        return 1024
    return 3072


def slab_gain(s):
    if s < 4:
        return 0
    if s < 12:
        return 1 + (s % 2)
    if s < 28:
        return 0
    if s < 60:
        return 3
    return None


def build_nc(n_layers, NR0, debug=False, dmacast=()):
    nc = bass.Bass("TRN2", target_bir_lowering=False)
    dbg_ops = []

    def din(name, shape, dt=F32):
        return nc.dram_tensor(name, shape, dt, kind="ExternalInput").ap()

    xin = din("xin", [NR0 * 64, D])
    wsl = din("wsl", [n_layers, NSW, 128, SLAB])
    spt = din("spt", [n_layers, 8, 128, 3072])
    gains = din("gains", [n_layers, 4, 128, SLAB])
    sgb = din("sgb", [n_layers, 128, 1024])
    lnc = din("lnc", [n_layers, 128, 32])
    postg = din("postg", [n_layers, 128, D])
    ident_d = din("ident", [128, 128], BF16)
    ones_d = din("ones", [128, 128], BF16)
    NRL = NR0 - 8 * n_layers
    y = nc.dram_tensor("y", [NRL * 64, D], F32, kind="ExternalOutput").ap()
    wq = nc.dram_tensor("wq", [n_layers, NS, 128, SLAB], BF16, kind="Internal").ap()
    x1s = None
    if n_layers == 2:
        x1s = nc.dram_tensor("x1s", [(NR0 - 8) * 64, D], F32, kind="Internal").ap()

    with ExitStack() as es:
        off = [0]

        def alloc(nbytes):
            o = off[0]
            off[0] += (nbytes + 63) // 64 * 64
            return o

        o_hT = [alloc(16384), alloc(16384)]
        o_K = alloc(24576)
        o_V = alloc(24576)
        o_slab = [alloc(8192) for _ in range(3)]
        o_xn = alloc(8192)
        o_xr = alloc(8192)
        o_xs = alloc(4096)
        o_qT = [alloc(1024), alloc(1024)]
        o_sza = [alloc(2048), alloc(2048)]
        o_gt = alloc(2048)
        o_szb = alloc(2048)
        o_vn = alloc(8192)
        o_yb = alloc(8192)
        o_pT = [alloc(1536), alloc(1536)]
        o_mg = alloc(16384)
        o_msg = [alloc(2048), alloc(2048)]
        o_rden = alloc(2048)
        o_at = alloc(2048)
        o_bg = alloc(10240)
        o_sp = [alloc(1536), alloc(1536)]
        o_wst = alloc(2048)
        o_tt = alloc(4096)
        o_lnc = alloc(128)
        o_pg = alloc(8192)
        o_id = alloc(256)
        o_on = alloc(256)
        o_st = alloc(1024)
        o_junk = alloc(1024)
        TOTAL = off[0]
        assert TOTAL <= 212000, TOTAL
        A = es.enter_context(nc.sbuf_tensor("arena", [128, TOTAL // 2], BF16))
        PS = es.enter_context(nc.psum_tensor("ps", [128, 4096], F32))

        def bf(o, n):
            return A[:, o // 2:o // 2 + n]

        def f32(o, n):
            return A[:, o // 2:o // 2 + 2 * n].bitcast(F32)

        def bank(b, n=512, c0=0):
            return PS[:, b * 512 + c0:b * 512 + c0 + n]

        def bank_bf(b, n):
            return PS[:, b * 512:b * 512 + (n + 1) // 2].bitcast(BF16)

        hT = [bf(o, 8192).rearrange("p (k t) -> p k t", k=16) for o in o_hT]
        Kr = bf(o_K, 12288).rearrange("p (h t) -> p h t", h=8)
        Vr = bf(o_V, 12288).rearrange("p (s c) -> p s c", s=12)
        slabs = [bf(o, 4096) for o in o_slab]
        xn = f32(o_xn, 2048)
        xr = f32(o_xr, 2048)
        xs = bf(o_xs, 2048)
        qT = [bf(o, 512) for o in o_qT]
        sza = [f32(o, 512) for o in o_sza]
        gt = f32(o_gt, 512)
        szb = f32(o_szb, 512)
        vn = bf(o_vn, 4096).rearrange("p (t c) -> p t c", t=4)
        yaT = bf(o_vn, 4096).rearrange("p (h t) -> p h t", h=8)
        ybT = bf(o_yb, 4096).rearrange("p (h t) -> p h t", h=8)
        pT = [bf(o, 768) for o in o_pT]
        mgT = bf(o_mg, 8192).rearrange("p (k t) -> p k t", k=16)
        njunk = bf(o_mg, 2048)
        msg = [f32(o, 512) for o in o_msg]
        rden = f32(o_rden, 512)
        atb = f32(o_at, 512)
        bgen = bf(o_bg, 5120).rearrange("p (i h q) -> p i h q", i=5, h=8)
        spb = [bf(o, 768).rearrange("p (i q) -> p i q", i=6) for o in o_sp]
        wst = bf(o_wst, 1024).rearrange("p (g s) -> p g s", g=8)
        ttab = f32(o_tt, 1024).rearrange("p (g s) -> p g s", g=8)
        lncb = f32(o_lnc, 32)
        pgb = f32(o_pg, 2048)
        ident = bf(o_id, 128)
        ones = bf(o_on, 128)
        st = f32(o_st, 256)
        ojunk = bf(o_junk, 512)

        P = Prog(nc)
        T = Tok

        def dump(name, ap, n, dt, toks):
            if not debug:
                return
            d = nc.dram_tensor("dbg_" + name, [128, n], dt, kind="ExternalOutput").ap()
            dbg_ops.append(P.op("sp", lambda h: h.dma_start(out=d, in_=ap), reads=toks, dma_key="dbg"))

        t_hT = [T("hT0"), T("hT1")]
        t_hTx = [[T("hT0a"), T("hT0b")], [T("hT1a"), T("hT1b")]]
        t_K, t_V = T("K"), T("V")
        t_slab = [T(f"slab{i}") for i in range(3)]
        t_xn, t_xr, t_xs = T("xn"), T("xr"), T("xs")
        t_qT = [T("q0"), T("q1")]
        t_sza = [T("sza0"), T("sza1")]
        t_gt, t_szb = T("gt"), T("szb")
        t_vn, t_yb = T("vn"), T("yb")
        t_pT = [T("pT0"), T("pT1")]
        t_mg = T("mg")
        t_msg = [T("msg0"), T("msg1")]
        t_rden, t_at = T("rden"), T("at")
        t_bg = T("bg")
        t_sp = [T("sp0"), T("sp1")]
        t_wst, t_tt, t_lnc, t_pg = T("wst"), T("tt"), T("lnc"), T("pg")
        t_const = T("const")
        t_nst, t_lst, t_ost = T("nst"), T("lst"), T("ost")
        t_junk = T("junk")
        t_bank = [T(f"bank{i}") for i in range(8)]
        t_x1c = [T(f"x1c{i}") for i in range((NR0 - 8) * 64 // 128)]
        t_xrh = [T("xrh0"), T("xrh1")]
        t_lsa, t_lsd = T("lsa"), T("lsd")
        main_toks = (t_hT + t_hTx[0] + t_hTx[1] + [t_K, t_V] + t_slab + [t_xn, t_xr, t_xs] + t_qT + t_sza + [t_gt, t_szb, t_vn, t_yb]
                     + t_pT + [t_mg] + t_msg + [t_rden, t_at, t_bg] + t_sp + [t_wst, t_tt, t_lnc, t_pg, t_const,
                                                                           t_nst, t_lst, t_ost, t_junk, t_lsa, t_lsd] + t_xrh)

        NST = st[:, 0:16]
        LST = st[:, 16:64]
        OST = st[:, 64:96]
        EPS_RMS = st[:, 96:97]
        EPS_LN = st[:, 97:98]

        NB = 2
        pci = [f32(16384 * i, 4096) for i in range(NB)]
        pco = [bf(49152 + 8192 * i, 4096) for i in range(NB)]
        gti = [[f32(73728 + 16384 * (l * 4 + k), 4096) for k in range(4)] for l in range(n_layers)]
        assert 73728 + 16384 * 4 * n_layers <= TOTAL
        t_pci = [T(f"pci{i}") for i in range(NB)]
        t_pco = [T(f"pco{i}") for i in range(NB)]
        t_gain = [T(f"gain{k}") for k in range(4 * n_layers)]
        t_wq = [[T(f"wq{l}_{s}") for s in range(NS)] for l in range(n_layers)]
        jobs = [(l, s) for l in range(n_layers) if l not in dmacast for s in range(NS)]
        for l in range(n_layers):
            if l in dmacast:
                continue
            for k in range(4):
                P.op("sp", lambda h, l=l, k=k: h.dma_start(out=gti[l][k], in_=gains[l, k]), writes=[t_gain[l * 4 + k]],
                     dma_key=f"gain{l * 4 + k}")

        def pc_load(q):
            l, s = jobs[q]
            n = slab_used(s)
            i = q % NB
            srcap = wsl[l, s, :, 0:n] if s < NSW else spt[l, s - NSW]
            P.op("sp", lambda h: h.dma_start(out=pci[i][:, 0:n], in_=srcap), writes=[t_pci[i]], dma_key=f"pci{i}")

        def pc_cast(q):
            l, s = jobs[q]
            n = slab_used(s)
            i = q % NB
            g = slab_gain(s)
            if g is None:
                P.op("act", lambda h: h.activation(out=pco[i][:, 0:n], in_=pci[i][:, 0:n], func=AF.Copy),
                     reads=[t_pci[i]], writes=[t_pco[i]])
            else:
                eng = "dve"
                P.op(eng, lambda h: h.tensor_tensor(out=pco[i][:, 0:n], in0=pci[i][:, 0:n], in1=gti[l][g][:, 0:n], op=ALU.mult),
                     reads=[t_pci[i], t_gain[l * 4 + g]], writes=[t_pco[i]])

        def pc_store(q):
            l, s = jobs[q]
            n = slab_used(s)
            i = q % NB
            P.op("sp", lambda h: h.dma_start(out=wq[l, s, :, 0:n], in_=pco[i][:, 0:n]),
                 reads=[t_pco[i]], writes=[t_wq[l][s]], dma_key=f"pco{i}")

        NCK = 8
        t_cchain = [T(f"cch{k}") for k in range(NCK)]
        cast_q = [(l, s_) for l in range(n_layers) if l in dmacast for s_ in range(NS)]
        cast_pos = [0]

        def emit_casts(n):
            while n > 0 and cast_pos[0] < len(cast_q):
                i = cast_pos[0]
                l, s_ = cast_q[i]
                nn = slab_used(s_)
                srcap = wsl[l, s_, :, 0:nn] if s_ < NSW else spt[l, s_ - NSW]
                P.op("pool", lambda h, l=l, s_=s_, nn=nn, srcap=srcap: h.dma_start(out=wq[l, s_, :, 0:nn], in_=srcap),
                     writes=[t_wq[l][s_], t_cchain[i % NCK]], dma_key=f"cc{i % NCK}")
                cast_pos[0] += 1
                n -= 1

        if jobs:
            pc_load(0)
        for q in range(len(jobs)):
            if q + 1 < len(jobs):
                pc_load(q + 1)
            pc_cast(q)
            pc_store(q)
        P.op("pool", lambda h: h.memset(st[:, 128:129], 0.0), writes=t_pci + t_pco + t_gain + main_toks)
        P.op("pool", lambda h: h.memset(EPS_RMS, 1e-6), writes=[t_const])
        P.op("pool", lambda h: h.memset(EPS_LN, 1e-5), writes=[t_const])
        P.op("sp", lambda h: h.dma_start(out=ident, in_=ident_d), writes=[t_const], dma_key="const")
        P.op("sp", lambda h: h.dma_start(out=ones, in_=ones_d), writes=[t_const], dma_key="const")

        class Stream:
            def __init__(self):
                self.plan = []
                self.issued = 0
                self.pos = 0

            def issue(self, upto):
                while self.issued < min(upto, len(self.plan)):
                    i = self.issued
                    l, s = self.plan[i]
                    n = slab_used(s)
                    b = i % 3
                    P.op("sp", lambda h, l=l, s=s, n=n, b=b: h.dma_start(out=slabs[b][:, 0:n], in_=wq[l, s, :, 0:n]),
                         reads=[t_wq[l][s]], writes=[t_slab[b]], dma_key=f"slab{b}")
                    self.issued += 1

            def get(self, l, s):
                i = self.pos
                assert self.plan[i] == (l, s), (i, self.plan[i], l, s)
                self.issue(i + 3)
                self.pos += 1
                return slabs[i % 3], t_slab[i % 3]

        stream = Stream()

        def layer_plan(l, nfull):
            pl = []
            kv = [(l, s) for s in range(0, 8)]
            pl += kv + kv
            for s_ in range(nfull):
                pl += [(l, s) for s in range(8, 12)]
                pl += [(l, s) for s in range(12, 20)]
                pl += kv
                pl += [(l, s) for s in range(20, 28)]
                pl += [(l, s) for s in range(28, 60)]
                pl += [(l, s) for s in range(60, 68)]
            return pl

        for l in range(n_layers):
            NR = NR0 - 8 * l
            stream.plan += layer_plan(l, (NR - 8) // 8)

        out_stores = []

        def do_layer(l):
            NR = NR0 - 8 * l
            nfull = (NR - 8) // 8
            src = xin if l == 0 else x1s
            dst = y if l == n_layers - 1 else x1s
            def src_tok(r0):
                return [t_x1c[r0 // 128]] if l > 0 else []

            def dst_tok(q0):
                return [t_x1c[q0 // 128]] if l < n_layers - 1 else []

            xrh = [f32(o_xr, 1024), f32(o_xr + 4096, 1024)]
            e0 = 8 - 4 * l if NR0 == 80 else 4
            if NR0 != 80:
                assert n_layers == 1

            def blk(j):
                if j == 0:
                    return 0, 256, 0
                if j == nfull + 1:
                    return NR - 4, 256, (NR - 4) // 2
                return 4 + 8 * (j - 1), 512, 2 + 4 * (j - 1)

            def slot(pi):
                return (pi + 2) % 12

            P.op("sp", lambda h, l=l: h.dma_start(out=bf(o_bg, 4096), in_=wq[l, 68, :, 0:4096]),
                 reads=[t_wq[l][68]], writes=[t_bg], dma_key="lc_bg")
            P.op("sp", lambda h, l=l: h.dma_start(out=bf(o_bg + 8192, 1024), in_=wq[l, 69, :, 0:1024]),
                 reads=[t_wq[l][69]], writes=[t_bg], dma_key="lc_bg")
            P.op("sp", lambda h, l=l: h.dma_start(out=bf(o_wst, 1024), in_=wq[l, 70, :, 0:1024]),
                 reads=[t_wq[l][70]], writes=[t_wst], dma_key="lc_wst")
            P.op("sp", lambda h, l=l: h.dma_start(out=f32(o_tt, 1024), in_=sgb[l]), writes=[t_tt], dma_key="lc_tt")
            P.op("sp", lambda h, l=l: h.dma_start(out=lncb, in_=lnc[l]), writes=[t_lnc], dma_key="lc_lnc")
            P.op("sp", lambda h, l=l: h.dma_start(out=pgb, in_=postg[l]), writes=[t_pg], dma_key="lc_pg")
            for c in range(2):
                P.op("pe", lambda h, c=c: h.matmul(bank(c), lhsT=ones, rhs=bf(o_wst + c * 1024, 512), start=True, stop=True),
                     reads=[t_const, t_wst], writes=[t_bank[c]])
            for g in range(8):
                P.op("dve", lambda h, g=g: h.scalar_tensor_tensor(
                    out=ttab[:, g, :], in0=bank(g // 4, 128, (g % 4) * 128), scalar=lncb[:, 8 + g:9 + g],
                    in1=ttab[:, g, :], op0=ALU.mult, op1=ALU.add),
                    reads=[t_bank[g // 4], t_lnc, t_tt], writes=[t_tt])

            def norm_elem_tile(j, t):
                row0, ntok, _ = blk(j)
                r0 = row0 * 64 + t * 128
                P.op("sp", lambda h: h.dma_start(out=xn, in_=src[r0:r0 + 128, :]),
                     reads=src_tok(r0), writes=[t_xn], dma_key="xn")
                P.op("act", lambda h: h.activation(out=xs, in_=xn, func=AF.Square, accum_out=NST[:, t:t + 1]),
                     reads=[t_xn], writes=[t_xs, t_nst])
                P.op("act", lambda h: h.activation(out=NST[:, 4 + t:5 + t], in_=NST[:, t:t + 1], func=AF.Sqrt,
                                                   scale=1.0 / D, bias=EPS_RMS),
                     reads=[t_nst, t_const], writes=[t_nst])
                P.op("dve", lambda h: h.reciprocal(out=NST[:, 8 + t:9 + t], in_=NST[:, 4 + t:5 + t]),
                     reads=[t_nst], writes=[t_nst])
                P.op("act", lambda h: h.activation(out=xs, in_=xn, func=AF.Copy, scale=NST[:, 8 + t:9 + t]),
                     reads=[t_xn, t_nst], writes=[t_xs])

            def norm_trans_tile(j, t):
                par = j % 2
                for kc in range(16):
                    P.op("pe", lambda h, kc=kc: h.transpose(out=bank_bf(6, 2048)[:, kc * 128:(kc + 1) * 128],
                                                            in_=xs[:, kc * 128:(kc + 1) * 128], identity=ident),
                         reads=[t_xs, t_const], writes=[t_bank[6], t_bank[7]])
                if l in dmacast:
                    for kc in range(16):
                        P.op("dve", lambda h, kc=kc: h.tensor_scalar(
                            out=hT[par][:, kc, t * 128:(t + 1) * 128], in0=bank_bf(6, 2048)[:, kc * 128:(kc + 1) * 128],
                            scalar1=lncb[:, 16 + kc:17 + kc], scalar2=1.0, op0=ALU.mult, op1=ALU.mult),
                            reads=[t_bank[6], t_bank[7], t_lnc], writes=[t_hT[par]] + t_hTx[par])
                else:
                    P.op("dve", lambda h: h.tensor_copy(
                        out=hT[par][:, :, t * 128:(t + 1) * 128],
                        in_=bank_bf(6, 2048).rearrange("p (k c) -> p k c", c=128)),
                        reads=[t_bank[6], t_bank[7]], writes=[t_hT[par]] + t_hTx[par])

            def norm_elem(j):
                for t in range(blk(j)[1] // 128):
                    norm_elem_tile(j, t)
                    norm_trans_tile(j, t)

            def kv_phase(j):
                row0, ntok, p0 = blk(j)
                nt = ntok // 128
                par = j % 2
                s0 = slot(p0)
                for i in range(4):
                    sl, tsl = stream.get(l, i)
                    for ch in range(2):
                        hd = 2 * i + ch
                        b = (2 * i + ch) % 8
                        for kc in range(16):
                            P.op("pe", lambda h, sl=sl, ch=ch, kc=kc, b=b: h.matmul(
                                bank(b, ntok), lhsT=sl[:, ch * 2048 + kc * 128:ch * 2048 + (kc + 1) * 128],
                                rhs=hT[par][:, kc, 0:ntok], start=(kc == 0), stop=(kc == 15)),
                                reads=[tsl, t_hT[par]], writes=[t_bank[b]])
                        P.op("dve", lambda h, hd=hd, b=b: h.tensor_copy(out=Kr[:, hd, s0 * 128:s0 * 128 + ntok], in_=bank(b, ntok)),
                             reads=[t_bank[b]], writes=[t_K])
                for cg in range(2):
                    for kh in range(2):
                        sl, tsl = stream.get(l, 4 + cg * 2 + kh)
                        for t in range(nt):
                            b = cg * 4 + t
                            for kcl in range(8):
                                kc = kh * 8 + kcl
                                P.op("pe", lambda h, sl=sl, t=t, kc=kc, kcl=kcl, b=b: h.matmul(
                                    bank(b), lhsT=hT[par][:, kc, t * 128:(t + 1) * 128],
                                    rhs=sl[:, kcl * 512:(kcl + 1) * 512], start=(kc == 0), stop=(kc == 15)),
                                    reads=[tsl, t_hT[par]], writes=[t_bank[b]])
                    for t in range(nt):
                        b = cg * 4 + t
                        P.op("act", lambda h, t=t, b=b, cg=cg: h.activation(
                            out=Vr[:, s0 + t, cg * 512:(cg + 1) * 512], in_=bank(b), func=AF.Copy),
                            reads=[t_bank[b]], writes=[t_V])

            def vs_slab(j, cg, kh):
                par = j % 2
                sl, tsl = stream.get(l, 8 + cg * 2 + kh)
                for t in range(4):
                    b = cg * 4 + t
                    for kcl in range(8):
                        kc = kh * 8 + kcl
                        P.op("pe", lambda h, sl=sl, t=t, kc=kc, kcl=kcl, b=b: h.matmul(
                            bank(b), lhsT=hT[par][:, kc, t * 128:(t + 1) * 128],
                            rhs=sl[:, kcl * 512:(kcl + 1) * 512], start=(kc == 0), stop=(kc == 15)),
                            reads=[tsl, t_hT[par]], writes=[t_bank[b]])

            def g_phase_ln(j):
                for cg in range(2):
                    for t in range(4):
                        b = cg * 4 + t
                        c = cg * 4 + t
                        P.op("act", lambda h, b=b, c=c: h.activation(out=ojunk, in_=bank(b), func=AF.Square,
                                                                     accum_out=LST[:, 8 + c:9 + c]),
                             reads=[t_bank[b]], writes=[t_junk, t_lst])
                        P.op("dve", lambda h, b=b, c=c: h.tensor_reduce(out=LST[:, c:c + 1], in_=bank(b), axis=AX.X, op=ALU.add),
                             reads=[t_bank[b]], writes=[t_lst])
                rl = [t_lst]
                P.op("dve", lambda h: h.tensor_tensor(out=LST[:, 16:20], in0=LST[:, 0:4], in1=LST[:, 4:8], op=ALU.add), reads=rl, writes=rl)
                P.op("dve", lambda h: h.tensor_tensor(out=LST[:, 20:24], in0=LST[:, 8:12], in1=LST[:, 12:16], op=ALU.add), reads=rl, writes=rl)
                P.op("dve", lambda h: h.tensor_scalar(out=LST[:, 16:20], in0=LST[:, 16:20], scalar1=1.0 / 1024, scalar2=0.0,
                                                      op0=ALU.mult, op1=ALU.add), reads=rl, writes=rl)
                P.op("dve", lambda h: h.tensor_tensor(out=LST[:, 24:28], in0=LST[:, 16:20], in1=LST[:, 16:20], op=ALU.mult), reads=rl, writes=rl)
                P.op("dve", lambda h: h.scalar_tensor_tensor(out=LST[:, 24:28], in0=LST[:, 20:24], scalar=1.0 / 1024,
                                                             in1=LST[:, 24:28], op0=ALU.mult, op1=ALU.subtract), reads=rl, writes=rl)
                P.op("act", lambda h: h.activation(out=LST[:, 24:28], in_=LST[:, 24:28], func=AF.Sqrt, scale=1.0, bias=EPS_LN),
                     reads=[t_lst, t_const], writes=rl)
                P.op("dve", lambda h: h.reciprocal(out=LST[:, 28:32], in_=LST[:, 24:28]), reads=rl, writes=rl)
                P.op("dve", lambda h: h.scalar_tensor_tensor(out=LST[:, 32:36], in0=LST[:, 16:20], scalar=-1.0,
                                                             in1=LST[:, 28:32], op0=ALU.mult, op1=ALU.mult), reads=rl, writes=rl)
                for cg in range(2):
                    for t in range(4):
                        b = cg * 4 + t
                        P.op("act", lambda h, t=t, b=b, cg=cg: h.activation(
                            out=vn[:, t, cg * 512:(cg + 1) * 512], in_=bank(b), func=AF.Identity,
                            scale=LST[:, 28 + t:29 + t], bias=LST[:, 32 + t:33 + t]),
                            reads=[t_bank[b], t_lst], writes=[t_vn])

            def g_phase_uz(j, g):
                par = j % 2
                sl, tsl = stream.get(l, 12 + g)
                bs = (g % 2) * 3
                bu, bz, bsx = bs, bs + 1, bs + 2
                for which, b in ((0, bu), (1, bz)):
                    for kc in range(16):
                        P.op("pe", lambda h, sl=sl, which=which, kc=kc, b=b: h.matmul(
                            bank(b), lhsT=sl[:, which * 2048 + kc * 128:which * 2048 + (kc + 1) * 128],
                            rhs=hT[par][:, kc, :], start=(kc == 0), stop=(kc == 15)),
                            reads=[tsl, t_hT[par]], writes=[t_bank[b]])
                for t in range(4):
                    P.op("pe", lambda h, t=t, g=g, b=bsx: h.matmul(
                        bank(b, 128, t * 128), lhsT=vn[:, t, g * 128:(g + 1) * 128], rhs=wst[:, g, :], start=True, stop=True),
                        reads=[t_vn, t_wst], writes=[t_bank[bsx]])
                P.op("act", lambda h, b=bz: h.activation(out=szb, in_=bank(b), func=AF.Silu), reads=[t_bank[bz]], writes=[t_szb])
                for t in range(4):
                    P.op("dve", lambda h, t=t, g=g, b=bsx: h.scalar_tensor_tensor(
                        out=gt[:, t * 128:(t + 1) * 128], in0=bank(b, 128, t * 128), scalar=lncb[:, g:g + 1],
                        in1=ttab[:, g, :], op0=ALU.mult, op1=ALU.add),
                        reads=[t_bank[bsx], t_lnc, t_tt], writes=[t_gt])
                P.op("dve", lambda h, b=bu: h.tensor_tensor(out=gt, in0=gt, in1=bank(b), op=ALU.mult),
                     reads=[t_gt, t_bank[bu]], writes=[t_gt])
                P.op("dve", lambda h, g=g: h.tensor_tensor(out=ybT[:, g, :], in0=gt, in1=szb, op=ALU.mult),
                     reads=[t_gt, t_szb], writes=[t_yb])

            def q_proj(j, hd):
                par = j % 2
                sl, tsl = stream.get(l, 20 + hd)
                for which, b in ((0, 0), (1, 1)):
                    for kc in range(16):
                        P.op("pe", lambda h, sl=sl, which=which, kc=kc, b=b: h.matmul(
                            bank(b), lhsT=sl[:, which * 2048 + kc * 128:which * 2048 + (kc + 1) * 128],
                            rhs=hT[par][:, kc, :], start=(kc == 0), stop=(kc == 15)),
                            reads=[tsl, t_hT[par]], writes=[t_bank[b]])
                P.op("dve", lambda h, hd=hd: h.tensor_scalar(out=qT[hd % 2], in0=bank(0), scalar1=QSCALE, scalar2=0.0,
                                                             op0=ALU.mult, op1=ALU.add),
                     reads=[t_bank[0]], writes=[t_qT[hd % 2]])
                if EXPSILU:
                    P.op("act", lambda h, hd=hd: h.activation(out=sza[hd % 2], in_=bank(1), func=AF.Exp, scale=-1.0),
                         reads=[t_bank[1]], writes=[t_sza[hd % 2]])
                    P.op("dve", lambda h, hd=hd: h.tensor_scalar(out=sza[hd % 2], in0=sza[hd % 2], scalar1=1.0, scalar2=0.0,
                                                                 op0=ALU.add, op1=ALU.add),
                         reads=[t_sza[hd % 2]], writes=[t_sza[hd % 2]])
                    P.op("dve", lambda h, hd=hd: h.reciprocal(out=sza[hd % 2], in_=sza[hd % 2]),
                         reads=[t_sza[hd % 2]], writes=[t_sza[hd % 2]])
                    P.op("dve", lambda h, hd=hd: h.tensor_tensor(out=sza[hd % 2], in0=sza[hd % 2], in1=bank(1), op=ALU.mult),
                         reads=[t_bank[1], t_sza[hd % 2]], writes=[t_sza[hd % 2]])
                else:
                    P.op("act", lambda h, hd=hd: h.activation(out=sza[hd % 2], in_=bank(1), func=AF.Silu),
                         reads=[t_bank[1]], writes=[t_sza[hd % 2]])

            sp_ctr = [0]

            def attn(j, hd):
                _, _, p0 = blk(j)
                tiles = []
                for t in range(4):
                    tp = p0 + t
                    row = 2 * tp
                    spi = None
                    if row == e0:
                        spi, offs = 0, list(range(-2, 4))
                    elif row == e0 + 2:
                        spi, offs = 1, list(range(-2, 3))
                    elif row == e0 + 60:
                        spi, offs = 2, list(range(-2, 3))
                    elif row == e0 + 62:
                        spi, offs = 3, list(range(-3, 3))
                    else:
                        offs = list(range(-2, 3))
                    tiles.append((tp, spi, offs))

                def S(t):
                    tp, spi, offs = tiles[t]
                    bx, by = (4, 5) if t % 2 == 0 else (6, 7)
                    tb = None
                    ttb = t_bg
                    if spi is not None:
                        k = sp_ctr[0] % 2
                        sp_ctr[0] += 1
                        sidx = 71 + spi * 2 + hd // 4
                        P.op("sp", lambda h, k=k, sidx=sidx: h.dma_start(
                            out=bf(o_sp[k], 768), in_=wq[l, sidx, :, (hd % 4) * 768:(hd % 4) * 768 + 768]),
                            reads=[t_wq[l][sidx]], writes=[t_sp[k]], dma_key=f"spt{k}")
                        tb = spb[k]
                        ttb = t_sp[k]
                    for i, o in enumerate(offs):
                        pi = tp + o
                        sk = slot(pi)
                        b = bx if i < 4 else by
                        c0 = (i % 4) * 128
                        P.op("pe", lambda h, sk=sk, b=b, c0=c0, t=t: h.matmul(
                            bank(b, 128, c0), lhsT=Kr[:, hd, sk * 128:(sk + 1) * 128], rhs=qT[hd % 2][:, t * 128:(t + 1) * 128],
                            start=True, stop=False),
                            reads=[t_K, t_qT[hd % 2]], writes=[t_bank[b]])
                        if tb is None:
                            rhs = bgen[:, i, hd, :]
                        else:
                            rhs = tb[:, i, :]
                        P.op("pe", lambda h, b=b, c0=c0, rhs=rhs: h.matmul(bank(b, 128, c0), lhsT=ident, rhs=rhs, start=False, stop=True),
                             reads=[t_const, ttb], writes=[t_bank[b]])

                def E(t):
                    tp, spi, offs = tiles[t]
                    bx, by = (4, 5) if t % 2 == 0 else (6, 7)
                    n2 = (len(offs) - 4) * 128
                    P.op("act", lambda h, t=t, bx=bx: h.activation(out=pT[t % 2][:, 0:512], in_=bank(bx), func=AF.Exp),
                         reads=[t_bank[bx]], writes=[t_pT[t % 2]])
                    P.op("act", lambda h, t=t, by=by, n2=n2: h.activation(out=pT[t % 2][:, 512:512 + n2], in_=bank(by, n2), func=AF.Exp),
                         reads=[t_bank[by]], writes=[t_pT[t % 2]])

                def PV(t):
                    tp, spi, offs = tiles[t]
                    n = len(offs)
                    for i, o in enumerate(offs):
                        sk = slot(tp + o)
                        P.op("pe", lambda h, sk=sk, i=i, t=t, n=n: h.matmul(
                            bank(2, 128, t * 128), lhsT=Vr[:, sk, hd * 128:(hd + 1) * 128], rhs=pT[t % 2][:, i * 128:(i + 1) * 128],
                            start=(i == 0), stop=(i == n - 1)),
                            reads=[t_V, t_pT[t % 2]], writes=[t_bank[2]])
                        P.op("pe", lambda h, i=i, t=t, n=n: h.matmul(
                            bank(3, 128, t * 128), lhsT=ones, rhs=pT[t % 2][:, i * 128:(i + 1) * 128],
                            start=(i == 0), stop=(i == n - 1)),
                            reads=[t_const, t_pT[t % 2]], writes=[t_bank[3]])

                S(0)
                for t in range(4):
                    if t + 1 < 4:
                        S(t + 1)
                    E(t)
                    PV(t)
                P.op("dve", lambda h: h.reciprocal(out=rden, in_=bank(3)), reads=[t_bank[3]], writes=[t_rden])
                P.op("dve", lambda h: h.tensor_tensor(out=atb, in0=bank(2), in1=rden, op=ALU.mult),
                     reads=[t_bank[2], t_rden], writes=[t_at])
                P.op("pool", lambda h: h.tensor_tensor(out=yaT[:, hd, :], in0=atb, in1=sza[hd % 2], op=ALU.mult),
                     reads=[t_at, t_sza[hd % 2]], writes=[t_vn])

            def m_phase(j):
                par = j % 2
                for jj in range(16):
                    bs = (jj % 2) * 4
                    for half in range(2):
                        sl, tsl = stream.get(l, 28 + 2 * jj + half)
                        bg_, bp_ = bs + 2 * half, bs + 2 * half + 1
                        for kc in range(16):
                            P.op("pe", lambda h, sl=sl, kc=kc, b=bg_: h.matmul(
                                bank(b), lhsT=sl[:, kc * 128:(kc + 1) * 128], rhs=hT[par][:, kc, :],
                                start=(kc == 0), stop=(kc == 15)),
                                reads=[tsl, t_hT[par]], writes=[t_bank[bg_]])
                        yT = yaT if half == 0 else ybT
                        ty = t_vn if half == 0 else t_yb
                        for c in range(8):
                            P.op("pe", lambda h, sl=sl, c=c, b=bp_, yT=yT: h.matmul(
                                bank(b), lhsT=sl[:, 2048 + c * 128:2048 + (c + 1) * 128], rhs=yT[:, c, :],
                                start=(c == 0), stop=(c == 7)),
                                reads=[tsl, ty], writes=[t_bank[bp_]])
                        P.op("act", lambda h, half=half, b=bg_: h.activation(out=msg[half], in_=bank(b), func=AF.Sigmoid),
                             reads=[t_bank[bg_]], writes=[t_msg[half]])
                        P.op("dve", lambda h, half=half, b=bp_: h.tensor_tensor(out=msg[half], in0=msg[half], in1=bank(b), op=ALU.mult),
                             reads=[t_msg[half], t_bank[bp_]], writes=[t_msg[half]])
                    P.op("pool", lambda h, jj=jj: h.tensor_tensor(out=mgT[:, jj, :], in0=msg[0], in1=msg[1], op=ALU.add),
                         reads=t_msg, writes=[t_mg])

            def o_phase(j, early=None):
                row0, _, _ = blk(j)
                par = j % 2
                ob = [f32(o_msg[0], 2048), f32(o_yb, 2048), f32(o_hT[par], 2048), f32(o_hT[par] + 8192, 2048)]
                t_ob = [[t_msg[0], t_msg[1], t_rden, t_at], [t_yb], [t_hTx[par][0]], [t_hTx[par][1]]]
                for cg in range(4):
                    bs = (cg % 2) * 4
                    for kh in range(2):
                        sl, tsl = stream.get(l, 60 + cg * 2 + kh)
                        for t in range(4):
                            b = bs + t
                            for kcl in range(8):
                                kc = kh * 8 + kcl
                                P.op("pe", lambda h, sl=sl, t=t, kc=kc, kcl=kcl, b=b: h.matmul(
                                    bank(b), lhsT=mgT[:, kc, t * 128:(t + 1) * 128], rhs=sl[:, kcl * 512:(kcl + 1) * 512],
                                    start=(kc == 0), stop=(kc == 15)),
                                    reads=[tsl, t_mg], writes=[t_bank[b]])
                    for t in range(4):
                        b = bs + t
                        wt = t_ob[t] + ([t_hT[par]] if (cg == 0 and t >= 2) else [])
                        P.op("act", lambda h, t=t, b=b, cg=cg: h.activation(out=ob[t][:, cg * 512:(cg + 1) * 512], in_=bank(b), func=AF.Copy),
                             reads=[t_bank[b]], writes=wt)
                        P.op("act", lambda h, t=t, cg=cg: h.activation(out=ojunk, in_=ob[t][:, cg * 512:(cg + 1) * 512], func=AF.Square,
                                                                       accum_out=OST[:, t * 4 + cg:t * 4 + cg + 1]),
                             reads=t_ob[t], writes=[t_junk, t_ost])
                if early is not None:
                    early()
                ro = [t_ost]
                P.op("dve", lambda h: h.tensor_reduce(out=OST[:, 16:20], in_=OST[:, 0:16].rearrange("p (t c) -> p t c", c=4),
                                                      axis=AX.X, op=ALU.add), reads=ro, writes=ro)
                P.op("act", lambda h: h.activation(out=OST[:, 20:24], in_=OST[:, 16:20], func=AF.Sqrt, scale=1.0 / D, bias=EPS_RMS),
                     reads=[t_ost, t_const], writes=ro)
                P.op("dve", lambda h: h.reciprocal(out=OST[:, 24:28], in_=OST[:, 20:24]), reads=ro, writes=ro)
                for t in (1, 2, 3, 0):
                    r0 = row0 * 64 + t * 128
                    q0 = (row0 - 4) * 64 + t * 128
                    P.op(TAILQ, lambda h, r0=r0: h.dma_start(out=xr, in_=src[r0:r0 + 128, :]),
                         reads=src_tok(r0), writes=t_xrh, dma_key="xr")
                    P.op("dve", lambda h, t=t: h.scalar_tensor_tensor(out=ob[t], in0=ob[t], scalar=OST[:, 24 + t:25 + t], in1=pgb,
                                                                      op0=ALU.mult, op1=ALU.mult),
                         reads=t_ob[t] + [t_ost, t_pg], writes=t_ob[t])
                    for hf in range(2):
                        P.op("pool", lambda h, t=t, hf=hf: h.tensor_tensor(out=ob[t][:, hf * 1024:(hf + 1) * 1024],
                                                                          in0=ob[t][:, hf * 1024:(hf + 1) * 1024], in1=xrh[hf], op=ALU.add),
                             reads=t_ob[t] + [t_xrh[hf]], writes=t_ob[t])
                    o = P.op(TAILQ, lambda h, t=t, q0=q0: h.dma_start(out=dst[q0:q0 + 128, :], in_=ob[t]),
                             reads=t_ob[t], writes=dst_tok(q0), dma_key=f"out{t}")
                    if l == n_layers - 1:
                        out_stores.append(o)

            if l in dmacast:
                emit_casts(10 ** 6)
            norm_elem(0)
            kv_phase(0)
            norm_elem(1)
            kv_phase(1)
            pre = False
            for s_ in range(nfull):
                j = s_ + 1
                ntn = blk(j + 1)[1] // 128
                if (l + 1) in dmacast:
                    emit_casts(12)
                if not pre:
                    norm_elem_tile(j + 1, 0)
                vs_slab(j, 0, 0)
                vs_slab(j, 0, 1)
                vs_slab(j, 1, 0)
                vs_slab(j, 1, 1)
                if j == 1 and l == 0:
                    dump("hT", bf(o_hT[1], 8192), 8192, BF16, [t_hT[1]])
                g_phase_ln(j)
                if j == 1 and l == 0:
                    dump("vn", bf(o_vn, 4096), 4096, BF16, [t_vn])
                tsched = {2: 0, 4: 1, 6: 2, 7: 3}
                for g in range(8):
                    g_phase_uz(j, g)
                    k = tsched.get(g)
                    if k is not None and k < ntn:
                        norm_trans_tile(j + 1, k)
                        if k + 1 < ntn:
                            norm_elem_tile(j + 1, k + 1)
                if j == 1 and l == 0:
                    dump("yb", bf(o_yb, 4096), 4096, BF16, [t_yb])
                kv_phase(j + 1)
                if j == 1 and l == 0:
                    dump("K", bf(o_K, 12288), 12288, BF16, [t_K])
                    dump("V", bf(o_V, 12288), 12288, BF16, [t_V])
                q_proj(j, 0)
                for hd in range(8):
                    if hd + 1 < 8:
                        q_proj(j, hd + 1)
                    attn(j, hd)
                if j == 1 and l == 0:
                    dump("ya", bf(o_vn, 4096), 4096, BF16, [t_vn])
                m_phase(j)
                if j == 1 and l == 0:
                    dump("mg", bf(o_mg, 8192), 8192, BF16, [t_mg])
                pre = (j + 1 <= nfull)
                o_phase(j, early=(lambda j=j: norm_elem_tile(j + 2, 0)) if pre else None)

        for l in range(n_layers):
            do_layer(l)

        assert stream.pos == len(stream.plan), (stream.pos, len(stream.plan))
        last = {}
        for o in out_stores:
            last[o.dma_key] = o
        P.emit(final_dma_ops=list(last.values()) + dbg_ops[-1:])
    return nc


def _a_chunk(W3, col0):
    return W3[:, :, col0:col0 + 128].transpose(1, 0, 2).reshape(128, -1)


def _b_slab(W, col0, kh):
    Wk = W.reshape(-1, 128, W.shape[1])
    return Wk[kh * 8:(kh + 1) * 8, :, col0:col0 + 512].transpose(1, 0, 2).reshape(128, 4096)


def _pair_table(rpb, qrow_g, offs, rows_total, generic):
    kc = np.arange(64)
    qc = np.arange(64)
    cs = np.clip(qc - 8, 0, 48)
    colvalid = (kc[:, None] >= cs[None, :]) & (kc[:, None] < cs[None, :] + 16)
    dc = kc[:, None] - qc[None, :] + 15
    dcc = np.clip(dc, 0, 30)
    out = np.full((2, 64, len(offs), 8, 2, 64), NEG, dtype=np.float32)
    for i, o in enumerate(offs):
        for kr in range(2):
            for qr in range(2):
                krow = qrow_g + 2 * o + kr
                qrow = qrow_g + qr
                if generic:
                    rs = qrow - 4
                else:
                    rs = min(max(qrow - 4, 0), rows_total - 8)
                    if qrow < 0 or qrow >= rows_total:
                        rs = qrow - 4
                dr = krow - qrow + 7
                if krow < rs or krow >= rs + 8 or dr < 0 or dr > 14:
                    continue
                vals = rpb[:, dr, :][:, dcc]
                sel = np.where(colvalid[None], vals, np.float32(NEG))
                out[kr, :, i, :, qr, :] = sel.transpose(1, 0, 2)
    return out.reshape(128, len(offs), 8, 128)


def _layer_arrays(lw, wsl):
    (pre_g, post_g, w_in, rpb, ln_g, ln_b, sg_w, sg_b, w_pa, w_pb, w_out) = lw
    W3 = w_in.reshape(16, 128, NIN)
    for i in range(4):
        wsl[i, :, 0:2048] = _a_chunk(W3, 1024 + (2 * i) * 128)
        wsl[i, :, 2048:4096] = _a_chunk(W3, 1024 + (2 * i + 1) * 128)
    for cg in range(2):
        for kh in range(2):
            wsl[4 + cg * 2 + kh] = _b_slab(w_in, 2048 + cg * 512, kh)
            wsl[8 + cg * 2 + kh] = _b_slab(w_in, 5120 + cg * 512, kh)
    for g in range(8):
        wsl[12 + g, :, 0:2048] = _a_chunk(W3, 4096 + g * 128)
        wsl[12 + g, :, 2048:4096] = _a_chunk(W3, 6144 + g * 128)
        wsl[20 + g, :, 0:2048] = _a_chunk(W3, g * 128)
        wsl[20 + g, :, 2048:4096] = _a_chunk(W3, 3072 + g * 128)
    PA3 = w_pa.reshape(8, 128, D)
    PB3 = w_pb.reshape(8, 128, D)
    for j in range(16):
        wsl[28 + 2 * j, :, 0:2048] = _a_chunk(W3, 7168 + j * 128)
        wsl[28 + 2 * j, :, 2048:3072] = _a_chunk(PA3, j * 128)
        wsl[29 + 2 * j, :, 0:2048] = _a_chunk(W3, 9216 + j * 128)
        wsl[29 + 2 * j, :, 2048:3072] = _a_chunk(PB3, j * 128)
    for cg in range(4):
        for kh in range(2):
            wsl[60 + cg * 2 + kh] = _b_slab(w_out, cg * 512, kh)
    return wsl


def _gains(pre_g):
    gc = pre_g.reshape(16, 128).T
    e = np.arange(SLAB)
    GA = gc[:, (e // 128) % 16]
    GB0 = gc[:, e // 512]
    GB1 = gc[:, 8 + e // 512]
    GM = np.where(e[None, :] < 2048, GA, np.float32(1.0))
    return np.stack([GA, GB0, GB1, GM]).astype(np.float32)


_NC_CACHE = {}


def _get_nc(n_layers, NR0, dmacast=()):
    key = (n_layers, NR0, tuple(dmacast))
    if key not in _NC_CACHE:
        _NC_CACHE[key] = build_nc(n_layers, NR0, dmacast=tuple(dmacast))
    return _NC_CACHE[key]


def _core_inputs(x, layers, NR0):
    nl = len(layers)
    ident = np.eye(128, dtype=np.float32).astype(ml_dtypes.bfloat16)
    ones = np.ones((128, 128), dtype=np.float32).astype(ml_dtypes.bfloat16)
    halo = (NR0 - 64) // 2
    xg = x.reshape(2, 256, 64, D)
    wsl = np.zeros((nl, NSW, 128, SLAB), dtype=np.float32)
    gains = np.zeros((nl, 4, 128, SLAB), dtype=np.float32)
    sgb = np.zeros((nl, 128, 1024), dtype=np.float32)
    lnc = np.zeros((nl, 128, 32), dtype=np.float32)
    postg = np.zeros((nl, 128, D), dtype=np.float32)
    for li, lw in enumerate(layers):
        (pre_g, post_g, w_in, rpb, ln_g, ln_b, sg_w, sg_b, w_pa, w_pb, w_out) = lw
        _layer_arrays(lw, wsl[li])
        gains[li] = _gains(pre_g)
        sgb[li] = np.broadcast_to(sg_b.reshape(1, 1024), (128, 1024))
        lnc[li, :, 0:8] = ln_g.reshape(8, 128).T
        lnc[li, :, 8:16] = ln_b.reshape(8, 128).T
        lnc[li, :, 16:32] = pre_g.reshape(16, 128).T
        postg[li] = np.broadcast_to(post_g.reshape(1, D), (128, D))
        wsl[li, 70, :, 0:1024] = sg_w.transpose(2, 0, 1).reshape(128, 1024)
        g5 = _pair_table(rpb, 0, list(range(-2, 3)), 256, True).reshape(128, 5120)
        wsl[li, 68] = g5[:, 0:4096]
        wsl[li, 69, :, 0:1024] = g5[:, 4096:5120]
    in_maps = []
    for c in range(8):
        bi, p = c // 4, c % 4
        R0 = 64 * p
        xl = np.zeros((NR0, 64, D), dtype=np.float32)
        lo, hi = R0 - halo, R0 + 64 + halo
        slo, shi = max(lo, 0), min(hi, 256)
        xl[slo - lo:shi - lo] = xg[bi, slo:shi]
        spt = np.zeros((nl, 8, 128, 3072), dtype=np.float32)
        for li, lw in enumerate(layers):
            rpb = lw[3]
            specs = [(R0, list(range(-2, 4))), (R0 + 2, list(range(-2, 3))),
                     (R0 + 60, list(range(-2, 3))), (R0 + 62, list(range(-3, 3)))]
            for spi, (qrow, offs) in enumerate(specs):
                tb = _pair_table(rpb, qrow, offs, 256, False)
                full = np.full((128, 6, 8, 128), NEG, dtype=np.float32)
                full[:, 0:len(offs)] = tb
                hm = full.transpose(0, 2, 1, 3)
                spt[li, spi * 2 + 0] = hm[:, 0:4].reshape(128, 3072)
                spt[li, spi * 2 + 1] = hm[:, 4:8].reshape(128, 3072)
        in_maps.append({"xin": xl.reshape(NR0 * 64, D), "wsl": wsl, "spt": spt, "gains": gains, "sgb": sgb,
                        "lnc": lnc, "postg": postg, "ident": ident, "ones": ones})
    return in_maps


FUSED = True
DMACAST = (1,)


def kernel(x, pre_norm_g, post_norm_g, w_in, na_rpb, sg_ln_g, sg_ln_b, sg_w, sg_b, w_proj_a, w_proj_b, w_out):
    f = lambda a: np.ascontiguousarray(np.asarray(a, dtype=np.float32))
    x = f(x)
    layers = []
    for l in range(2):
        layers.append(tuple(f(a[l]) for a in (pre_norm_g, post_norm_g, w_in, na_rpb, sg_ln_g, sg_ln_b, sg_w, sg_b,
                                               w_proj_a, w_proj_b, w_out)))
    if FUSED:
        nc = _get_nc(2, 80, DMACAST)
        in_maps = _core_inputs(x, layers, 80)
        res = run_bass_kernel_spmd(nc, in_maps, core_ids=list(range(8)))
        out = np.zeros((2, 256, 64, D), dtype=np.float32)
        for c in range(8):
            bi, p = c // 4, c % 4
            out[bi, 64 * p:64 * p + 64] = res.results[c]["y"].reshape(64, 64, D)
        return out.reshape(2, 16384, D)
    nc = _get_nc(1, 72)
    cur = x
    for l in range(2):
        in_maps = _core_inputs(cur, [layers[l]], 72)
        res = run_bass_kernel_spmd(nc, in_maps, core_ids=list(range(8)))
        out = np.zeros((2, 256, 64, D), dtype=np.float32)
        for c in range(8):
            bi, p = c // 4, c % 4
            out[bi, 64 * p:64 * p + 64] = res.results[c]["y"].reshape(64, 64, D)
        cur = out.reshape(2, 16384, D)
    return cur
```

```python
from contextlib import ExitStack
import numpy as np
import ml_dtypes
import concourse.bass as bass
import concourse.mybir as mybir
from concourse.bass_utils import run_bass_kernel_spmd

F32 = mybir.dt.float32
BF16 = mybir.dt.bfloat16
AF = mybir.ActivationFunctionType
ALU = mybir.AluOpType
AX = mybir.AxisListType

ENGS = ("pe", "act", "dve", "pool", "sp")

D = 2048
NIN = 11264
NS = 79
SLAB = 4096
QSCALE = 128 ** -0.5
EXPSILU = True
TAILQ = "pool"
NEG = -30000.0


class Tok:
    __slots__ = ("name", "w", "r")

    def __init__(self, name):
        self.name = name
        self.w = None
        self.r = []


class Op:
    __slots__ = ("eng", "fn", "idx", "deps", "signal", "dma_key", "cnt")


class Prog:
    def __init__(self, nc):
        self.nc = nc
        self.ops = {e: [] for e in ENGS}
        self.dma_counts = {}

    def op(self, eng, fn, reads=(), writes=(), dma_key=None):
        o = Op()
        o.eng = eng
        o.fn = fn
        o.idx = len(self.ops[eng])
        o.signal = False
        o.dma_key = dma_key
        o.cnt = 0
        if dma_key is not None:
            c = self.dma_counts.get(dma_key, 0) + 16
            self.dma_counts[dma_key] = c
            o.cnt = c
        deps = {}

        def add(d, kind):
            if d is None or d is o:
                return
            if d.dma_key is None and d.eng == eng:
                if eng == "pe":
                    return
                if kind != "raw":
                    return
                if o.idx - d.idx > 8:
                    return
            deps[id(d)] = d

        for t in reads:
            add(t.w, "raw")
        for t in writes:
            add(t.w, "waw")
            for r in t.r:
                add(r, "war")
        for t in reads:
            t.r.append(o)
        for t in writes:
            t.w = o
            t.r = []
        o.deps = list(deps.values())
        for d in o.deps:
            if d.dma_key is None:
                d.signal = True
        self.ops[eng].append(o)
        return o

    def emit(self, final_dma_ops=()):
        nc = self.nc
        with ExitStack() as es:
            sems = {e: es.enter_context(nc.semaphore(f"s_{e}")) for e in ENGS}
            dsems = {k: es.enter_context(nc.semaphore(f"d_{k}")) for k in self.dma_counts}
            for e in ENGS:
                c = 0
                for o in self.ops[e]:
                    if o.dma_key is None and o.signal:
                        c += 1
                        o.cnt = c
            block = es.enter_context(nc.Block())

            def run(e, h):
                seen = {}
                for o in self.ops[e]:
                    need = {}
                    for d in o.deps:
                        s = dsems[d.dma_key] if d.dma_key is not None else sems[d.eng]
                        k = id(s)
                        if need.get(k, (None, 0))[1] < d.cnt:
                            need[k] = (s, d.cnt)
                    for k, (s, v) in need.items():
                        if seen.get(k, 0) < v:
                            h.wait_ge(s, v)
                            seen[k] = v
                    ins = o.fn(h)
                    if o.dma_key is not None:
                        ins.then_inc(dsems[o.dma_key], 16)
                    elif o.signal:
                        ins.then_inc(sems[e], 1)
                if e == "sp":
                    for o in final_dma_ops:
                        h.wait_ge(dsems[o.dma_key], o.cnt)

            @block.tensor
            def _(h):
                run("pe", h)

            @block.scalar
            def _(h):
                run("act", h)

            @block.vector
            def _(h):
                run("dve", h)

            @block.gpsimd
            def _(h):
                run("pool", h)

            @block.sync
            def _(h):
                run("sp", h)


NSW = 71


def slab_used(s):
    if s < 28:
        return 4096
    if s < 60:
        return 3072
    if s < 68:
        return 4096
    if s == 68:
        return 4096
    if s == 69:
        return 1024
    if s == 70:
        return 1024
    return 3072


def slab_gain(s):
    if s < 4:
        return 0
    if s < 12:
        return 1 + (s % 2)
    if s < 28:
        return 0
    if s < 60:
        return 3
    return None


def build_nc(n_layers, NR0, debug=False, dmacast=()):
    nc = bass.Bass("TRN2", target_bir_lowering=False)
    dbg_ops = []

    def din(name, shape, dt=F32):
        return nc.dram_tensor(name, shape, dt, kind="ExternalInput").ap()

    xin = din("xin", [NR0 * 64, D])
    wsl = din("wsl", [n_layers, NSW, 128, SLAB])
    spt = din("spt", [n_layers, 8, 128, 3072])
    gains = din("gains", [n_layers, 4, 128, SLAB])
    sgb = din("sgb", [n_layers, 128, 1024])
    lnc = din("lnc", [n_layers, 128, 32])
    postg = din("postg", [n_layers, 128, D])
    ident_d = din("ident", [128, 128], BF16)
    ones_d = din("ones", [128, 128], BF16)
    NRL = NR0 - 8 * n_layers
    y = nc.dram_tensor("y", [NRL * 64, D], F32, kind="ExternalOutput").ap()
    wq = nc.dram_tensor("wq", [n_layers, NS, 128, SLAB], BF16, kind="Internal").ap()
    x1s = None
    if n_layers == 2:
        x1s = nc.dram_tensor("x1s", [(NR0 - 8) * 64, D], F32, kind="Internal").ap()

    with ExitStack() as es:
        off = [0]

        def alloc(nbytes):
            o = off[0]
            off[0] += (nbytes + 63) // 64 * 64
            return o

        o_hT = [alloc(16384), alloc(16384)]
        o_K = alloc(24576)
        o_V = alloc(24576)
        o_slab = [alloc(8192) for _ in range(3)]
        o_xn = alloc(8192)
        o_xr = alloc(8192)
        o_xs = alloc(4096)
        o_qT = [alloc(1024), alloc(1024)]
        o_sza = [alloc(2048), alloc(2048)]
        o_gt = alloc(2048)
        o_szb = alloc(2048)
        o_vn = alloc(8192)
        o_yb = alloc(8192)
        o_pT = [alloc(1536), alloc(1536)]
        o_mg = alloc(16384)
        o_msg = [alloc(2048), alloc(2048)]
        o_rden = alloc(2048)
        o_at = alloc(2048)
        o_bg = alloc(10240)
        o_sp = [alloc(1536), alloc(1536)]
        o_wst = alloc(2048)
        o_tt = alloc(4096)
        o_lnc = alloc(128)
        o_pg = alloc(8192)
        o_id = alloc(256)
        o_on = alloc(256)
        o_st = alloc(1024)
        o_junk = alloc(1024)
        TOTAL = off[0]
        assert TOTAL <= 212000, TOTAL
        A = es.enter_context(nc.sbuf_tensor("arena", [128, TOTAL // 2], BF16))
        PS = es.enter_context(nc.psum_tensor("ps", [128, 4096], F32))

        def bf(o, n):
            return A[:, o // 2:o // 2 + n]

        def f32(o, n):
            return A[:, o // 2:o // 2 + 2 * n].bitcast(F32)

        def bank(b, n=512, c0=0):
            return PS[:, b * 512 + c0:b * 512 + c0 + n]

        def bank_bf(b, n):
            return PS[:, b * 512:b * 512 + (n + 1) // 2].bitcast(BF16)

        hT = [bf(o, 8192).rearrange("p (k t) -> p k t", k=16) for o in o_hT]
        Kr = bf(o_K, 12288).rearrange("p (h t) -> p h t", h=8)
        Vr = bf(o_V, 12288).rearrange("p (s c) -> p s c", s=12)
        slabs = [bf(o, 4096) for o in o_slab]
        xn = f32(o_xn, 2048)
        xr = f32(o_xr, 2048)
        xs = bf(o_xs, 2048)
        qT = [bf(o, 512) for o in o_qT]
        sza = [f32(o, 512) for o in o_sza]
        gt = f32(o_gt, 512)
        szb = f32(o_szb, 512)
        vn = bf(o_vn, 4096).rearrange("p (t c) -> p t c", t=4)
        yaT = bf(o_vn, 4096).rearrange("p (h t) -> p h t", h=8)
        ybT = bf(o_yb, 4096).rearrange("p (h t) -> p h t", h=8)
        pT = [bf(o, 768) for o in o_pT]
        mgT = bf(o_mg, 8192).rearrange("p (k t) -> p k t", k=16)
        njunk = bf(o_mg, 2048)
        msg = [f32(o, 512) for o in o_msg]
        rden = f32(o_rden, 512)
        atb = f32(o_at, 512)
        bgen = bf(o_bg, 5120).rearrange("p (i h q) -> p i h q", i=5, h=8)
        spb = [bf(o, 768).rearrange("p (i q) -> p i q", i=6) for o in o_sp]
        wst = bf(o_wst, 1024).rearrange("p (g s) -> p g s", g=8)
        ttab = f32(o_tt, 1024).rearrange("p (g s) -> p g s", g=8)
        lncb = f32(o_lnc, 32)
        pgb = f32(o_pg, 2048)
        ident = bf(o_id, 128)
        ones = bf(o_on, 128)
        st = f32(o_st, 256)
        ojunk = bf(o_junk, 512)

        P = Prog(nc)
        T = Tok

        def dump(name, ap, n, dt, toks):
            if not debug:
                return
            d = nc.dram_tensor("dbg_" + name, [128, n], dt, kind="ExternalOutput").ap()
            dbg_ops.append(P.op("sp", lambda h: h.dma_start(out=d, in_=ap), reads=toks, dma_key="dbg"))

        t_hT = [T("hT0"), T("hT1")]
        t_hTx = [[T("hT0a"), T("hT0b")], [T("hT1a"), T("hT1b")]]
        t_K, t_V = T("K"), T("V")
        t_slab = [T(f"slab{i}") for i in range(3)]
        t_xn, t_xr, t_xs = T("xn"), T("xr"), T("xs")
        t_qT = [T("q0"), T("q1")]
        t_sza = [T("sza0"), T("sza1")]
        t_gt, t_szb = T("gt"), T("szb")
        t_vn, t_yb = T("vn"), T("yb")
        t_pT = [T("pT0"), T("pT1")]
        t_mg = T("mg")
        t_msg = [T("msg0"), T("msg1")]
        t_rden, t_at = T("rden"), T("at")
        t_bg = T("bg")
        t_sp = [T("sp0"), T("sp1")]
        t_wst, t_tt, t_lnc, t_pg = T("wst"), T("tt"), T("lnc"), T("pg")
        t_const = T("const")
        t_nst, t_lst, t_ost = T("nst"), T("lst"), T("ost")
        t_junk = T("junk")
        t_bank = [T(f"bank{i}") for i in range(8)]
        t_x1c = [T(f"x1c{i}") for i in range((NR0 - 8) * 64 // 128)]
        t_xrh = [T("xrh0"), T("xrh1")]
        t_lsa, t_lsd = T("lsa"), T("lsd")
        main_toks = (t_hT + t_hTx[0] + t_hTx[1] + [t_K, t_V] + t_slab + [t_xn, t_xr, t_xs] + t_qT + t_sza + [t_gt, t_szb, t_vn, t_yb]
                     + t_pT + [t_mg] + t_msg + [t_rden, t_at, t_bg] + t_sp + [t_wst, t_tt, t_lnc, t_pg, t_const,
                                                                           t_nst, t_lst, t_ost, t_junk, t_lsa, t_lsd] + t_xrh)

        NST = st[:, 0:16]
        LST = st[:, 16:64]
        OST = st[:, 64:96]
        EPS_RMS = st[:, 96:97]
        EPS_LN = st[:, 97:98]

        NB = 2
        pci = [f32(16384 * i, 4096) for i in range(NB)]
        pco = [bf(49152 + 8192 * i, 4096) for i in range(NB)]
        gti = [[f32(73728 + 16384 * (l * 4 + k), 4096) for k in range(4)] for l in range(n_layers)]
        assert 73728 + 16384 * 4 * n_layers <= TOTAL
        t_pci = [T(f"pci{i}") for i in range(NB)]
        t_pco = [T(f"pco{i}") for i in range(NB)]
        t_gain = [T(f"gain{k}") for k in range(4 * n_layers)]
        t_wq = [[T(f"wq{l}_{s}") for s in range(NS)] for l in range(n_layers)]
        jobs = [(l, s) for l in range(n_layers) if l not in dmacast for s in range(NS)]
        for l in range(n_layers):
            if l in dmacast:
                continue
            for k in range(4):
                P.op("sp", lambda h, l=l, k=k: h.dma_start(out=gti[l][k], in_=gains[l, k]), writes=[t_gain[l * 4 + k]],
                     dma_key=f"gain{l * 4 + k}")

        def pc_load(q):
            l, s = jobs[q]
            n = slab_used(s)
            i = q % NB
            srcap = wsl[l, s, :, 0:n] if s < NSW else spt[l, s - NSW]
            P.op("sp", lambda h: h.dma_start(out=pci[i][:, 0:n], in_=srcap), writes=[t_pci[i]], dma_key=f"pci{i}")

        def pc_cast(q):
            l, s = jobs[q]
            n = slab_used(s)
            i = q % NB
            g = slab_gain(s)
            if g is None:
                P.op("act", lambda h: h.activation(out=pco[i][:, 0:n], in_=pci[i][:, 0:n], func=AF.Copy),
                     reads=[t_pci[i]], writes=[t_pco[i]])
            else:
                eng = "dve"
                P.op(eng, lambda h: h.tensor_tensor(out=pco[i][:, 0:n], in0=pci[i][:, 0:n], in1=gti[l][g][:, 0:n], op=ALU.mult),
                     reads=[t_pci[i], t_gain[l * 4 + g]], writes=[t_pco[i]])

        def pc_store(q):
            l, s = jobs[q]
            n = slab_used(s)
            i = q % NB
            P.op("sp", lambda h: h.dma_start(out=wq[l, s, :, 0:n], in_=pco[i][:, 0:n]),
                 reads=[t_pco[i]], writes=[t_wq[l][s]], dma_key=f"pco{i}")

        NCK = 8
        t_cchain = [T(f"cch{k}") for k in range(NCK)]
        cast_q = [(l, s_) for l in range(n_layers) if l in dmacast for s_ in range(NS)]
        cast_pos = [0]

        def emit_casts(n):
            while n > 0 and cast_pos[0] < len(cast_q):
                i = cast_pos[0]
                l, s_ = cast_q[i]
                nn = slab_used(s_)
                srcap = wsl[l, s_, :, 0:nn] if s_ < NSW else spt[l, s_ - NSW]
                P.op("pool", lambda h, l=l, s_=s_, nn=nn, srcap=srcap: h.dma_start(out=wq[l, s_, :, 0:nn], in_=srcap),
                     writes=[t_wq[l][s_], t_cchain[i % NCK]], dma_key=f"cc{i % NCK}")
                cast_pos[0] += 1
                n -= 1

        if jobs:
            pc_load(0)
        for q in range(len(jobs)):
            if q + 1 < len(jobs):
                pc_load(q + 1)
            pc_cast(q)
            pc_store(q)
        P.op("pool", lambda h: h.memset(st[:, 128:129], 0.0), writes=t_pci + t_pco + t_gain + main_toks)
        P.op("pool", lambda h: h.memset(EPS_RMS, 1e-6), writes=[t_const])
        P.op("pool", lambda h: h.memset(EPS_LN, 1e-5), writes=[t_const])
        P.op("sp", lambda h: h.dma_start(out=ident, in_=ident_d), writes=[t_const], dma_key="const")
        P.op("sp", lambda h: h.dma_start(out=ones, in_=ones_d), writes=[t_const], dma_key="const")

        class Stream:
            def __init__(self):
                self.plan = []
                self.issued = 0
                self.pos = 0

            def issue(self, upto):
                while self.issued < min(upto, len(self.plan)):
                    i = self.issued
                    l, s = self.plan[i]
                    n = slab_used(s)
                    b = i % 3
                    P.op("sp", lambda h, l=l, s=s, n=n, b=b: h.dma_start(out=slabs[b][:, 0:n], in_=wq[l, s, :, 0:n]),
                         reads=[t_wq[l][s]], writes=[t_slab[b]], dma_key=f"slab{b}")
                    self.issued += 1

            def get(self, l, s):
                i = self.pos
                assert self.plan[i] == (l, s), (i, self.plan[i], l, s)
                self.issue(i + 3)
                self.pos += 1
                return slabs[i % 3], t_slab[i % 3]

        stream = Stream()

        def layer_plan(l, nfull):
            pl = []
            kv = [(l, s) for s in range(0, 8)]
            pl += kv + kv
            for s_ in range(nfull):
                pl += [(l, s) for s in range(8, 12)]
                pl += [(l, s) for s in range(12, 20)]
                pl += kv
                pl += [(l, s) for s in range(20, 28)]
                pl += [(l, s) for s in range(28, 60)]
                pl += [(l, s) for s in range(60, 68)]
            return pl

        for l in range(n_layers):
            NR = NR0 - 8 * l
            stream.plan += layer_plan(l, (NR - 8) // 8)

        out_stores = []

        def do_layer(l):
            NR = NR0 - 8 * l
            nfull = (NR - 8) // 8
            src = xin if l == 0 else x1s
            dst = y if l == n_layers - 1 else x1s
            def src_tok(r0):
                return [t_x1c[r0 // 128]] if l > 0 else []

            def dst_tok(q0):
                return [t_x1c[q0 // 128]] if l < n_layers - 1 else []

            xrh = [f32(o_xr, 1024), f32(o_xr + 4096, 1024)]
            e0 = 8 - 4 * l if NR0 == 80 else 4
            if NR0 != 80:
                assert n_layers == 1

            def blk(j):
                if j == 0:
                    return 0, 256, 0
                if j == nfull + 1:
                    return NR - 4, 256, (NR - 4) // 2
                return 4 + 8 * (j - 1), 512, 2 + 4 * (j - 1)

            def slot(pi):
                return (pi + 2) % 12

            P.op("sp", lambda h, l=l: h.dma_start(out=bf(o_bg, 4096), in_=wq[l, 68, :, 0:4096]),
                 reads=[t_wq[l][68]], writes=[t_bg], dma_key="lc_bg")
            P.op("sp", lambda h, l=l: h.dma_start(out=bf(o_bg + 8192, 1024), in_=wq[l, 69, :, 0:1024]),
                 reads=[t_wq[l][69]], writes=[t_bg], dma_key="lc_bg")
            P.op("sp", lambda h, l=l: h.dma_start(out=bf(o_wst, 1024), in_=wq[l, 70, :, 0:1024]),
                 reads=[t_wq[l][70]], writes=[t_wst], dma_key="lc_wst")
            P.op("sp", lambda h, l=l: h.dma_start(out=f32(o_tt, 1024), in_=sgb[l]), writes=[t_tt], dma_key="lc_tt")
            P.op("sp", lambda h, l=l: h.dma_start(out=lncb, in_=lnc[l]), writes=[t_lnc], dma_key="lc_lnc")
            P.op("sp", lambda h, l=l: h.dma_start(out=pgb, in_=postg[l]), writes=[t_pg], dma_key="lc_pg")
            for c in range(2):
                P.op("pe", lambda h, c=c: h.matmul(bank(c), lhsT=ones, rhs=bf(o_wst + c * 1024, 512), start=True, stop=True),
                     reads=[t_const, t_wst], writes=[t_bank[c]])
            for g in range(8):
                P.op("dve", lambda h, g=g: h.scalar_tensor_tensor(
                    out=ttab[:, g, :], in0=bank(g // 4, 128, (g % 4) * 128), scalar=lncb[:, 8 + g:9 + g],
                    in1=ttab[:, g, :], op0=ALU.mult, op1=ALU.add),
                    reads=[t_bank[g // 4], t_lnc, t_tt], writes=[t_tt])

            def norm_elem_tile(j, t):
                row0, ntok, _ = blk(j)
                r0 = row0 * 64 + t * 128
                P.op("sp", lambda h: h.dma_start(out=xn, in_=src[r0:r0 + 128, :]),
                     reads=src_tok(r0), writes=[t_xn], dma_key="xn")
                P.op("act", lambda h: h.activation(out=xs, in_=xn, func=AF.Square, accum_out=NST[:, t:t + 1]),
                     reads=[t_xn], writes=[t_xs, t_nst])
                P.op("act", lambda h: h.activation(out=NST[:, 4 + t:5 + t], in_=NST[:, t:t + 1], func=AF.Sqrt,
                                                   scale=1.0 / D, bias=EPS_RMS),
                     reads=[t_nst, t_const], writes=[t_nst])
                P.op("dve", lambda h: h.reciprocal(out=NST[:, 8 + t:9 + t], in_=NST[:, 4 + t:5 + t]),
                     reads=[t_nst], writes=[t_nst])
                P.op("act", lambda h: h.activation(out=xs, in_=xn, func=AF.Copy, scale=NST[:, 8 + t:9 + t]),
                     reads=[t_xn, t_nst], writes=[t_xs])

            def norm_trans_tile(j, t):
                par = j % 2
                for kc in range(16):
                    P.op("pe", lambda h, kc=kc: h.transpose(out=bank_bf(6, 2048)[:, kc * 128:(kc + 1) * 128],
                                                            in_=xs[:, kc * 128:(kc + 1) * 128], identity=ident),
                         reads=[t_xs, t_const], writes=[t_bank[6], t_bank[7]])
                if l in dmacast:
                    for kc in range(16):
                        P.op("dve", lambda h, kc=kc: h.tensor_scalar(
                            out=hT[par][:, kc, t * 128:(t + 1) * 128], in0=bank_bf(6, 2048)[:, kc * 128:(kc + 1) * 128],
                            scalar1=lncb[:, 16 + kc:17 + kc], scalar2=1.0, op0=ALU.mult, op1=ALU.mult),
                            reads=[t_bank[6], t_bank[7], t_lnc], writes=[t_hT[par]] + t_hTx[par])
                else:
                    P.op("dve", lambda h: h.tensor_copy(
                        out=hT[par][:, :, t * 128:(t + 1) * 128],
                        in_=bank_bf(6, 2048).rearrange("p (k c) -> p k c", c=128)),
                        reads=[t_bank[6], t_bank[7]], writes=[t_hT[par]] + t_hTx[par])

            def norm_elem(j):
                for t in range(blk(j)[1] // 128):
                    norm_elem_tile(j, t)
                    norm_trans_tile(j, t)

            def kv_phase(j):
                row0, ntok, p0 = blk(j)
                nt = ntok // 128
                par = j % 2
                s0 = slot(p0)
                for i in range(4):
                    sl, tsl = stream.get(l, i)
                    for ch in range(2):
                        hd = 2 * i + ch
                        b = (2 * i + ch) % 8
                        for kc in range(16):
                            P.op("pe", lambda h, sl=sl, ch=ch, kc=kc, b=b: h.matmul(
                                bank(b, ntok), lhsT=sl[:, ch * 2048 + kc * 128:ch * 2048 + (kc + 1) * 128],
                                rhs=hT[par][:, kc, 0:ntok], start=(kc == 0), stop=(kc == 15)),
                                reads=[tsl, t_hT[par]], writes=[t_bank[b]])
                        P.op("dve", lambda h, hd=hd, b=b: h.tensor_copy(out=Kr[:, hd, s0 * 128:s0 * 128 + ntok], in_=bank(b, ntok)),
                             reads=[t_bank[b]], writes=[t_K])
                for cg in range(2):
                    for kh in range(2):
                        sl, tsl = stream.get(l, 4 + cg * 2 + kh)
                        for t in range(nt):
                            b = cg * 4 + t
                            for kcl in range(8):
                                kc = kh * 8 + kcl
                                P.op("pe", lambda h, sl=sl, t=t, kc=kc, kcl=kcl, b=b: h.matmul(
                                    bank(b), lhsT=hT[par][:, kc, t * 128:(t + 1) * 128],
                                    rhs=sl[:, kcl * 512:(kcl + 1) * 512], start=(kc == 0), stop=(kc == 15)),
                                    reads=[tsl, t_hT[par]], writes=[t_bank[b]])
                    for t in range(nt):
                        b = cg * 4 + t
                        P.op("act", lambda h, t=t, b=b, cg=cg: h.activation(
                            out=Vr[:, s0 + t, cg * 512:(cg + 1) * 512], in_=bank(b), func=AF.Copy),
                            reads=[t_bank[b]], writes=[t_V])

            def vs_slab(j, cg, kh):
                par = j % 2
                sl, tsl = stream.get(l, 8 + cg * 2 + kh)
                for t in range(4):
                    b = cg * 4 + t
                    for kcl in range(8):
                        kc = kh * 8 + kcl
                        P.op("pe", lambda h, sl=sl, t=t, kc=kc, kcl=kcl, b=b: h.matmul(
                            bank(b), lhsT=hT[par][:, kc, t * 128:(t + 1) * 128],
                            rhs=sl[:, kcl * 512:(kcl + 1) * 512], start=(kc == 0), stop=(kc == 15)),
                            reads=[tsl, t_hT[par]], writes=[t_bank[b]])

            def g_phase_ln(j):
                for cg in range(2):
                    for t in range(4):
                        b = cg * 4 + t
                        c = cg * 4 + t
                        P.op("act", lambda h, b=b, c=c: h.activation(out=ojunk, in_=bank(b), func=AF.Square,
                                                                     accum_out=LST[:, 8 + c:9 + c]),
                             reads=[t_bank[b]], writes=[t_junk, t_lst])
                        P.op("dve", lambda h, b=b, c=c: h.tensor_reduce(out=LST[:, c:c + 1], in_=bank(b), axis=AX.X, op=ALU.add),
                             reads=[t_bank[b]], writes=[t_lst])
                rl = [t_lst]
                P.op("dve", lambda h: h.tensor_tensor(out=LST[:, 16:20], in0=LST[:, 0:4], in1=LST[:, 4:8], op=ALU.add), reads=rl, writes=rl)
                P.op("dve", lambda h: h.tensor_tensor(out=LST[:, 20:24], in0=LST[:, 8:12], in1=LST[:, 12:16], op=ALU.add), reads=rl, writes=rl)
                P.op("dve", lambda h: h.tensor_scalar(out=LST[:, 16:20], in0=LST[:, 16:20], scalar1=1.0 / 1024, scalar2=0.0,
                                                      op0=ALU.mult, op1=ALU.add), reads=rl, writes=rl)
                P.op("dve", lambda h: h.tensor_tensor(out=LST[:, 24:28], in0=LST[:, 16:20], in1=LST[:, 16:20], op=ALU.mult), reads=rl, writes=rl)
                P.op("dve", lambda h: h.scalar_tensor_tensor(out=LST[:, 24:28], in0=LST[:, 20:24], scalar=1.0 / 1024,
                                                             in1=LST[:, 24:28], op0=ALU.mult, op1=ALU.subtract), reads=rl, writes=rl)
                P.op("act", lambda h: h.activation(out=LST[:, 24:28], in_=LST[:, 24:28], func=AF.Sqrt, scale=1.0, bias=EPS_LN),
                     reads=[t_lst, t_const], writes=rl)
                P.op("dve", lambda h: h.reciprocal(out=LST[:, 28:32], in_=LST[:, 24:28]), reads=rl, writes=rl)
                P.op("dve", lambda h: h.scalar_tensor_tensor(out=LST[:, 32:36], in0=LST[:, 16:20], scalar=-1.0,
                                                             in1=LST[:, 28:32], op0=ALU.mult, op1=ALU.mult), reads=rl, writes=rl)
                for cg in range(2):
                    for t in range(4):
                        b = cg * 4 + t
                        P.op("act", lambda h, t=t, b=b, cg=cg: h.activation(
                            out=vn[:, t, cg * 512:(cg + 1) * 512], in_=bank(b), func=AF.Identity,
                            scale=LST[:, 28 + t:29 + t], bias=LST[:, 32 + t:33 + t]),
                            reads=[t_bank[b], t_lst], writes=[t_vn])

            def g_phase_uz(j, g):
                par = j % 2
                sl, tsl = stream.get(l, 12 + g)
                bs = (g % 2) * 3
                bu, bz, bsx = bs, bs + 1, bs + 2
                for which, b in ((0, bu), (1, bz)):
                    for kc in range(16):
                        P.op("pe", lambda h, sl=sl, which=which, kc=kc, b=b: h.matmul(
                            bank(b), lhsT=sl[:, which * 2048 + kc * 128:which * 2048 + (kc + 1) * 128],
                            rhs=hT[par][:, kc, :], start=(kc == 0), stop=(kc == 15)),
                            reads=[tsl, t_hT[par]], writes=[t_bank[b]])
                for t in range(4):
                    P.op("pe", lambda h, t=t, g=g, b=bsx: h.matmul(
                        bank(b, 128, t * 128), lhsT=vn[:, t, g * 128:(g + 1) * 128], rhs=wst[:, g, :], start=True, stop=True),
                        reads=[t_vn, t_wst], writes=[t_bank[bsx]])
                P.op("act", lambda h, b=bz: h.activation(out=szb, in_=bank(b), func=AF.Silu), reads=[t_bank[bz]], writes=[t_szb])
                for t in range(4):
                    P.op("dve", lambda h, t=t, g=g, b=bsx: h.scalar_tensor_tensor(
                        out=gt[:, t * 128:(t + 1) * 128], in0=bank(b, 128, t * 128), scalar=lncb[:, g:g + 1],
                        in1=ttab[:, g, :], op0=ALU.mult, op1=ALU.add),
                        reads=[t_bank[bsx], t_lnc, t_tt], writes=[t_gt])
                P.op("dve", lambda h, b=bu: h.tensor_tensor(out=gt, in0=gt, in1=bank(b), op=ALU.mult),
                     reads=[t_gt, t_bank[bu]], writes=[t_gt])
                P.op("dve", lambda h, g=g: h.tensor_tensor(out=ybT[:, g, :], in0=gt, in1=szb, op=ALU.mult),
                     reads=[t_gt, t_szb], writes=[t_yb])

            def q_proj(j, hd):
                par = j % 2
                sl, tsl = stream.get(l, 20 + hd)
                for which, b in ((0, 0), (1, 1)):
                    for kc in range(16):
                        P.op("pe", lambda h, sl=sl, which=which, kc=kc, b=b: h.matmul(
                            bank(b), lhsT=sl[:, which * 2048 + kc * 128:which * 2048 + (kc + 1) * 128],
                            rhs=hT[par][:, kc, :], start=(kc == 0), stop=(kc == 15)),
                            reads=[tsl, t_hT[par]], writes=[t_bank[b]])
                P.op("dve", lambda h, hd=hd: h.tensor_scalar(out=qT[hd % 2], in0=bank(0), scalar1=QSCALE, scalar2=0.0,
                                                             op0=ALU.mult, op1=ALU.add),
                     reads=[t_bank[0]], writes=[t_qT[hd % 2]])
                if EXPSILU:
                    P.op("act", lambda h, hd=hd: h.activation(out=sza[hd % 2], in_=bank(1), func=AF.Exp, scale=-1.0),
                         reads=[t_bank[1]], writes=[t_sza[hd % 2]])
                    P.op("dve", lambda h, hd=hd: h.tensor_scalar(out=sza[hd % 2], in0=sza[hd % 2], scalar1=1.0, scalar2=0.0,
                                                                 op0=ALU.add, op1=ALU.add),
                         reads=[t_sza[hd % 2]], writes=[t_sza[hd % 2]])
                    P.op("dve", lambda h, hd=hd: h.reciprocal(out=sza[hd % 2], in_=sza[hd % 2]),
                         reads=[t_sza[hd % 2]], writes=[t_sza[hd % 2]])
                    P.op("dve", lambda h, hd=hd: h.tensor_tensor(out=sza[hd % 2], in0=sza[hd % 2], in1=bank(1), op=ALU.mult),
                         reads=[t_bank[1], t_sza[hd % 2]], writes=[t_sza[hd % 2]])
                else:
                    P.op("act", lambda h, hd=hd: h.activation(out=sza[hd % 2], in_=bank(1), func=AF.Silu),
                         reads=[t_bank[1]], writes=[t_sza[hd % 2]])

            sp_ctr = [0]

            def attn(j, hd):
                _, _, p0 = blk(j)
                tiles = []
                for t in range(4):
                    tp = p0 + t
                    row = 2 * tp
                    spi = None
                    if row == e0:
                        spi, offs = 0, list(range(-2, 4))
                    elif row == e0 + 2:
                        spi, offs = 1, list(range(-2, 3))
                    elif row == e0 + 60:
                        spi, offs = 2, list(range(-2, 3))
                    elif row == e0 + 62:
                        spi, offs = 3, list(range(-3, 3))
                    else:
                        offs = list(range(-2, 3))
                    tiles.append((tp, spi, offs))

                def S(t):
                    tp, spi, offs = tiles[t]
                    bx, by = (4, 5) if t % 2 == 0 else (6, 7)
                    tb = None
                    ttb = t_bg
                    if spi is not None:
                        k = sp_ctr[0] % 2
                        sp_ctr[0] += 1
                        sidx = 71 + spi * 2 + hd // 4
                        P.op("sp", lambda h, k=k, sidx=sidx: h.dma_start(
                            out=bf(o_sp[k], 768), in_=wq[l, sidx, :, (hd % 4) * 768:(hd % 4) * 768 + 768]),
                            reads=[t_wq[l][sidx]], writes=[t_sp[k]], dma_key=f"spt{k}")
                        tb = spb[k]
                        ttb = t_sp[k]
                    for i, o in enumerate(offs):
                        pi = tp + o
                        sk = slot(pi)
                        b = bx if i < 4 else by
                        c0 = (i % 4) * 128
                        P.op("pe", lambda h, sk=sk, b=b, c0=c0, t=t: h.matmul(
                            bank(b, 128, c0), lhsT=Kr[:, hd, sk * 128:(sk + 1) * 128], rhs=qT[hd % 2][:, t * 128:(t + 1) * 128],
                            start=True, stop=False),
                            reads=[t_K, t_qT[hd % 2]], writes=[t_bank[b]])
                        if tb is None:
                            rhs = bgen[:, i, hd, :]
                        else:
                            rhs = tb[:, i, :]
                        P.op("pe", lambda h, b=b, c0=c0, rhs=rhs: h.matmul(bank(b, 128, c0), lhsT=ident, rhs=rhs, start=False, stop=True),
                             reads=[t_const, ttb], writes=[t_bank[b]])

                def E(t):
                    tp, spi, offs = tiles[t]
                    bx, by = (4, 5) if t % 2 == 0 else (6, 7)
                    n2 = (len(offs) - 4) * 128
                    P.op("act", lambda h, t=t, bx=bx: h.activation(out=pT[t % 2][:, 0:512], in_=bank(bx), func=AF.Exp),
                         reads=[t_bank[bx]], writes=[t_pT[t % 2]])
                    P.op("act", lambda h, t=t, by=by, n2=n2: h.activation(out=pT[t % 2][:, 512:512 + n2], in_=bank(by, n2), func=AF.Exp),
                         reads=[t_bank[by]], writes=[t_pT[t % 2]])

                def PV(t):
                    tp, spi, offs = tiles[t]
                    n = len(offs)
                    for i, o in enumerate(offs):
                        sk = slot(tp + o)
                        P.op("pe", lambda h, sk=sk, i=i, t=t, n=n: h.matmul(
                            bank(2, 128, t * 128), lhsT=Vr[:, sk, hd * 128:(hd + 1) * 128], rhs=pT[t % 2][:, i * 128:(i + 1) * 128],
                            start=(i == 0), stop=(i == n - 1)),
                            reads=[t_V, t_pT[t % 2]], writes=[t_bank[2]])
                        P.op("pe", lambda h, i=i, t=t, n=n: h.matmul(
                            bank(3, 128, t * 128), lhsT=ones, rhs=pT[t % 2][:, i * 128:(i + 1) * 128],
                            start=(i == 0), stop=(i == n - 1)),
                            reads=[t_const, t_pT[t % 2]], writes=[t_bank[3]])

                S(0)
                for t in range(4):
                    if t + 1 < 4:
                        S(t + 1)
                    E(t)
                    PV(t)
                P.op("dve", lambda h: h.reciprocal(out=rden, in_=bank(3)), reads=[t_bank[3]], writes=[t_rden])
                P.op("dve", lambda h: h.tensor_tensor(out=atb, in0=bank(2), in1=rden, op=ALU.mult),
                     reads=[t_bank[2], t_rden], writes=[t_at])
                P.op("pool", lambda h: h.tensor_tensor(out=yaT[:, hd, :], in0=atb, in1=sza[hd % 2], op=ALU.mult),
                     reads=[t_at, t_sza[hd % 2]], writes=[t_vn])

            def m_phase(j):
                par = j % 2
                for jj in range(16):
                    bs = (jj % 2) * 4
                    for half in range(2):
                        sl, tsl = stream.get(l, 28 + 2 * jj + half)
                        bg_, bp_ = bs + 2 * half, bs + 2 * half + 1
                        for kc in range(16):
                            P.op("pe", lambda h, sl=sl, kc=kc, b=bg_: h.matmul(
                                bank(b), lhsT=sl[:, kc * 128:(kc + 1) * 128], rhs=hT[par][:, kc, :],
                                start=(kc == 0), stop=(kc == 15)),
                                reads=[tsl, t_hT[par]], writes=[t_bank[bg_]])
                        yT = yaT if half == 0 else ybT
                        ty = t_vn if half == 0 else t_yb
                        for c in range(8):
                            P.op("pe", lambda h, sl=sl, c=c, b=bp_, yT=yT: h.matmul(
                                bank(b), lhsT=sl[:, 2048 + c * 128:2048 + (c + 1) * 128], rhs=yT[:, c, :],
                                start=(c == 0), stop=(c == 7)),
                                reads=[tsl, ty], writes=[t_bank[bp_]])
                        P.op("act", lambda h, half=half, b=bg_: h.activation(out=msg[half], in_=bank(b), func=AF.Sigmoid),
                             reads=[t_bank[bg_]], writes=[t_msg[half]])
                        P.op("dve", lambda h, half=half, b=bp_: h.tensor_tensor(out=msg[half], in0=msg[half], in1=bank(b), op=ALU.mult),
                             reads=[t_msg[half], t_bank[bp_]], writes=[t_msg[half]])
                    P.op("pool", lambda h, jj=jj: h.tensor_tensor(out=mgT[:, jj, :], in0=msg[0], in1=msg[1], op=ALU.add),
                         reads=t_msg, writes=[t_mg])

            def o_phase(j, early=None):
                row0, _, _ = blk(j)
                par = j % 2
                ob = [f32(o_msg[0], 2048), f32(o_yb, 2048), f32(o_hT[par], 2048), f32(o_hT[par] + 8192, 2048)]
                t_ob = [[t_msg[0], t_msg[1], t_rden, t_at], [t_yb], [t_hTx[par][0]], [t_hTx[par][1]]]
                for cg in range(4):
                    bs = (cg % 2) * 4
                    for kh in range(2):
                        sl, tsl = stream.get(l, 60 + cg * 2 + kh)
                        for t in range(4):
                            b = bs + t
                            for kcl in range(8):
                                kc = kh * 8 + kcl
                                P.op("pe", lambda h, sl=sl, t=t, kc=kc, kcl=kcl, b=b: h.matmul(
                                    bank(b), lhsT=mgT[:, kc, t * 128:(t + 1) * 128], rhs=sl[:, kcl * 512:(kcl + 1) * 512],
                                    start=(kc == 0), stop=(kc == 15)),
                                    reads=[tsl, t_mg], writes=[t_bank[b]])
                    for t in range(4):
                        b = bs + t
                        wt = t_ob[t] + ([t_hT[par]] if (cg == 0 and t >= 2) else [])
                        P.op("act", lambda h, t=t, b=b, cg=cg: h.activation(out=ob[t][:, cg * 512:(cg + 1) * 512], in_=bank(b), func=AF.Copy),
                             reads=[t_bank[b]], writes=wt)
                        P.op("act", lambda h, t=t, cg=cg: h.activation(out=ojunk, in_=ob[t][:, cg * 512:(cg + 1) * 512], func=AF.Square,
                                                                       accum_out=OST[:, t * 4 + cg:t * 4 + cg + 1]),
                             reads=t_ob[t], writes=[t_junk, t_ost])
                if early is not None:
                    early()
                ro = [t_ost]
                P.op("dve", lambda h: h.tensor_reduce(out=OST[:, 16:20], in_=OST[:, 0:16].rearrange("p (t c) -> p t c", c=4),
                                                      axis=AX.X, op=ALU.add), reads=ro, writes=ro)
                P.op("act", lambda h: h.activation(out=OST[:, 20:24], in_=OST[:, 16:20], func=AF.Sqrt, scale=1.0 / D, bias=EPS_RMS),
                     reads=[t_ost, t_const], writes=ro)
                P.op("dve", lambda h: h.reciprocal(out=OST[:, 24:28], in_=OST[:, 20:24]), reads=ro, writes=ro)
                for t in (1, 2, 3, 0):
                    r0 = row0 * 64 + t * 128
                    q0 = (row0 - 4) * 64 + t * 128
                    P.op(TAILQ, lambda h, r0=r0: h.dma_start(out=xr, in_=src[r0:r0 + 128, :]),
                         reads=src_tok(r0), writes=t_xrh, dma_key="xr")
                    P.op("dve", lambda h, t=t: h.scalar_tensor_tensor(out=ob[t], in0=ob[t], scalar=OST[:, 24 + t:25 + t], in1=pgb,
                                                                      op0=ALU.mult, op1=ALU.mult),
                         reads=t_ob[t] + [t_ost, t_pg], writes=t_ob[t])
                    for hf in range(2):
                        P.op("pool", lambda h, t=t, hf=hf: h.tensor_tensor(out=ob[t][:, hf * 1024:(hf + 1) * 1024],
                                                                          in0=ob[t][:, hf * 1024:(hf + 1) * 1024], in1=xrh[hf], op=ALU.add),
                             reads=t_ob[t] + [t_xrh[hf]], writes=t_ob[t])
                    o = P.op(TAILQ, lambda h, t=t, q0=q0: h.dma_start(out=dst[q0:q0 + 128, :], in_=ob[t]),
                             reads=t_ob[t], writes=dst_tok(q0), dma_key=f"out{t}")
                    if l == n_layers - 1:
                        out_stores.append(o)

            if l in dmacast:
                emit_casts(10 ** 6)
            norm_elem(0)
            kv_phase(0)
            norm_elem(1)
            kv_phase(1)
            pre = False
            for s_ in range(nfull):
                j = s_ + 1
                ntn = blk(j + 1)[1] // 128
                if (l + 1) in dmacast:
                    emit_casts(2)
                if not pre:
                    norm_elem_tile(j + 1, 0)
                vs_slab(j, 0, 0)
                vs_slab(j, 0, 1)
                vs_slab(j, 1, 0)
                vs_slab(j, 1, 1)
                if j == 1 and l == 0:
                    dump("hT", bf(o_hT[1], 8192), 8192, BF16, [t_hT[1]])
                g_phase_ln(j)
                if j == 1 and l == 0:
                    dump("vn", bf(o_vn, 4096), 4096, BF16, [t_vn])
                tsched = {2: 0, 4: 1, 6: 2, 7: 3}
                for g in range(8):
                    g_phase_uz(j, g)
                    k = tsched.get(g)
                    if k is not None and k < ntn:
                        norm_trans_tile(j + 1, k)
                        if k + 1 < ntn:
                            norm_elem_tile(j + 1, k + 1)
                if j == 1 and l == 0:
                    dump("yb", bf(o_yb, 4096), 4096, BF16, [t_yb])
                kv_phase(j + 1)
                if j == 1 and l == 0:
                    dump("K", bf(o_K, 12288), 12288, BF16, [t_K])
                    dump("V", bf(o_V, 12288), 12288, BF16, [t_V])
                q_proj(j, 0)
                for hd in range(8):
                    if hd + 1 < 8:
                        q_proj(j, hd + 1)
                    attn(j, hd)
                    if (l + 1) in dmacast:
                        emit_casts(1)
                if j == 1 and l == 0:
                    dump("ya", bf(o_vn, 4096), 4096, BF16, [t_vn])
                m_phase(j)
                if j == 1 and l == 0:
                    dump("mg", bf(o_mg, 8192), 8192, BF16, [t_mg])
                pre = (j + 1 <= nfull)
                o_phase(j, early=(lambda j=j: norm_elem_tile(j + 2, 0)) if pre else None)

        for l in range(n_layers):
            do_layer(l)
            if (l + 1) in dmacast:
                assert cast_pos[0] == len([c for c in cast_q if c[0] <= l + 1]), cast_pos[0]

        assert stream.pos == len(stream.plan), (stream.pos, len(stream.plan))
        last = {}
        for o in out_stores:
            last[o.dma_key] = o
        P.emit(final_dma_ops=list(last.values()) + dbg_ops[-1:])
    return nc


def _a_chunk(W3, col0):
    return W3[:, :, col0:col0 + 128].transpose(1, 0, 2).reshape(128, -1)


def _b_slab(W, col0, kh):
    Wk = W.reshape(-1, 128, W.shape[1])
    return Wk[kh * 8:(kh + 1) * 8, :, col0:col0 + 512].transpose(1, 0, 2).reshape(128, 4096)


def _pair_table(rpb, qrow_g, offs, rows_total, generic):
    kc = np.arange(64)
    qc = np.arange(64)
    cs = np.clip(qc - 8, 0, 48)
    colvalid = (kc[:, None] >= cs[None, :]) & (kc[:, None] < cs[None, :] + 16)
    dc = kc[:, None] - qc[None, :] + 15
    dcc = np.clip(dc, 0, 30)
    out = np.full((2, 64, len(offs), 8, 2, 64), NEG, dtype=np.float32)
    for i, o in enumerate(offs):
        for kr in range(2):
            for qr in range(2):
                krow = qrow_g + 2 * o + kr
                qrow = qrow_g + qr
                if generic:
                    rs = qrow - 4
                else:
                    rs = min(max(qrow - 4, 0), rows_total - 8)
                    if qrow < 0 or qrow >= rows_total:
                        rs = qrow - 4
                dr = krow - qrow + 7
                if krow < rs or krow >= rs + 8 or dr < 0 or dr > 14:
                    continue
                vals = rpb[:, dr, :][:, dcc]
                sel = np.where(colvalid[None], vals, np.float32(NEG))
                out[kr, :, i, :, qr, :] = sel.transpose(1, 0, 2)
    return out.reshape(128, len(offs), 8, 128)


def _layer_arrays(lw, wsl):
    (pre_g, post_g, w_in, rpb, ln_g, ln_b, sg_w, sg_b, w_pa, w_pb, w_out) = lw
    W3 = w_in.reshape(16, 128, NIN)
    for i in range(4):
        wsl[i, :, 0:2048] = _a_chunk(W3, 1024 + (2 * i) * 128)
        wsl[i, :, 2048:4096] = _a_chunk(W3, 1024 + (2 * i + 1) * 128)
    for cg in range(2):
        for kh in range(2):
            wsl[4 + cg * 2 + kh] = _b_slab(w_in, 2048 + cg * 512, kh)
            wsl[8 + cg * 2 + kh] = _b_slab(w_in, 5120 + cg * 512, kh)
    for g in range(8):
        wsl[12 + g, :, 0:2048] = _a_chunk(W3, 4096 + g * 128)
        wsl[12 + g, :, 2048:4096] = _a_chunk(W3, 6144 + g * 128)
        wsl[20 + g, :, 0:2048] = _a_chunk(W3, g * 128)
        wsl[20 + g, :, 2048:4096] = _a_chunk(W3, 3072 + g * 128)
    PA3 = w_pa.reshape(8, 128, D)
    PB3 = w_pb.reshape(8, 128, D)
    for j in range(16):
        wsl[28 + 2 * j, :, 0:2048] = _a_chunk(W3, 7168 + j * 128)
        wsl[28 + 2 * j, :, 2048:3072] = _a_chunk(PA3, j * 128)
        wsl[29 + 2 * j, :, 0:2048] = _a_chunk(W3, 9216 + j * 128)
        wsl[29 + 2 * j, :, 2048:3072] = _a_chunk(PB3, j * 128)
    for cg in range(4):
        for kh in range(2):
            wsl[60 + cg * 2 + kh] = _b_slab(w_out, cg * 512, kh)
    return wsl


def _gains(pre_g):
    gc = pre_g.reshape(16, 128).T
    e = np.arange(SLAB)
    GA = gc[:, (e // 128) % 16]
    GB0 = gc[:, e // 512]
    GB1 = gc[:, 8 + e // 512]
    GM = np.where(e[None, :] < 2048, GA, np.float32(1.0))
    return np.stack([GA, GB0, GB1, GM]).astype(np.float32)


_NC_CACHE = {}


def _get_nc(n_layers, NR0, dmacast=()):
    key = (n_layers, NR0, tuple(dmacast))
    if key not in _NC_CACHE:
        _NC_CACHE[key] = build_nc(n_layers, NR0, dmacast=tuple(dmacast))
    return _NC_CACHE[key]


def _core_inputs(x, layers, NR0):
    nl = len(layers)
    ident = np.eye(128, dtype=np.float32).astype(ml_dtypes.bfloat16)
    ones = np.ones((128, 128), dtype=np.float32).astype(ml_dtypes.bfloat16)
    halo = (NR0 - 64) // 2
    xg = x.reshape(2, 256, 64, D)
    wsl = np.zeros((nl, NSW, 128, SLAB), dtype=np.float32)
    gains = np.zeros((nl, 4, 128, SLAB), dtype=np.float32)
    sgb = np.zeros((nl, 128, 1024), dtype=np.float32)
    lnc = np.zeros((nl, 128, 32), dtype=np.float32)
    postg = np.zeros((nl, 128, D), dtype=np.float32)
    for li, lw in enumerate(layers):
        (pre_g, post_g, w_in, rpb, ln_g, ln_b, sg_w, sg_b, w_pa, w_pb, w_out) = lw
        _layer_arrays(lw, wsl[li])
        gains[li] = _gains(pre_g)
        sgb[li] = np.broadcast_to(sg_b.reshape(1, 1024), (128, 1024))
        lnc[li, :, 0:8] = ln_g.reshape(8, 128).T
        lnc[li, :, 8:16] = ln_b.reshape(8, 128).T
        lnc[li, :, 16:32] = pre_g.reshape(16, 128).T
        postg[li] = np.broadcast_to(post_g.reshape(1, D), (128, D))
        wsl[li, 70, :, 0:1024] = sg_w.transpose(2, 0, 1).reshape(128, 1024)
        g5 = _pair_table(rpb, 0, list(range(-2, 3)), 256, True).reshape(128, 5120)
        wsl[li, 68] = g5[:, 0:4096]
        wsl[li, 69, :, 0:1024] = g5[:, 4096:5120]
    in_maps = []
    for c in range(8):
        bi, p = c // 4, c % 4
        R0 = 64 * p
        xl = np.zeros((NR0, 64, D), dtype=np.float32)
        lo, hi = R0 - halo, R0 + 64 + halo
        slo, shi = max(lo, 0), min(hi, 256)
        xl[slo - lo:shi - lo] = xg[bi, slo:shi]
        spt = np.zeros((nl, 8, 128, 3072), dtype=np.float32)
        for li, lw in enumerate(layers):
            rpb = lw[3]
            specs = [(R0, list(range(-2, 4))), (R0 + 2, list(range(-2, 3))),
                     (R0 + 60, list(range(-2, 3))), (R0 + 62, list(range(-3, 3)))]
            for spi, (qrow, offs) in enumerate(specs):
                tb = _pair_table(rpb, qrow, offs, 256, False)
                full = np.full((128, 6, 8, 128), NEG, dtype=np.float32)
                full[:, 0:len(offs)] = tb
                hm = full.transpose(0, 2, 1, 3)
                spt[li, spi * 2 + 0] = hm[:, 0:4].reshape(128, 3072)
                spt[li, spi * 2 + 1] = hm[:, 4:8].reshape(128, 3072)
        in_maps.append({"xin": xl.reshape(NR0 * 64, D), "wsl": wsl, "spt": spt, "gains": gains, "sgb": sgb,
                        "lnc": lnc, "postg": postg, "ident": ident, "ones": ones})
    return in_maps


FUSED = True
DMACAST = (1,)


def kernel(x, pre_norm_g, post_norm_g, w_in, na_rpb, sg_ln_g, sg_ln_b, sg_w, sg_b, w_proj_a, w_proj_b, w_out):
    f = lambda a: np.ascontiguousarray(np.asarray(a, dtype=np.float32))
    x = f(x)
    layers = []
    for l in range(2):
        layers.append(tuple(f(a[l]) for a in (pre_norm_g, post_norm_g, w_in, na_rpb, sg_ln_g, sg_ln_b, sg_w, sg_b,
                                               w_proj_a, w_proj_b, w_out)))
    if FUSED:
        nc = _get_nc(2, 80, DMACAST)
        in_maps = _core_inputs(x, layers, 80)
        res = run_bass_kernel_spmd(nc, in_maps, core_ids=list(range(8)))
        out = np.zeros((2, 256, 64, D), dtype=np.float32)
        for c in range(8):
            bi, p = c // 4, c % 4
            out[bi, 64 * p:64 * p + 64] = res.results[c]["y"].reshape(64, 64, D)
        return out.reshape(2, 16384, D)
    nc = _get_nc(1, 72)
    cur = x
    for l in range(2):
        in_maps = _core_inputs(cur, [layers[l]], 72)
        res = run_bass_kernel_spmd(nc, in_maps, core_ids=list(range(8)))
        out = np.zeros((2, 256, 64, D), dtype=np.float32)
        for c in range(8):
            bi, p = c // 4, c % 4
            out[bi, 64 * p:64 * p + 64] = res.results[c]["y"].reshape(64, 64, D)
        cur = out.reshape(2, 16384, D)
    return cur
```
